# Optimizing a Trainium2 kernel written in Bass

```python
import math
import jax
import jax.numpy as jnp
from jax import lax
import numpy as np

D_MODEL = 1024
BATCH = 4
SEQ = 8192
DEPTH = 2

GRID_W = 64
CTX_LEN = 256

MLA_HEADS = 8
MLA_NOPE = 64
MLA_ROPE = 32
MLA_V = 64
MLA_Q_RANK = 384
MLA_KV_RANK = 256
MLA_WIDTH = MLA_HEADS * MLA_V
ROPE_BASE = 10000.0
Q_BLOCK = 128

SSD_HEADS = 8
SSD_HEAD_DIM = 64
SSD_WIDTH = SSD_HEADS * SSD_HEAD_DIM
SSD_GROUPS = 2
SSD_STATE = 64
SSD_CONV = 3
SSD_CHUNK = 128
SSD_CONV_DIM = SSD_WIDTH + 2 * SSD_GROUPS * SSD_STATE

MIX_SPLITS = (MLA_Q_RANK, MLA_KV_RANK, MLA_ROPE, MLA_WIDTH, SSD_WIDTH, SSD_CONV_DIM, 2 * SSD_HEADS)
MIX_IN = sum(MIX_SPLITS)
MIX_OUT = MLA_WIDTH + SSD_WIDTH

POOL_WINDOWS = (2, 4, 8, 16)
POOL_GROUPS = len(POOL_WINDOWS)
POOL_WIDTH = D_MODEL
POOL_GROUP_DIM = POOL_WIDTH // POOL_GROUPS

RMS_EPS = 1e-6

kernel_name = "hybrid_mla_ssd_pool_diffusion_block"


def _rmsnorm(x, w):
    xf = x.astype(jnp.float32)
    y = xf * lax.rsqrt(jnp.mean(xf * xf, axis=-1, keepdims=True) + RMS_EPS)
    return (y * w.astype(jnp.float32)).astype(x.dtype)


def _split(t, sizes):
    idx = np.cumsum(sizes)[:-1].tolist()
    return jnp.split(t, idx, axis=-1)


def _ident(t):
    return t


def _flip(t):
    return jnp.flip(t, axis=1)


def _axial_rope_tables(n):
    rows = n // GRID_W
    row = jnp.repeat(jnp.arange(rows, dtype=jnp.float32), GRID_W)
    col = jnp.tile(jnp.arange(GRID_W, dtype=jnp.float32), rows)
    axis_dim = MLA_ROPE // 2
    inv_freq = 1.0 / (ROPE_BASE ** (jnp.arange(0, axis_dim, 2, dtype=jnp.float32) / axis_dim))
    ang = jnp.concatenate([row[:, None] * inv_freq, col[:, None] * inv_freq], axis=-1)
    return jnp.cos(ang), jnp.sin(ang)


def _rope(t, cos, sin):
    half = t.shape[-1] // 2
    t1, t2 = t[..., :half], t[..., half:]
    out = jnp.concatenate([t1 * cos - t2 * sin, t2 * cos + t1 * sin], axis=-1)
    return out.astype(t.dtype)


def _mla_q(q_a, q_norm, w_uq):
    b, n, _ = q_a.shape
    q = (_rmsnorm(q_a, q_norm) @ w_uq).reshape(b, n, MLA_HEADS, MLA_NOPE + MLA_ROPE)
    return q[..., :MLA_NOPE], q[..., MLA_NOPE:]


def _mla_kv(kv_a, kv_norm, w_ukv):
    b, n, _ = kv_a.shape
    kv = (_rmsnorm(kv_a, kv_norm) @ w_ukv).reshape(b, n, MLA_HEADS, MLA_NOPE + MLA_V)
    return kv[..., :MLA_NOPE], kv[..., MLA_NOPE:]


def _block_attention(q_nope, q_pe, k_nope, k_pe, v):
    b, n, H, _ = q_nope.shape
    nb = n // Q_BLOCK
    scale = (MLA_NOPE + MLA_ROPE) ** -0.5

    def one_block(args):
        qn, qp = args
        s = jnp.einsum("bqhd,bkhd->bhqk", qn, k_nope) + jnp.einsum("bqhr,bkr->bhqk", qp, k_pe)
        p = jax.nn.softmax(s.astype(jnp.float32) * scale, axis=-1).astype(v.dtype)
        return jnp.einsum("bhqk,bkhd->bqhd", p, v)

    qn_b = q_nope.reshape(b, nb, Q_BLOCK, H, MLA_NOPE).transpose(1, 0, 2, 3, 4)
    qp_b = q_pe.reshape(b, nb, Q_BLOCK, H, MLA_ROPE).transpose(1, 0, 2, 3, 4)
    out = lax.map(one_block, (qn_b, qp_b))
    return out.transpose(1, 0, 2, 3, 4).reshape(b, n, H * MLA_V)


def _centred_dwconv(u, w, bias):
    n = u.shape[1]
    pad = SSD_CONV // 2
    up = jnp.pad(u, ((0, 0), (pad, pad), (0, 0)))
    out = bias
    for k in range(SSD_CONV):
        out = out + up[:, k:k + n] * w[k]
    return out


def _ssd_inputs(xbc, dt_raw, conv_w, conv_b, dt_bias):
    b, n, _ = xbc.shape
    u = jax.nn.silu(_centred_dwconv(xbc, conv_w, conv_b))
    xs, bs, cs = _split(u, (SSD_WIDTH, SSD_GROUPS * SSD_STATE, SSD_GROUPS * SSD_STATE))
    dt = jax.nn.softplus(dt_raw.astype(jnp.float32).reshape(b, n, 2, SSD_HEADS) + dt_bias.astype(jnp.float32))
    return (xs.reshape(b, n, SSD_HEADS, SSD_HEAD_DIM),
            bs.reshape(b, n, SSD_GROUPS, SSD_STATE),
            cs.reshape(b, n, SSD_GROUPS, SSD_STATE),
            dt)


def _segsum(a_cs):
    T = a_cs.shape[-1]
    diff = a_cs[..., :, None] - a_cs[..., None, :]
    return jnp.where(jnp.tril(jnp.ones((T, T), dtype=bool)), diff, -jnp.inf)


def _ssd_prepare(x, dt, A, B):
    b, L, H, P = x.shape
    G, N = B.shape[2], B.shape[3]
    R = H // G
    nc = L // SSD_CHUNK
    xd = (x.astype(jnp.float32) * dt[..., None]).reshape(b, nc, SSD_CHUNK, G, R, P)
    a = (dt * A).reshape(b, nc, SSD_CHUNK, G, R).transpose(0, 3, 4, 1, 2)
    a_cs = jnp.cumsum(a, axis=-1)
    bc = B.astype(jnp.float32).reshape(b, nc, SSD_CHUNK, G, N)
    return xd, a_cs, bc


def _ssd_pass_states(xd, a_cs, bc, h0):
    decay_to_end = jnp.exp(a_cs[..., -1:] - a_cs)
    states = jnp.einsum("bcsgn,bgrcs,bcsgrp->cbgrpn", bc, decay_to_end, xd)
    chunk_decay = jnp.exp(a_cs[..., -1]).transpose(3, 0, 1, 2)

    def step(h, inp):
        s, dec = inp
        return h * dec[..., None, None] + s, h

    return lax.scan(step, h0, (states, chunk_decay))


def _ssd_final_state(x, dt, A, B, h0):
    xd, a_cs, bc = _ssd_prepare(x, dt, A, B)
    h_final, _ = _ssd_pass_states(xd, a_cs, bc, h0)
    return h_final


def _ssd_scan(x, dt, A, B, C, h0):
    xd, a_cs, bc = _ssd_prepare(x, dt, A, B)
    h_final, h_in = _ssd_pass_states(xd, a_cs, bc, h0)
    b, nc = xd.shape[0], xd.shape[1]
    cc = C.astype(jnp.float32).reshape(b, nc, SSD_CHUNK, C.shape[2], C.shape[3])
    l_mat = jnp.exp(_segsum(a_cs))
    cb = jnp.einsum("bclgn,bcsgn->bgcls", cc, bc)
    y_diag = jnp.einsum("bgcls,bgrcls,bcsgrp->bclgrp", cb, l_mat, xd)
    y_off = jnp.einsum("bclgn,cbgrpn,bgrcl->bclgrp", cc, h_in, jnp.exp(a_cs))
    return (y_diag + y_off).reshape(x.shape), h_final


def _merge_branches(attn, gate_a, y_ssd, z, ssd_norm, w_out):
    b, n, _ = attn.shape
    y = y_ssd.reshape(b, n, SSD_WIDTH) * jax.nn.silu(z.astype(jnp.float32))
    ssd_out = _rmsnorm(y, ssd_norm).astype(attn.dtype)
    return jnp.concatenate([attn * jax.nn.silu(gate_a), ssd_out], axis=-1) @ w_out


def _mixing_sublayer(h_lat, h_ctx, need_ctx_out, cos, sin, w_in, q_norm, w_uq, kv_norm, w_ukv,
                     conv_w, conv_b, a_log, dt_bias, d_skip, ssd_norm, w_out):
    q_a_l, kv_a_l, kpe_l, ga_l, z_l, xbc_l, dtr_l = _split(h_lat @ w_in, MIX_SPLITS)
    q_a_c, kv_a_c, kpe_c, ga_c, z_c, xbc_c, dtr_c = _split(h_ctx @ w_in, MIX_SPLITS)

    kn_c, v_c = _mla_kv(kv_a_c, kv_norm, w_ukv)
    kn_l, v_l = _mla_kv(kv_a_l, kv_norm, w_ukv)
    qn_l, qp_l = _mla_q(q_a_l, q_norm, w_uq)
    qp_l = _rope(qp_l, cos[:, None, :], sin[:, None, :])
    kpe_l = _rope(kpe_l, cos, sin)
    attn_l = _block_attention(qn_l, qp_l,
                              jnp.concatenate([kn_c, kn_l], axis=1),
                              jnp.concatenate([kpe_c, kpe_l], axis=1),
                              jnp.concatenate([v_c, v_l], axis=1))

    xs_l, b_l, c_l, dt_l = _ssd_inputs(xbc_l, dtr_l, conv_w, conv_b, dt_bias)
    xs_c, b_c, c_c, dt_c = _ssd_inputs(xbc_c, dtr_c, conv_w, conv_b, dt_bias)
    bsz = h_lat.shape[0]
    h0 = jnp.zeros((bsz, SSD_GROUPS, SSD_HEADS // SSD_GROUPS, SSD_HEAD_DIM, SSD_STATE), jnp.float32)
    y_l, y_c = [], []
    for d in range(2):
        fl = _flip if d == 1 else _ident
        A = -jnp.exp(a_log[d].astype(jnp.float32))
        skip = d_skip[d].astype(jnp.float32)[:, None]
        if need_ctx_out:
            yc, hc = _ssd_scan(fl(xs_c), fl(dt_c[:, :, d]), A, fl(b_c), fl(c_c), h0)
            y_c.append(fl(yc) + skip * xs_c)
        else:
            hc = _ssd_final_state(fl(xs_c), fl(dt_c[:, :, d]), A, fl(b_c), h0)
        yl, _ = _ssd_scan(fl(xs_l), fl(dt_l[:, :, d]), A, fl(b_l), fl(c_l), hc)
        y_l.append(fl(yl) + skip * xs_l)

    out_l = _merge_branches(attn_l, ga_l, y_l[0] + y_l[1], z_l, ssd_norm, w_out)
    if not need_ctx_out:
        return out_l, None
    qn_c, qp_c = _mla_q(q_a_c, q_norm, w_uq)
    attn_c = _block_attention(qn_c, qp_c, kn_c, kpe_c, v_c)
    out_c = _merge_branches(attn_c, ga_c, y_c[0] + y_c[1], z_c, ssd_norm, w_out)
    return out_l, out_c


def _multiscale_pool(u, lin, scale):
    b, n, _ = u.shape
    uf = u.astype(jnp.float32).reshape(b, n, POOL_GROUPS, POOL_GROUP_DIM)
    csum = jnp.pad(jnp.cumsum(uf, axis=1), ((0, 0), (1, 0), (0, 0), (0, 0)))
    t = jnp.arange(n)
    diffs = []
    for g, w in enumerate(POOL_WINDOWS):
        lo = jnp.maximum(t - w // 2, 0)
        hi = jnp.minimum(t + (w - w // 2 - 1), n - 1)
        cg = csum[:, :, g]
        mean = (cg[:, hi + 1] - cg[:, lo]) / (hi - lo + 1).astype(jnp.float32)[:, None]
        diffs.append(mean - uf[:, :, g])
    m = jnp.stack(diffs, axis=2).astype(u.dtype)
    y = jnp.einsum("bngc,gcd->bngd", m, lin).reshape(b, n, POOL_WIDTH)
    return y * scale


def _pool_sublayer(h, w_in, pool_lin, pool_scale, w_out):
    u, g = _split(h @ w_in, (POOL_WIDTH, POOL_WIDTH))
    return (_multiscale_pool(u, pool_lin, pool_scale) * jax.nn.silu(g)) @ w_out


def setup_inputs(seed: int = 0) -> dict:
    key = jax.random.key(seed)
    keys = list(jax.random.split(key, 32))
    f32 = jnp.float32
    ne, no = (DEPTH + 1) // 2, DEPTH // 2

    def nrm(k, shape, scale):
        return jax.random.normal(k, shape, f32) * scale

    def gain(k, shape):
        return 1.0 + 0.02 * jax.random.normal(k, shape, f32)

    dt0 = jnp.exp(jax.random.uniform(keys[14], (ne, 2, SSD_HEADS), f32, math.log(1e-3), math.log(1e-1)))
    return {
        "x": nrm(keys[0], (BATCH, SEQ, D_MODEL), 1.0),
        "c": nrm(keys[1], (BATCH, D_MODEL), 1.0),
        "ctx": nrm(keys[2], (BATCH, CTX_LEN, D_MODEL), 1.0),
        "c_ctx": nrm(keys[3], (D_MODEL,), 1.0),
        "mod_w": nrm(keys[4], (DEPTH, D_MODEL, 3 * D_MODEL), D_MODEL ** -0.5),
        "mod_b": nrm(keys[5], (DEPTH, 3 * D_MODEL), 0.02),
        "norm_w": gain(keys[6], (DEPTH, D_MODEL)),
        "w_in_mix": nrm(keys[7], (ne, D_MODEL, MIX_IN), D_MODEL ** -0.5),
        "q_norm": gain(keys[8], (ne, MLA_Q_RANK)),
        "w_uq": nrm(keys[9], (ne, MLA_Q_RANK, MLA_HEADS * (MLA_NOPE + MLA_ROPE)), MLA_Q_RANK ** -0.5),
        "kv_norm": gain(keys[10], (ne, MLA_KV_RANK)),
        "w_ukv": nrm(keys[11], (ne, MLA_KV_RANK, MLA_HEADS * (MLA_NOPE + MLA_V)), MLA_KV_RANK ** -0.5),
        "conv_w": nrm(keys[12], (ne, SSD_CONV, SSD_CONV_DIM), SSD_CONV ** -0.5),
        "conv_b": nrm(keys[13], (ne, SSD_CONV_DIM), 0.02),
        "a_log": jnp.log(jax.random.uniform(keys[15], (ne, 2, SSD_HEADS), f32, 1.0, 16.0)),
        "dt_bias": dt0 + jnp.log(-jnp.expm1(-dt0)),
        "d_skip": 1.0 + 0.1 * jax.random.normal(keys[16], (ne, 2, SSD_HEADS), f32),
        "ssd_norm": gain(keys[17], (ne, SSD_WIDTH)),
        "w_out_mix": nrm(keys[18], (ne, MIX_OUT, D_MODEL), MIX_OUT ** -0.5),
        "w_in_pool": nrm(keys[19], (no, D_MODEL, 2 * POOL_WIDTH), D_MODEL ** -0.5),
        "pool_lin": nrm(keys[20], (no, POOL_GROUPS, POOL_GROUP_DIM, POOL_GROUP_DIM), POOL_GROUP_DIM ** -0.5),
        "pool_scale": 1.0 + 0.1 * jax.random.normal(keys[21], (no, POOL_WIDTH), f32),
        "w_out_pool": nrm(keys[22], (no, POOL_WIDTH, D_MODEL), POOL_WIDTH ** -0.5),
        "final_norm": gain(keys[23], (D_MODEL,)),
    }


def reference(x, c, ctx, c_ctx, mod_w, mod_b, norm_w, w_in_mix, q_norm, w_uq, kv_norm, w_ukv,
              conv_w, conv_b, a_log, dt_bias, d_skip, ssd_norm, w_out_mix, w_in_pool, pool_lin,
              pool_scale, w_out_pool, final_norm):
    n = x.shape[1]
    cos, sin = _axial_rope_tables(n)
    last_mix = ((DEPTH - 1) // 2) * 2
    silu_c = jax.nn.silu(c)
    silu_cc = jax.nn.silu(c_ctx)
    for i in range(DEPTH):
        j = i // 2
        need_ctx_out = i < last_mix
        shift, scale, gate = jnp.split((silu_c @ mod_w[i] + mod_b[i])[:, None, :], 3, axis=-1)
        h = _rmsnorm(x, norm_w[i]) * (1 + scale) + shift
        if i % 2 == 0 or need_ctx_out:
            shift_c, scale_c, gate_c = jnp.split(silu_cc @ mod_w[i] + mod_b[i], 3, axis=-1)
            h_c = _rmsnorm(ctx, norm_w[i]) * (1 + scale_c) + shift_c
        if i % 2 == 0:
            o, o_c = _mixing_sublayer(h, h_c, need_ctx_out, cos, sin, w_in_mix[j], q_norm[j], w_uq[j],
                                      kv_norm[j], w_ukv[j], conv_w[j], conv_b[j], a_log[j], dt_bias[j],
                                      d_skip[j], ssd_norm[j], w_out_mix[j])
        else:
            o = _pool_sublayer(h, w_in_pool[j], pool_lin[j], pool_scale[j], w_out_pool[j])
            o_c = _pool_sublayer(h_c, w_in_pool[j], pool_lin[j], pool_scale[j], w_out_pool[j]) if need_ctx_out else None
        x = x + gate * o
        if need_ctx_out:
            ctx = ctx + gate_c * o_c
    return _rmsnorm(x, final_norm)
```

```python
import numpy as np
import ml_dtypes
from contextlib import ExitStack
import concourse.bass as bass
import concourse.mybir as mybir
from concourse.bass_utils import run_bass_kernel_spmd

F32 = mybir.dt.float32
BF16 = mybir.dt.bfloat16
AF = mybir.ActivationFunctionType
ALU = mybir.AluOpType

import os
_CUT = float(os.environ.get('KCUT', '99'))
_CUT3 = float(os.environ.get('KCUT3', '99'))
_CUT2 = float(os.environ.get('KCUT2', '99'))
_SMALL = os.environ.get("KSMALL") == "1"
D = 1024
NT = 4 if _SMALL else 64
NOWN = 3 if _SMALL else 33
NOUT = NOWN - 1
SEQ = NT * 128
NSLOT = NT + 2
NKEY = NSLOT * 128
NQ = NOWN * 128
EPS = 1e-6
QSCALE = 96 ** -0.5
NEG = -30000.0
VP = 66


class Trk:
    def __init__(self, nc, es):
        self.nc = nc
        self.es = es
        self.engs = {"pe": nc.tensor, "act": nc.scalar, "dve": nc.vector, "pool": nc.gpsimd, "sp": nc.sync}
        self.esem = {}
        self.cnt = {}
        self.seen = {e: {} for e in self.engs}
        self.last_w = {}
        self.readers = {}
        self.dsem = {}
        self.dcnt = {}
        self.nsem = 0
        self.epoch = 0
        self.rec = None
        self.scratch = set()
        self._bundle = None
        self.excl = set()
        self.new_epoch()

    def _newsem(self, name):
        self.nsem += 1
        return self.es.enter_context(self.nc.semaphore(f"{name}_{self.nsem}"))

    def new_epoch(self):
        self.epoch += 1
        if not self.esem:
            for e in self.engs:
                self.esem[e] = self._newsem(f"e{e}")
                self.cnt[e] = 0
        self.keymap = {}
        self.last_w = {}
        self.readers = {}

    def _wait(self, eng, tok):
        kind, key, val = tok
        if kind == "e":
            if key == eng and eng == "pe":
                return
            sem = self.esem[key]
        else:
            sem = self.dsem[key]
        sk = (kind, key)
        if self.seen[eng].get(sk, 0) >= val:
            return
        self.seen[eng][sk] = val
        self.engs[eng].wait_ge(sem, val)

    def _deps(self, eng, r, w):
        deps = []
        for b in r:
            t = self.last_w.get(b)
            if t:
                deps.append(t)
        for b in w:
            t = self.last_w.get(b)
            if t:
                deps.append(t)
            deps.extend(self.readers.get(b, ()))
        for t in deps:
            self._wait(eng, t)

    def _commit(self, tok, r, w):
        for b in w:
            self.last_w[b] = tok
            self.readers[b] = []
        for b in r:
            if b not in w:
                self.readers.setdefault(b, []).append(tok)

    def _x(self, r, w):
        xr = [b for b in r if b in self.excl and b not in w]
        return (r, list(w) + xr) if xr else (r, w)

    def op(self, eng, fn, r=(), w=()):
        if self.rec is not None:
            self._rec_add(lambda: self.op(eng, fn, r, w), r, w)
            return
        r, w = self._x(r, w)
        self._deps(eng, r, w)
        inst = fn(self.engs[eng])
        self.cnt[eng] += 1
        inst.then_inc(self.esem[eng], 1)
        self._commit(("e", eng, self.cnt[eng]), r, w)

    def grp(self, eng, fns, r=(), w=()):
        if self.rec is not None:
            self._rec_add(lambda: self.grp(eng, fns, r, w), r, w)
            return
        r, w = self._x(r, w)
        self._deps(eng, r, w)
        inst = None
        for fn in fns:
            inst = fn(self.engs[eng])
        self.cnt[eng] += 1
        inst.then_inc(self.esem[eng], 1)
        self._commit(("e", eng, self.cnt[eng]), r, w)

    def _rec_add(self, clo, r, w):
        hit_w = any(b in self.scratch for b in w) and not any(b in self.scratch for b in r)
        hit_r = any(b in self.scratch for b in r)
        if self._bundle is not None:
            self._bundle.append(clo)
            if hit_r:
                inner, self._bundle = self._bundle, None
                self.rec.append(lambda: [g() for g in inner])
            return
        if hit_w:
            self._bundle = [clo]
            return
        self.rec.append(clo)

    def atomic(self, f):
        if self.rec is None:
            f()
            return
        outer, self.rec = self.rec, []
        f()
        inner, self.rec = self.rec, outer
        self.rec.append(lambda: [g() for g in inner])

    def record(self, f, *a):
        assert self.rec is None
        self.rec = []
        f(*a)
        assert self._bundle is None
        ops, self.rec = self.rec, None
        return ops

    @staticmethod
    def interleave(a, b):
        na, nb_ = len(a), len(b)
        ia = ib = 0
        while ia < na or ib < nb_:
            if ib >= nb_ or (ia < na and ia * nb_ <= ib * na):
                a[ia](); ia += 1
            else:
                b[ib](); ib += 1

    def dma(self, q, key, out, in_, r=(), w=(), **kw):
        if self.rec is not None:
            self._rec_add(lambda: self.dma(q, key, out, in_, r, w, **kw), r, w)
            return
        km = self.keymap.setdefault(q, {})
        key = (q, km.setdefault(key, len(km)))
        if key not in self.dsem:
            self.dsem[key] = self._newsem(f"d{key[0]}{key[1]}")
            self.dcnt[key] = 0
        self._deps(q, r, w)
        self.engs[q].dma_start(out=out, in_=in_, **kw).then_inc(self.dsem[key], 16)
        self.dcnt[key] += 16
        self._commit(("d", key, self.dcnt[key]), r, w)

    def barrier(self):
        for e in self.engs:
            for e2 in self.engs:
                if e2 != e and self.cnt[e2] > 0:
                    self._wait(e, ("e", e2, self.cnt[e2]))
            for key, v in self.dcnt.items():
                self._wait(e, ("d", key, v))
        self.new_epoch()


def build(dbg=False, stop=99):
    nc = bass.Bass("TRN2", target_bir_lowering=False)

    def din(name, shape, dt=F32):
        return nc.dram_tensor(name, list(shape), dt, kind="ExternalInput").ap()

    def dscr(name, shape, dt=BF16):
        kind = "ExternalOutput" if (dbg and name in DBG) else "Internal"
        return nc.dram_tensor(name, list(shape), dt, kind=kind).ap()

    DBG = {"QT", "KT", "KPE", "VS", "ZS", "XBC", "MODS", "DTS", "AT2", "YT", "X1"}

    xs = din("xs", [SEQ, D])
    ctx = din("ctx", [256, D])
    cv = din("cv", [128, 16])
    mod_w = None if _SMALL else din("mod_w", [2, D, 3 * D])
    mod_b = din("mod_b", [2, 3 * D])
    norm_w = din("norm_w", [2, D])
    w_in = din("w_in", [D, 2480])
    q_norm = din("q_norm", [1, 384])
    w_uq = din("w_uq", [384, 768])
    kv_norm = din("kv_norm", [1, 256])
    w_ukv = din("w_ukv", [256, 1024])
    conv_w = din("conv_w", [3, 768])
    conv_b = din("conv_b", [1, 768])
    ssdp = din("ssdp", [3, 16])
    ssd_norm = din("ssd_norm", [1, 512])
    w_out_mix = din("w_out_mix", [D, D])
    w_in_pool = din("w_in_pool", [D, 2 * D])
    pool_lin = din("pool_lin", [4, 256, 256])
    pool_scale = din("pool_scale", [1, D])
    w_out_pool = din("w_out_pool", [D, D])
    final_norm = din("final_norm", [1, D])
    ident_d = din("ident", [128, 128], BF16)
    rope_d = din("rope", [NSLOT, 128, 64])
    masks_d = din("masks", [128, 2, 128])
    negm_d = din("negm", [128, 2, 8 * 128], BF16)
    poolm_d = din("poolm", [128, 36, 128], BF16)
    cwt_d = din("cwt", [128, 6, 4])
    out_d = nc.dram_tensor("out", [NOUT * 128, D], F32, kind="ExternalOutput").ap()

    MODS = din("MODS", [3, 3 * D]) if _SMALL else dscr("MODS", [3, 3 * D], F32)
    QT = dscr("QT", [768, NQ])
    KT = dscr("KT", [512, NKEY])
    KPE = dscr("KPE", [32, NKEY])
    GA = dscr("GA", [512, NQ])
    ZS = dscr("ZS", [NOWN, 128, 512])
    XBC = dscr("XBC", [768, NKEY])
    AT = dscr("AT", [8, 64, NQ])
    YT = dscr("YT", [512, NQ])

    with ExitStack() as es:
        tk = Trk(nc, es)

        def sb(st, name, shape, dt=F32):
            return st.enter_context(nc.sbuf_tensor("s_" + name, list(shape), dt))

        def ps(st, name, shape, dt=F32):
            tk.excl.add(name)
            return st.enter_context(nc.psum_tensor("p_" + name, list(shape), dt))

        ident = sb(es, "ident", [128, 128], BF16)
        DT = sb(es, "DT", [128, NSLOT, 16])
        tk.dma("sp", "c0", ident[:], ident_d[:, :], w=["ident"])
        p13 = es.enter_context(ExitStack())
        Vres = sb(p13, "Vres", [128, NSLOT, 8, VP], BF16)
        GAres = sb(p13, "GAres", [128, NOWN, 512], BF16)

        with ExitStack() as p0:
          if not _SMALL:
              cvt = sb(p0, "cvt", [128, 16])
              scb = sb(p0, "scb", [128, 16], BF16)
              mw = sb(p0, "mw", [128, 8, 3 * D], BF16)
              mrow = sb(p0, "mrow", [1, 3 * D])
              mbrow = sb(p0, "mbrow", [1, 3 * D])
              pm = ps(p0, "pm", [128, 512])
              tk.dma("sp", "c1", cvt[:], cv[:, :], w=["cvt"])
              tk.op("act", lambda e: e.activation(out=scb[:], in_=cvt[:], func=AF.Silu), r=["cvt"], w=["scb"])
              vi = 0
              for layer in range(2):
                  tk.dma("pool", "mw", mw[:], mod_w[layer].rearrange("(k p) n -> p k n", p=128), w=["mw"])
                  for var in ([0, 1] if layer == 0 else [0]):
                      tk.dma("sp", "c2", mbrow[:], mod_b[layer:layer + 1, :], w=["mbrow"])
                      for jb in range(6):
                          fns = []
                          for k in range(8):
                              fns.append(lambda e, k=k, jb=jb, var=var: e.matmul(
                                  pm[0:1, :], lhsT=scb[:, var * 8 + k:var * 8 + k + 1],
                                  rhs=mw[:, k, jb * 512:(jb + 1) * 512], start=(k == 0), stop=(k == 7)))
                          tk.grp("pe", fns, r=["scb", "mw"], w=["pm"])
                          tk.op("dve", lambda e, jb=jb: e.tensor_tensor(
                              out=mrow[:, jb * 512:(jb + 1) * 512], in0=pm[0:1, :],
                              in1=mbrow[:, jb * 512:(jb + 1) * 512], op=ALU.add),
                              r=["pm", "mbrow"], w=["mrow"])
                      tk.dma("sp", "c3", MODS[vi:vi + 1, :], mrow[:], r=["mrow"], w=[("MODS", vi)])
                      vi += 1
        tk.barrier()
        if stop < 1:
            return nc

        with ExitStack() as p1:
            win = sb(p1, "win", [128, 8, 2480], BF16)
            wuq = sb(p1, "wuq", [128, 3, 768], BF16)
            wukv = sb(p1, "wukv", [128, 2, 1024], BF16)
            tk.dma("pool", "w1", win[:], w_in.rearrange("(k p) n -> p k n", p=128), w=["win"])
            tk.dma("pool", "w2", wuq[:], w_uq.rearrange("(k p) n -> p k n", p=128), w=["wuq"])
            tk.dma("pool", "w3", wukv[:], w_ukv.rearrange("(k p) n -> p k n", p=128), w=["wukv"])
            tf = sb(p1, "tf", [128, D])
            nwbc = tf
            Abc = [sb(p1, f"Abc{v}", [128, D]) for v in range(2)]
            Bbc = [sb(p1, f"Bbc{v}", [128, D]) for v in range(2)]
            qnbc = sb(p1, "qnbc", [128, 384])
            kvnbc = sb(p1, "kvnbc", [128, 256])
            dtbbc = sb(p1, "dtbbc", [128, 16])
            tk.dma("sp", "c4", nwbc[:], norm_w[0:1, :].partition_broadcast(128), w=["tf"])
            tk.dma("sp", "c5", qnbc[:], q_norm[0:1, :].partition_broadcast(128), w=["qnbc"])
            tk.dma("sp", "c6", kvnbc[:], kv_norm[0:1, :].partition_broadcast(128), w=["kvnbc"])
            tk.dma("sp", "c7", dtbbc[:], ssdp[1:2, :].partition_broadcast(128), w=["dtbbc"])
            for v in range(2):
                tk.dma("sp", f"c8{v}", Bbc[v][:], MODS[v:v + 1, 0:D].partition_broadcast(128),
                       r=[("MODS", v)], w=[f"Bbc{v}"])
                tk.dma("sp", f"c9{v}", Abc[v][:], MODS[v:v + 1, D:2 * D].partition_broadcast(128),
                       r=[("MODS", v)], w=[f"Abc{v}"])
                tk.op("dve", lambda e, v=v: e.scalar_tensor_tensor(
                    out=Abc[v][:], in0=Abc[v][:], scalar=1.0, in1=nwbc[:], op0=ALU.add, op1=ALU.mult),
                    r=[f"Abc{v}", "tf"], w=[f"Abc{v}"])

            NB = 2
            xt = [sb(p1, f"xt{i}", [128, D]) for i in range(NB)]
            ropet = [sb(p1, f"ropet{i}", [128, 64]) for i in range(3)]
            junk = sb(p1, "junk", [128, D], BF16)
            stat = [sb(p1, f"stat{i}", [128, 12]) for i in range(NB)]
            hb = sb(p1, "hb", [128, D], BF16)
            hT = [sb(p1, f"hT{i}", [128, 8, 128], BF16) for i in range(NB)]
            kvn = sb(p1, "kvn", [128, 256], BF16)
            kvnT = sb(p1, "kvnT", [128, 2, 128], BF16)
            knb = sb(p1, "knb", [128, 512], BF16)
            knT_1 = sb(p1, "knT0", [128, 4, 128], BF16)
            knT = [knT_1, knT_1]
            kf = sb(p1, "kf", [128, 32])
            kr = sb(p1, "kr", [128, 64])
            kpb = sb(p1, "kpb", [128, 32], BF16)
            kpT = [sb(p1, f"kpT{i}", [32, 128], BF16) for i in range(NB)]
            dtt = sb(p1, "dtt", [128, 16])
            qan = sb(p1, "qan", [128, 384], BF16)
            qanT = sb(p1, "qanT", [128, 3, 128], BF16)
            qb = sb(p1, "qb", [128, 8, 96], BF16)
            qf = sb(p1, "qf", [128, 8, 32])
            qr = sb(p1, "qr", [128, 8, 64])
            qT_1 = sb(p1, "qT0", [128, 6, 128], BF16)
            qT = [qT_1, qT_1]
            zs_1 = sb(p1, "zs0", [128, 512], BF16)
            zs = [zs_1, zs_1]
            xbT = [sb(p1, f"xbT{i}", [128, 6, 128], BF16) for i in range(NB)]
            TP = ps(p1, "TP", [128, 1024], BF16)
            TPF = TP
            MB = ps(p1, "MB", [128, 512])
            MA = ps(p1, "MA", [128, 512])
            MZ = ps(p1, "MZ", [128, 512])
            W0 = ps(p1, "W0", [128, 1024])
            F0 = ps(p1, "F0", [128, 512])
            F1 = ps(p1, "F1", [128, 512])
            tk.op("pool", lambda e: e.memset(Vres[:].rearrange("p j h c -> p (j h c)"), 1.0), w=["Vres"])

            C_QA, C_KVA, C_KPE, C_GA, C_Z, C_XBC, C_DT = 0, 384, 640, 672, 1184, 1696, 2464

            def rms_stats(src_ap, n, st_ap, rb, wb, name):
                tk.op("act", lambda e: e.activation(out=junk[:, 0:n], in_=src_ap, func=AF.Square,
                                                    accum_out=st_ap[:, 0:1]), r=rb, w=["junk", wb])
                tk.op("act", lambda e: e.activation(out=st_ap[:, 1:2], in_=st_ap[:, 0:1], func=AF.Sqrt,
                                                    scale=1.0 / n, bias=EPS), r=[wb], w=[wb])
                tk.op("dve", lambda e: e.reciprocal(out=st_ap[:, 2:3], in_=st_ap[:, 1:2]), r=[wb], w=[wb])

            def rope_ops(src, dst, tmp, rt, nh, rb, wb, tb):
                c2 = rt[:, 0:32].unsqueeze(1).to_broadcast([128, nh, 32])
                s2a = rt[:, 32:48].unsqueeze(1).to_broadcast([128, nh, 16])
                s2b = rt[:, 48:64].unsqueeze(1).to_broadcast([128, nh, 16])
                tk.op("dve", lambda e: e.tensor_tensor(out=tmp[:, :, 0:32], in0=src, in1=c2, op=ALU.mult),
                      r=rb, w=[tb])
                tk.op("dve", lambda e: e.tensor_tensor(out=tmp[:, :, 32:48], in0=src[:, :, 16:32], in1=s2a,
                                                       op=ALU.mult), r=rb, w=[tb])
                tk.op("dve", lambda e: e.tensor_tensor(out=tmp[:, :, 48:64], in0=src[:, :, 0:16], in1=s2b,
                                                       op=ALU.mult), r=rb, w=[tb])
                tk.op("dve", lambda e: e.tensor_tensor(out=dst, in0=tmp[:, :, 0:32], in1=tmp[:, :, 32:64],
                                                       op=ALU.add), r=[tb], w=wb)

            def kind_of(j):
                return "ctx" if j < 2 else ("own" if j - 2 < NOWN else "oth")

            def front(j):
                s = j % NB
                kind = kind_of(j)
                v = 1 if kind == "ctx" else 0
                src = ctx[j * 128:(j + 1) * 128, :] if kind == "ctx" else xs[(j - 2) * 128:(j - 1) * 128, :]
                tk.dma("sp", f"xt{s}", xt[s][:], src, w=[f"xt{s}"])
                tk.dma("sp", f"rp{j % 3}", ropet[j % 3][:], rope_d[j], w=[f"ropet{j % 3}"])
                st = stat[s]
                rms_stats(xt[s][:], D, st, [f"xt{s}"], f"stat{s}a", "x")
                tk.op("dve", lambda e: e.scalar_tensor_tensor(
                    out=tf[:], in0=xt[s][:], scalar=st[:, 2:3], in1=Abc[v][:], op0=ALU.mult, op1=ALU.mult),
                    r=[f"xt{s}", f"stat{s}a", f"Abc{v}"], w=["tf"])
                tk.op("pool", lambda e: e.tensor_tensor(out=hb[:], in0=tf[:], in1=Bbc[v][:], op=ALU.add),
                      r=["tf", f"Bbc{v}"], w=["hb"])
                def _tr():
                    tk.grp("pe", [lambda e, k=k: e.transpose(TPF[:, k * 128:(k + 1) * 128], hb[:, k * 128:(k + 1) * 128],
                                                              ident[:]) for k in range(8)],
                           r=["hb", "ident"], w=["TP"])
                    tk.op("act", lambda e: e.copy(out=hT[s][:].rearrange("p k t -> p (k t)"), in_=TPF[:, :]),
                          r=["TP"], w=[f"hT{s}"])
                tk.atomic(_tr)

            def mm_all(j, part):
                s = j % NB
                kind = kind_of(j)
                hTs = hT[s]

                def tok_major(pt, c0, width, col0, name):
                    tk.grp("pe", [lambda e, k=k: e.matmul(pt[:, c0:c0 + width], lhsT=hTs[:, k, :],
                                                          rhs=win[:, k, col0:col0 + width],
                                                          start=(k == 0), stop=(k == 7)) for k in range(8)],
                           r=[f"hT{s}", "win"], w=[name])

                def feat_major(pt, c0, col0, name):
                    tk.grp("pe", [lambda e, k=k: e.matmul(pt[:, c0:c0 + 128], lhsT=win[:, k, col0:col0 + 128],
                                                          rhs=hTs[:, k, :],
                                                          start=(k == 0), stop=(k == 7)) for k in range(8)],
                           r=[f"hT{s}", "win"], w=[name])

                if part == 0:
                    tok_major(MB, 0, 288, C_KVA, "MB")
                    tok_major(MB, 288, 16, C_DT, "MB")
                    feat_major(MB, 304, C_XBC + 512, "MB")
                    return
                if part == 1:
                    if kind == "own":
                        tok_major(MA, 0, 384, C_QA, "MA")
                        feat_major(MA, 384, C_XBC + 640, "MA")
                    for c in range(2):
                        feat_major(F1, c * 128, C_XBC + c * 128, "F1")
                    return
                for c in range(2, 4):
                    feat_major(F1, c * 128, C_XBC + c * 128, "F1")
                if kind == "own":
                    tok_major(MZ, 0, 512, C_Z, "MZ")
                    tok_major(F0, 0, 512, C_GA, "F0")

            def chains_a1(j):
                s = j % NB
                st = stat[s]
                tk.op("dve", lambda e: e.tensor_copy(out=xbT[s][:, 4, :], in_=MB[:, 304:432]),
                      r=["MB"], w=[f"xbT{s}"])
                rms_stats(MB[:, 0:256], 256, st[:, 3:6], ["MB"], f"stat{s}b", "kv")
                tk.op("dve", lambda e: e.scalar_tensor_tensor(
                    out=kvn[:], in0=MB[:, 0:256], scalar=st[:, 5:6], in1=kvnbc[:], op0=ALU.mult, op1=ALU.mult),
                    r=["MB", f"stat{s}b", "kvnbc"], w=["kvn"])
                tk.op("act", lambda e: e.copy(out=kf[:], in_=MB[:, 256:288]), r=["MB"], w=["kf"])
                tk.op("dve", lambda e: e.tensor_tensor(out=dtt[:], in0=MB[:, 288:304], in1=dtbbc[:], op=ALU.add),
                      r=["MB", "dtbbc"], w=["dtt"])

            def chains_a2(j):
                s = j % NB
                kind = kind_of(j)
                st = stat[s]
                jq = j - 2
                nxc = 6 if kind == "own" else 5
                tk.op("act", lambda e: e.copy(out=xbT[s][:, 0:4, :].rearrange("p k t -> p (k t)"), in_=F1[:, :]),
                      r=["F1"], w=[f"xbT{s}"])
                if kind == "own":
                    tk.op("dve", lambda e: e.tensor_copy(out=xbT[s][:, 5, :], in_=MA[:, 384:512]),
                          r=["MA"], w=[f"xbT{s}"])
                tk.dma("sp", f"xb{s}", XBC[0:nxc * 128, j * 128:(j + 1) * 128].rearrange("(c p) t -> p c t", p=128),
                       xbT[s][:, 0:nxc, :], r=[f"xbT{s}"], w=[("XBC", j)])
                if kind == "own":
                    rms_stats(MA[:, 0:384], 384, st[:, 6:9], ["MA"], f"stat{s}c", "q")
                    tk.op("dve", lambda e: e.scalar_tensor_tensor(
                        out=qan[:], in0=MA[:, 0:384], scalar=st[:, 8:9], in1=qnbc[:], op0=ALU.mult, op1=ALU.mult),
                        r=["MA", f"stat{s}c", "qnbc"], w=["qan"])
                    tk.op("act", lambda e: e.activation(out=zs[s][:], in_=MZ[:, :], func=AF.Silu), r=["MZ"], w=["zs0"])
                    tk.dma("sp", "zs0", ZS[jq], zs[s][:], r=["zs0"], w=[("ZS", jq)])
                    tk.op("act", lambda e: e.activation(out=GAres[:, jq, :], in_=F0[:, :], func=AF.Silu),
                          r=["F0"], w=[("GAres", jq)])

            def chains_b1(j):
                s = j % NB
                kind = kind_of(j)
                st = stat[s]
                jq = j - 2
                tk.grp("pe", [lambda e, k=k: e.transpose(TP[:, k * 128:(k + 1) * 128], kvn[:, k * 128:(k + 1) * 128],
                                                          ident[:]) for k in range(2)],
                       r=["kvn", "ident"], w=["TP"])
                tk.op("dve", lambda e: e.tensor_copy(out=kvnT[:].rearrange("p k t -> p (k t)"), in_=TP[:, 0:256]),
                      r=["TP"], w=["kvnT"])
                fns = []
                for half in range(2):
                    for k in range(2):
                        fns.append(lambda e, k=k, half=half: e.matmul(
                            W0[:, half * 512:(half + 1) * 512], lhsT=kvnT[:, k, :],
                            rhs=wukv[:, k, half * 512:(half + 1) * 512], start=(k == 0), stop=(k == 1)))
                tk.grp("pe", fns, r=["kvnT", "wukv"], w=["W0"])
                kv3 = W0[:, :].rearrange("p (h c) -> p h c", h=8)
                tk.op("act", lambda e: e.copy(out=Vres[:, j, :, 0:64], in_=kv3[:, :, 64:128]),
                      r=["W0", "Vres"], w=[("V", j)])
                tk.op("dve", lambda e: e.tensor_copy(out=knb[:].rearrange("p (h c) -> p h c", h=8),
                                                     in_=kv3[:, :, 0:64]), r=["W0"], w=["knb"])

            def chains_b2(j):
                s = j % NB
                kind = kind_of(j)
                st = stat[s]
                jq = j - 2
                if kind == "own":
                    tk.grp("pe", [lambda e, k=k: e.transpose(TP[:, k * 128:(k + 1) * 128],
                                                              qan[:, k * 128:(k + 1) * 128], ident[:]) for k in range(3)],
                           r=["qan", "ident"], w=["TP"])
                    tk.op("dve", lambda e: e.tensor_copy(out=qanT[:].rearrange("p k t -> p (k t)"), in_=TP[:, 0:384]),
                          r=["TP"], w=["qanT"])
                    fns = []
                    for (c0, wd) in ((0, 512), (512, 256)):
                        for k in range(3):
                            fns.append(lambda e, k=k, c0=c0, wd=wd: e.matmul(
                                W0[:, c0:c0 + wd], lhsT=qanT[:, k, :], rhs=wuq[:, k, c0:c0 + wd],
                                start=(k == 0), stop=(k == 2)))
                    tk.grp("pe", fns, r=["qanT", "wuq"], w=["W0"])
                tk.grp("pe", [lambda e, k=k: e.transpose(TP[:, k * 128:(k + 1) * 128], knb[:, k * 128:(k + 1) * 128],
                                                          ident[:]) for k in range(4)],
                       r=["knb", "ident"], w=["TP"])
                tk.op("act", lambda e: e.copy(out=knT[s][:].rearrange("p k t -> p (k t)"), in_=TP[:, 0:512]),
                      r=["TP"], w=["knT0"])
                tk.dma("sp", f"kt{s}", KT[:, j * 128:(j + 1) * 128].rearrange("(c p) t -> p c t", p=128),
                       knT[s][:], r=["knT0"], w=[("KT", j)])
                rope_ops(kf[:].unsqueeze(1), kpb[:].unsqueeze(1), kr[:].unsqueeze(1), ropet[j % 3], 1,
                         ["kf", f"ropet{j % 3}"], ["kpb"], "kr")
                tk.op("pe", lambda e: e.transpose(TP[0:32, 0:128], kpb[:, :], ident[:]),
                      r=["kpb", "ident"], w=["TP"])
                tk.op("dve", lambda e: e.tensor_copy(out=kpT[s][:], in_=TP[0:32, 0:128]), r=["TP"], w=[f"kpT{s}"])
                tk.dma("sp", f"kp{s}", KPE[:, j * 128:(j + 1) * 128], kpT[s][:], r=[f"kpT{s}"], w=[("KPE", j)])
                tk.op("act", lambda e: e.activation(out=dtt[:], in_=dtt[:], func=AF.Exp), r=["dtt"], w=["dtt"])
                tk.op("act", lambda e: e.activation(out=DT[:, j, :], in_=dtt[:], func=AF.Ln, bias=1.0),
                      r=["dtt"], w=[("DT", j)])
                if kind != "own":
                    return
                q3 = W0[:, 0:768].rearrange("p (h c) -> p h c", h=8)
                tk.op("act", lambda e: e.activation(out=qb[:, :, 0:64], in_=q3[:, :, 0:64], func=AF.Copy,
                                                    scale=QSCALE), r=["W0"], w=["qb"])
                tk.op("act", lambda e: e.activation(out=qf[:], in_=q3[:, :, 64:96], func=AF.Copy, scale=QSCALE),
                      r=["W0"], w=["qf"])
                rope_ops(qf[:], qb[:, :, 64:96], qr[:], ropet[j % 3], 8, ["qf", f"ropet{j % 3}"], ["qb"], "qr")
                qb2 = qb[:].rearrange("p h c -> p (h c)")
                tk.grp("pe", [lambda e, k=k: e.transpose(TP[:, k * 128:(k + 1) * 128], qb2[:, k * 128:(k + 1) * 128],
                                                          ident[:]) for k in range(6)],
                       r=["qb", "ident"], w=["TP"])
                tk.op("act", lambda e: e.copy(out=qT[s][:].rearrange("p k t -> p (k t)"), in_=TP[:, 0:768]),
                      r=["TP"], w=["qT0"])
                tk.dma("sp", f"qt{s}", QT[:, jq * 128:(jq + 1) * 128].rearrange("(c p) t -> p c t", p=128),
                       qT[s][:], r=["qT0"], w=[("QT", jq)])

            def mm_front(j):
                if j < NSLOT:
                    mm_all(j)
                if j + 1 < NSLOT:
                    front(j + 1)

            def mm_front(j):
                if j < NSLOT:
                    mm_all(j)
                if j + 1 < NSLOT:
                    front(j + 1)

            front(0)
            for j in range(NSLOT):
                own = kind_of(j) == "own"
                tk.scratch.add("TP")
                mm_all(j, 0)
                chains_a1(j)
                m1 = tk.record(mm_all, j, 1)
                m2 = tk.record(mm_all, j, 2)
                b1 = tk.record(chains_b1, j)
                a2 = tk.record(chains_a2, j)
                b2 = tk.record(chains_b2, j)
                fr = tk.record(front, j + 1) if j + 1 < NSLOT else []
                tk.scratch.discard("TP")
                assert len(b1) == 4
                if own:
                    assert len(m1) == 4 and len(m2) == 4 and len(a2) == 10 and len(b2) == 20, (len(m1), len(m2), len(a2), len(b2))
                    seq = (m1[0:2] + a2[3:7] + m1[2:4] + b1[0:1] + m2[0:2] + b1[1:4] + fr + m2[2:3] + b2[0:1] + b2[2:4]
                           + m2[3:4] + b2[1:2] + b2[4:12] + a2[0:3] + a2[7:10] + b2[12:20])
                else:
                    assert len(m1) == 2 and len(m2) == 2 and len(a2) == 2 and len(b2) == 10, (len(m1), len(m2), len(a2), len(b2))
                    seq = m1 + b1[0:1] + m2 + b1[1:4] + fr + a2 + b2
                for f_ in seq:
                    f_()
            if dbg:
                VSd = dscr("VS", [128, NSLOT * 8 * VP])
                tk.dma("sp", "vsd", VSd[:, :], Vres[:].rearrange("p j h c -> p (j h c)"),
                       r=[("V", j) for j in range(NSLOT)], w=["VSd"])
                DTS = dscr("DTS", [128, NSLOT * 16], F32)
                tk.dma("sp", "dts", DTS[:, :], DT[:].rearrange("p j c -> p (j c)"),
                       r=[("DT", j) for j in range(NSLOT)], w=["DTS"])
        tk.barrier()
        if stop < 2:
            return nc

        AT2 = dscr("AT2", [NOWN, 128, 512])
        with ExitStack() as p3:
            KH = [sb(p3, f"KH{i}", [96, NKEY], BF16) for i in range(2)]
            QH = [sb(p3, f"QH{i}", [96, NQ], BF16) for i in range(2)]
            ATres = sb(p3, "ATres", [128, NOWN, 512], BF16)
            NPT = 3
            PT = [sb(p3, f"PT{i}", [128, 2, 512], BF16) for i in range(3)]
            zb = sb(p3, "zb", [128, 512], BF16)
            rcp = sb(p3, "rcp", [128, 4])
            t1 = sb(p3, "t1", [128, 4, 64])
            ST = [ps(p3, f"ST{i}", [128, 1024]) for i in range(NPT)]
            OT = [ps(p3, f"OT{i}", [128, 512]) for i in range(2)]
            tk.op("pool", lambda e: e.memset(zb[:], 0.0), w=["zb"])
            for i in range(2):
                tk.dma("sp", f"khr{i}", KH[i][64:96, :], KPE[:, :], w=[f"KHr{i}"])
            blocks = [(b * 4, 4) for b in range(NOWN // 4)]
            if NOWN % 4:
                blocks.append(((NOWN // 4) * 4, NOWN % 4))

            def load_head(h):
                i = h % 2
                tk.dma("sp", f"kh{i}", KH[i][0:64, :], KT[h * 64:(h + 1) * 64, :], w=[f"KH{i}"])
                tk.dma("sp", f"qh{i}", QH[i][:, :], QT[h * 96:(h + 1) * 96, :], w=[f"QH{i}"])

            load_head(0)
            bi = 0
            for h in range(8):
                i = h % 2
                if h + 1 < 8:
                    load_head(h + 1)
                for (t0, nt) in blocks:
                    q0, qw = t0 * 128, nt * 128
                    ob = bi % 2
                    bi += 1
                    tk.op("pe", lambda e: e.matmul(OT[ob][:, :], lhsT=zb[:, 0:128], rhs=zb[:, :], start=True, stop=True),
                          r=["zb"], w=[f"OT{ob}"])
                    NPAIR = NSLOT // 2

                    def S(pk):
                        b_ = pk % NPT
                        tk.grp("pe", [lambda e, u=u: e.matmul(ST[b_][:, u * 512:u * 512 + qw],
                                                              lhsT=KH[i][0:96, (2 * pk + u) * 128:(2 * pk + u + 1) * 128],
                                                              rhs=QH[i][0:96, q0:q0 + qw], start=True, stop=True)
                                      for u in range(2)],
                               r=[f"KH{i}", f"KHr{i}", f"QH{i}"], w=[f"ST{b_}"])

                    S(0)
                    S(1)
                    for pk in range(NPAIR):
                        b_ = pk % NPT
                        pb_ = pk % 3
                        if pk + 2 < NPAIR:
                            S(pk + 2)
                        tk.op("act", lambda e: e.activation(
                            out=PT[pb_][:, :, 0:qw], in_=ST[b_][:, :].rearrange("p (u c) -> p u c", u=2)[:, :, 0:qw],
                            func=AF.Exp), r=[f"ST{b_}"], w=[f"PT{pb_}"])
                        tk.grp("pe", [lambda e, qt=qt, u=u: e.matmul(OT[ob][:, qt * 128:qt * 128 + 65],
                                                                      lhsT=PT[pb_][:, u, qt * 128:(qt + 1) * 128],
                                                                      rhs=Vres[:, 2 * pk + u, h, 0:65], start=False,
                                                                      stop=True, skip_group_check=True)
                                      for u in range(2) for qt in range(nt)], r=[f"PT{pb_}"], w=[f"OT{ob}"])
                    O3 = OT[ob][:, :].rearrange("p (t c) -> p t c", t=4)
                    tk.op("dve", lambda e: e.reciprocal(out=rcp[:, 0:nt], in_=O3[:, 0:nt, 64]), r=[f"OT{ob}"], w=["rcp"])
                    tk.op("dve", lambda e: e.tensor_tensor(out=t1[:, 0:nt, :], in0=O3[:, 0:nt, 0:64],
                                                           in1=rcp[:, 0:nt].unsqueeze(2).to_broadcast([128, nt, 64]),
                                                           op=ALU.mult), r=[f"OT{ob}", "rcp"], w=["t1"])
                    tk.op("pool", lambda e: e.tensor_tensor(out=ATres[:, t0:t0 + nt, h * 64:(h + 1) * 64], in0=t1[:, 0:nt, :],
                                                            in1=GAres[:, t0:t0 + nt, h * 64:(h + 1) * 64], op=ALU.mult),
                          r=["t1"], w=["ATres"])
            tk.dma("sp", "at2", AT2.rearrange("j p c -> p j c"), ATres[:], r=["ATres"], w=["AT2"])
        tk.barrier()
        p13.close()
        if stop < 3:
            return nc

        with ExitStack() as p2:
            U5 = sb(p2, "U5", [128, 5, NKEY], BF16)
            UC = sb(p2, "UC", [128, NQ], BF16)
            cw = sb(p2, "cw", [128, 6, 4])
            masks = sb(p2, "masks", [128, 2, 128])
            negm = sb(p2, "negm", [128, 2, 1024], BF16)
            onesf = sb(p2, "onesf2", [128, 128])
            Abc16 = sb(p2, "Abc16", [128, 16])
            skip16 = sb(p2, "skip16", [128, 16])
            skip8 = sb(p2, "skip8", [128, 8])
            ssdnbc = sb(p2, "ssdnbc", [128, 512])
            Hs = [sb(p2, f"Hs{d}", [128, 256]) for d in range(2)]
            Hb = [sb(p2, f"Hb{d}", [128, 512], BF16) for d in range(2)]
            Cblk = sb(p2, "Cblk", [128, 256], BF16)
            Yb = sb(p2, "Yb", [128, NOWN, 512], BF16)
            a8 = sb(p2, "a8", [128, 8])
            acs16 = sb(p2, "acs16", [128, 16])
            ahl = sb(p2, "ahl", [128, 16], BF16)
            masksb = sb(p2, "masksb", [128, 2, 128], BF16)
            w8 = sb(p2, "w8", [128, 8])
            rhsall = sb(p2, "rhsall", [128, 1024])
            acs8s = sb(p2, "acs8s", [128, 8])
            diff = sb(p2, "diff", [128, 1024])
            Ee = sb(p2, "Ee", [128, 1024])
            xd = sb(p2, "xd", [128, 512], BF16)
            xdw = sb(p2, "xdw", [128, 512], BF16)
            Bm = sb(p2, "Bm", [128, 128], BF16)
            MT = sb(p2, "MT", [128, 1024], BF16)
            ty = sb(p2, "ty", [128, 512])
            ysum = sb(p2, "ysum", [128, 512])
            zt = [sb(p2, f"zt{i}", [128, 512], BF16) for i in range(2)]
            st2 = sb(p2, "st2", [128, 4])
            junk2 = sb(p2, "junk2", [128, 512], BF16)
            yn = sb(p2, "yn", [128, 512], BF16)
            ynT = [sb(p2, f"ynT{i}", [128, 4, 128], BF16) for i in range(2)]
            TPs = ps(p2, "TPs", [128, 1024], BF16)
            ACS = ps(p2, "ACS", [128, 1024])
            G8 = ps(p2, "G8", [128, 512])
            Yp = ps(p2, "Yp", [128, 512])
            YO = ps(p2, "YO", [128, 512])
            SP = ps(p2, "SP", [128, 512])

            tk.dma("sp", "k0", cw[:], cwt_d[:, :, :], w=["cw"])
            tk.dma("sp", "k1", masks[:], masks_d[:, :, :], w=["masks"])
            tk.dma("sp", "k2", negm[:], negm_d[:, :, :], w=["negm"])
            tk.dma("sp", "k3", Abc16[:], ssdp[0:1, :].partition_broadcast(128), w=["Abc16"])
            tk.dma("sp", "k4", skip16[:], ssdp[2:3, :].partition_broadcast(128), w=["skip16"])
            tk.dma("sp", "k5", ssdnbc[:], ssd_norm[0:1, :].partition_broadcast(128), w=["ssdnbc"])
            tk.op("pool", lambda e: e.memset(onesf[:], 1.0), w=["onesf"])
            tk.op("dve", lambda e: e.tensor_copy(out=masksb[:], in_=masks[:]), r=["masks"], w=["masksb"])
            tk.op("act", lambda e: e.activation(out=Abc16[:], in_=Abc16[:], func=AF.Exp), r=["Abc16"], w=["Abc16"])
            tk.op("dve", lambda e: e.tensor_scalar(out=Abc16[:], in0=Abc16[:], scalar1=-1.0, scalar2=None,
                                                   op0=ALU.mult), r=["Abc16"], w=["Abc16"])
            tk.op("dve", lambda e: e.tensor_tensor(out=skip8[:], in0=skip16[:, 0:8], in1=skip16[:, 8:16], op=ALU.add),
                  r=["skip16"], w=["skip8"])
            for d in range(2):
                tk.op("pool", lambda e, d=d: e.memset(Hs[d][:], 0.0), w=[f"Hs{d}"])
                tk.op("pool", lambda e, d=d: e.memset(Hb[d][:], 0.0), w=[f"Hb{d}"])
            tk.op("pool", lambda e: e.memset(Cblk[:], 0.0), w=["Cblk"])

            pc = p2.enter_context(ExitStack())
            raw = [sb(pc, f"raw{i}", [128, 6, 1026], BF16) for i in range(2)]
            cacc = [sb(pc, f"cacc{i}", [128, 1024]) for i in range(2)]
            spans = [(0, 256, 5, 0, 256)]
            a = 256
            while a < 256 + NQ:
                n = min(1024, 256 + NQ - a)
                spans.append((a, n, 6, 256, NKEY))
                a += n
            while a < NKEY:
                n = min(1024, NKEY - a)
                spans.append((a, n, 5, 256, NKEY))
                a += n
            CEND = 256 + NQ
            if _CUT2 <= 1:
                spans = []
            for si, (a, n, nch, s0, s1) in enumerate(spans):
                rs = si % 2
                rw = raw[rs]
                tk.op("pool", lambda e: e.memset(rw[:, :, 0:1], 0.0), w=[f"raw{rs}"])
                tk.op("pool", lambda e: e.memset(rw[:, :, n + 1:n + 2], 0.0), w=[f"raw{rs}"])
                lo, hi = max(a - 1, s0), min(a + n + 1, s1)
                tk.dma("sp", f"raw{rs}", rw[:, 0:5, lo - (a - 1):hi - (a - 1)],
                       XBC[0:640, lo:hi].rearrange("(c p) t -> p c t", p=128), w=[f"raw{rs}"])
                if nch == 6:
                    hic = min(a + n + 1, CEND)
                    tk.dma("sp", f"rawc{rs}", rw[:, 5, lo - (a - 1):hic - (a - 1)],
                           XBC[640:768, lo:hic], w=[f"raw{rs}"])
                for c in range(nch):
                    ca = cacc[c % 2]
                    cn = f"cacc{c % 2}"
                    tk.op("dve", lambda e: e.tensor_scalar(out=ca[:, 0:n], in0=rw[:, c, 1:n + 1], scalar1=cw[:, c, 1:2],
                                                           scalar2=cw[:, c, 3:4], op0=ALU.mult, op1=ALU.add),
                          r=[f"raw{rs}", "cw"], w=[cn])
                    tk.op("dve", lambda e: e.scalar_tensor_tensor(out=ca[:, 0:n], in0=rw[:, c, 0:n], scalar=cw[:, c, 0:1],
                                                                  in1=ca[:, 0:n], op0=ALU.mult, op1=ALU.add),
                          r=[f"raw{rs}", "cw", cn], w=[cn])
                    tk.op("dve", lambda e: e.scalar_tensor_tensor(out=ca[:, 0:n], in0=rw[:, c, 2:n + 2], scalar=cw[:, c, 2:3],
                                                                  in1=ca[:, 0:n], op0=ALU.mult, op1=ALU.add),
                          r=[f"raw{rs}", "cw", cn], w=[cn])
                    dst = U5[:, c, a:a + n] if c < 5 else UC[:, a - 256:a - 256 + n]
                    tk.op("act", lambda e: e.activation(out=dst, in_=ca[:, 0:n], func=AF.Silu), r=[cn], w=["U"])

            tk.barrier()
            pc.close()
            dec2 = [sb(p2, f"dec{i}", [128, 4]) for i in range(2)]
            ea82 = [sb(p2, f"ea8{i}", [128, 8]) for i in range(2)]
            ydg = [sb(p2, f"ydg{i}", [128, 512]) for i in range(2)]
            sps = [sb(p2, f"sps{i}", [128, 256]) for i in range(2)]
            tsk2 = [sb(p2, f"tsk{i}", [128, 512]) for i in range(2)]

            def ssd_front(j, d, need_out, final, par):
                cols = slice(j * 128, (j + 1) * 128)
                jq = j - 2
                qcols = slice(jq * 128, (jq + 1) * 128)
                last = 127 if d == 0 else 0
                dt8 = DT[:, j, d * 8:(d + 1) * 8]
                tk.grp("pe", [lambda e, c=c: e.transpose(TPs[:, c * 128:(c + 1) * 128], U5[:, c, cols], ident[:])
                              for c in range(5)], r=["U", "ident"], w=["TPs"])
                tk.op("dve", lambda e: e.tensor_tensor(out=a8[:], in0=dt8, in1=Abc16[:, d * 8:(d + 1) * 8], op=ALU.mult),
                      r=["Abc16"], w=["a8"])
                if need_out:
                    tk.op("act", lambda e: e.copy(out=ahl[:, 0:8], in_=a8[:]), r=["a8"], w=["ahl"])
                    tk.op("dve", lambda e: e.tensor_tensor(out=ahl[:, 8:16], in0=a8[:], in1=ahl[:, 0:8], op=ALU.subtract),
                          r=["a8", "ahl"], w=["ahl"])
                    fns = []
                    for h_ in range(8):
                        for u in range(2):
                            fns.append(lambda e, h_=h_, u=u: e.matmul(
                                ACS[:, h_ * 128:(h_ + 1) * 128],
                                lhsT=ahl[:, u * 8 + h_:u * 8 + h_ + 1].to_broadcast([128, 128]),
                                rhs=masksb[:, d, :], start=(h_ % 4 == 0 and u == 0), stop=False, skip_group_check=True))
                    for hf in range(2):
                        fns.append(lambda e, hf=hf: e.matmul(ACS[:, hf * 512:(hf + 1) * 512], lhsT=ident[:, :],
                                                             rhs=negm[:, d, hf * 512:(hf + 1) * 512], start=False, stop=True,
                                                             skip_group_check=True))
                    tk.grp("pe", fns, r=["ahl", "masksb", "negm", "ident"], w=["ACS"])
                    tk.op("pe", lambda e: e.matmul(G8[:, 256:264], lhsT=masks[:, d, :], rhs=a8[:, :], start=True, stop=True),
                          r=["masks", "a8"], w=["G8"])
                    tk.op("act", lambda e: e.copy(out=acs8s[:], in_=G8[:, 256:264]), r=["G8"], w=["acs8s"])
                    ACS3 = ACS[:, :].rearrange("p (h l) -> p h l", h=8)
                    tk.op("dve", lambda e: e.tensor_tensor(out=diff[:].rearrange("p (h l) -> p h l", h=8), in0=ACS3,
                                                           in1=acs8s[:].unsqueeze(2).to_broadcast([128, 8, 128]),
                                                           op=ALU.subtract), r=["ACS", "acs8s"], w=["diff"])
                    for g in range(2):
                        tk.op("act", lambda e, g=g: e.activation(out=dec2[par][g * 64:(g + 1) * 64, :],
                                                                 in_=ACS3[g * 64:(g + 1) * 64, g * 4:(g + 1) * 4, last],
                                                                 func=AF.Exp), r=["ACS"], w=[f"dec{par}"])
                    tk.op("act", lambda e: e.activation(out=Ee[:], in_=diff[:], func=AF.Exp), r=["diff"], w=["Ee"])
                    E3 = Ee[:].rearrange("p (h l) -> p h l", h=8)
                    wsrc = E3[:, :, last]
                else:
                    tk.grp("pe", [lambda e: e.matmul(G8[:, 256:264], lhsT=masks[:, d, :], rhs=a8[:, :], start=True, stop=True),
                                  lambda e: e.matmul(G8[:, 264:272], lhsT=onesf[:, :], rhs=a8[:, :], start=True, stop=True)],
                           r=["masks", "onesf", "a8"], w=["G8"])
                    tk.op("act", lambda e: e.copy(out=acs16[:], in_=G8[:, 256:272]), r=["G8"], w=["acs16"])
                    tk.op("dve", lambda e: e.tensor_tensor(out=w8[:], in0=acs16[:, 8:16], in1=acs16[:, 0:8], op=ALU.subtract),
                          r=["acs16"], w=["w8"])
                    tk.op("act", lambda e: e.activation(out=w8[:], in_=w8[:], func=AF.Exp), r=["w8"], w=["w8"])
                    for g in range(2):
                        tk.op("act", lambda e, g=g: e.activation(out=dec2[par][g * 64:(g + 1) * 64, :],
                                                                 in_=acs16[g * 64:(g + 1) * 64, 8 + g * 4:8 + (g + 1) * 4],
                                                                 func=AF.Exp), r=["acs16"], w=[f"dec{par}"])
                    wsrc = w8[:, :]
                xs3 = TPs[:, 0:512].rearrange("p (h c) -> p h c", h=8)
                tk.op("dve", lambda e: e.tensor_tensor(out=xd[:].rearrange("p (h c) -> p h c", h=8), in0=xs3,
                                                       in1=dt8.unsqueeze(2).to_broadcast([128, 8, 64]), op=ALU.mult),
                      r=["TPs"], w=["xd"])
                tk.op("act", lambda e: e.copy(out=Bm[:], in_=TPs[:, 512:640]), r=["TPs"], w=["Bm"])
                if final:
                    tk.op("dve", lambda e: e.tensor_tensor(out=tsk2[par][:].rearrange("p (h c) -> p h c", h=8), in0=xs3,
                                                           in1=skip8[:].unsqueeze(2).to_broadcast([128, 8, 64]),
                                                           op=ALU.mult), r=["TPs", "skip8"], w=[f"tsk{par}"])
                tk.op("pool", lambda e: e.tensor_tensor(out=xdw[:].rearrange("p (h c) -> p h c", h=8),
                                                        in0=xd[:].rearrange("p (h c) -> p h c", h=8),
                                                        in1=wsrc.unsqueeze(2).to_broadcast([128, 8, 64]),
                                                        op=ALU.mult), r=["xd", "Ee", "w8"], w=["xdw"])
                tk.op("pe", lambda e: e.matmul(SP[:, :], lhsT=Bm[:, :], rhs=xdw[:, :], start=True, stop=True),
                      r=["Bm", "xdw"], w=["SP"])
                for g in range(2):
                    tk.op("act", lambda e, g=g: e.copy(out=sps[par][g * 64:(g + 1) * 64, :],
                                                       in_=SP[g * 64:(g + 1) * 64, g * 256:(g + 1) * 256]),
                          r=["SP"], w=[f"sps{par}"])
                if need_out:
                    tk.op("act", lambda e: e.activation(out=ea82[par][:], in_=acs8s[:], func=AF.Exp),
                          r=["acs8s"], w=[f"ea8{par}"])
                    for g in range(2):
                        tk.op("pool", lambda e, g=g: e.tensor_copy(out=Cblk[g * 64:(g + 1) * 64, g * 128:(g + 1) * 128],
                                                                  in_=UC[g * 64:(g + 1) * 64, qcols]),
                              r=["U"], w=["Cblk"])
                    tk.op("pe", lambda e: e.matmul(G8[:, 0:256], lhsT=U5[:, 4, cols], rhs=Cblk[:, :],
                                                   start=True, stop=True), r=["U", "Cblk"], w=["G8"])
                    tk.op("dve", lambda e: e.tensor_tensor(
                        out=MT[:].rearrange("p (g r l) -> p g r l", g=2, r=4),
                        in0=Ee[:].rearrange("p (g r l) -> p g r l", g=2, r=4),
                        in1=G8[:, 0:256].rearrange("p (g l) -> p g l", g=2).unsqueeze(2).to_broadcast([128, 2, 4, 128]),
                        op=ALU.mult), r=["Ee", "G8"], w=["MT"])
                    tk.grp("pe", [lambda e, h=h: e.matmul(Yp[:, h * 64:(h + 1) * 64], lhsT=MT[:, h * 128:(h + 1) * 128],
                                                          rhs=xd[:, h * 64:(h + 1) * 64], start=True, stop=True)
                                  for h in range(8)], r=["MT", "xd"], w=["Yp"])
                    tk.op("act", lambda e: e.copy(out=ydg[par][:], in_=Yp[:, :]), r=["Yp"], w=[f"ydg{par}"])

            def ssd_back(j, d, need_out, final, par):
                jq = j - 2
                qcols = slice(jq * 128, (jq + 1) * 128)
                if need_out:
                    tk.op("pe", lambda e: e.matmul(YO[:, :], lhsT=UC[:, qcols], rhs=Hb[d][:, :], start=True, stop=True),
                          r=["U", f"Hb{d}"], w=["YO"])
                    tk.op("dve", lambda e: e.tensor_tensor(out=ty[:].rearrange("p (h c) -> p h c", h=8),
                                                           in0=YO[:, :].rearrange("p (h c) -> p h c", h=8),
                                                           in1=ea82[par][:].unsqueeze(2).to_broadcast([128, 8, 64]),
                                                           op=ALU.mult), r=["YO", f"ea8{par}"], w=["ty"])
                H3 = Hs[d][:].rearrange("p (r c) -> p r c", r=4)
                tk.op("dve", lambda e: e.tensor_tensor(out=H3, in0=H3,
                                                       in1=dec2[par][:].unsqueeze(2).to_broadcast([128, 4, 64]),
                                                       op=ALU.mult), r=[f"Hs{d}", f"dec{par}"], w=[f"Hs{d}"])
                tk.op("dve", lambda e: e.tensor_tensor(out=Hs[d][:], in0=Hs[d][:], in1=sps[par][:], op=ALU.add),
                      r=[f"Hs{d}", f"sps{par}"], w=[f"Hs{d}"])
                for g in range(2):
                    tk.op("act", lambda e, g=g: e.copy(out=Hb[d][g * 64:(g + 1) * 64, g * 256:(g + 1) * 256],
                                                       in_=Hs[d][g * 64:(g + 1) * 64, :]), r=[f"Hs{d}"], w=[f"Hb{d}"])
                if not need_out:
                    return
                if not final:
                    tk.op("pool", lambda e: e.tensor_tensor(out=Yb[:, jq, :], in0=ty[:], in1=ydg[par][:], op=ALU.add),
                          r=["ty", f"ydg{par}"], w=[("Yb", jq)])
                    return
                zi = jq % 2
                tk.dma("sp", f"zt{zi}", zt[zi][:], ZS[jq], w=[f"zt{zi}"])
                tk.op("pool", lambda e: e.tensor_tensor(out=ysum[:], in0=ty[:], in1=ydg[par][:], op=ALU.add),
                      r=["ty", f"ydg{par}"], w=["ysum"])
                tk.op("pool", lambda e: e.tensor_tensor(out=ysum[:], in0=ysum[:], in1=Yb[:, jq, :], op=ALU.add),
                      r=["ysum", ("Yb", jq)], w=["ysum"])
                tk.op("pool", lambda e: e.tensor_tensor(out=ysum[:], in0=ysum[:], in1=tsk2[par][:], op=ALU.add),
                      r=["ysum", f"tsk{par}"], w=["ysum"])
                tk.op("pool", lambda e: e.tensor_tensor(out=ysum[:], in0=ysum[:], in1=zt[zi][:], op=ALU.mult),
                      r=["ysum", f"zt{zi}"], w=["ysum"])
                tk.op("act", lambda e: e.activation(out=junk2[:], in_=ysum[:], func=AF.Square,
                                                    accum_out=st2[:, 0:1]), r=["ysum"], w=["junk2", "st2"])
                tk.op("act", lambda e: e.activation(out=st2[:, 1:2], in_=st2[:, 0:1], func=AF.Sqrt,
                                                    scale=1.0 / 512, bias=EPS), r=["st2"], w=["st2"])
                tk.op("dve", lambda e: e.reciprocal(out=st2[:, 2:3], in_=st2[:, 1:2]), r=["st2"], w=["st2"])
                tk.op("dve", lambda e: e.scalar_tensor_tensor(out=yn[:], in0=ysum[:], scalar=st2[:, 2:3],
                                                              in1=ssdnbc[:], op0=ALU.mult, op1=ALU.mult),
                      r=["ysum", "st2", "ssdnbc"], w=["yn"])
                tk.grp("pe", [lambda e, c=c: e.transpose(YO[:, :].bitcast(BF16)[:, c * 128:(c + 1) * 128],
                                                          yn[:, c * 128:(c + 1) * 128], ident[:])
                              for c in range(4)], r=["yn", "ident"], w=["YO"])
                tk.op("act", lambda e: e.copy(out=ynT[zi][:].rearrange("p c t -> p (c t)"),
                                              in_=YO[:, :].bitcast(BF16)[:, 0:512]), r=["YO"], w=[f"ynT{zi}"])
                tk.dma("sp", f"yt{zi}", YT[:, qcols].rearrange("(c p) t -> p c t", p=128), ynT[zi][:],
                       r=[f"ynT{zi}"], w=[("YT", jq)])

            steps = [(1, 1, False, False), (0, 1, False, False)]
            steps += [(j, 1, (j - 2) < NOWN, False) for j in range(NSLOT - 1, 1, -1)]
            steps += [(0, 0, False, False), (1, 0, False, False)]
            steps += [(j, 0, True, True) for j in range(2, 2 + NOWN)]
            ssd_front(*steps[0], 0)
            for k, stp in enumerate(steps):
                fo = tk.record(ssd_front, *steps[k + 1], (k + 1) % 2) if k + 1 < len(steps) else []
                bo = tk.record(ssd_back, *stp, k % 2)
                tk.interleave(fo, bo)
        tk.barrier()
        if stop < 4:
            return nc

        with ExitStack() as p4:
            wo = sb(p4, "wo", [128, 8, D], BF16)
            wip = sb(p4, "wip", [128, 8, 2 * D], BF16)
            lin = sb(p4, "lin", [128, 8, 256], BF16)
            wop = sb(p4, "wop", [128, 8, D], BF16)
            poolm = sb(p4, "poolm", [128, 36, 128], BF16)
            tk.dma("pool", "w1", wo[:], w_out_mix.rearrange("(k p) n -> p k n", p=128), w=["wo"])
            tk.dma("pool", "w2", wip[:], w_in_pool.rearrange("(k p) n -> p k n", p=128), w=["wip"])
            tk.dma("pool", "w3", lin[:], pool_lin.rearrange("g (kk p) n -> p (g kk) n", p=128), w=["lin"])
            tk.dma("pool", "w4", wop[:], w_out_pool.rearrange("(k p) n -> p k n", p=128), w=["wop"])
            tk.dma("sp", "c0", poolm[:], poolm_d[:, :, :], w=["poolm"])
            g0bc = sb(p4, "g0bc", [128, D])
            A1bc = sb(p4, "A1bc", [128, D])
            B1bc = sb(p4, "B1bc", [128, D])
            g1bc = sb(p4, "g1bc", [128, D])
            fnbc = sb(p4, "fnbc", [128, D])
            psbc = sb(p4, "psbc", [128, D])
            nw1 = sb(p4, "nw1", [128, D])
            tk.dma("sp", "c1", g0bc[:], MODS[0:1, 2 * D:3 * D].partition_broadcast(128), w=["g0bc"])
            tk.dma("sp", "c2", B1bc[:], MODS[2:3, 0:D].partition_broadcast(128), w=["B1bc"])
            tk.dma("sp", "c3", A1bc[:], MODS[2:3, D:2 * D].partition_broadcast(128), w=["A1bc"])
            tk.dma("sp", "c4", g1bc[:], MODS[2:3, 2 * D:3 * D].partition_broadcast(128), w=["g1bc"])
            tk.dma("sp", "c5", fnbc[:], final_norm[0:1, :].partition_broadcast(128), w=["fnbc"])
            tk.dma("sp", "c6", psbc[:], pool_scale[0:1, :].partition_broadcast(128), w=["psbc"])
            tk.dma("sp", "c7", nw1[:], norm_w[1:2, :].partition_broadcast(128), w=["nw1"])
            tk.op("dve", lambda e: e.scalar_tensor_tensor(out=A1bc[:], in0=A1bc[:], scalar=1.0, in1=nw1[:],
                                                          op0=ALU.add, op1=ALU.mult), r=["A1bc", "nw1"], w=["A1bc"])
            tk.op("dve", lambda e: e.tensor_tensor(out=wo[:], in0=wo[:], in1=g0bc[:].unsqueeze(1).to_broadcast([128, 8, D]),
                                                   op=ALU.mult), r=["wo", "g0bc"], w=["wo"])
            tk.op("pool", lambda e: e.tensor_tensor(out=wop[:], in0=wop[:], in1=g1bc[:].unsqueeze(1).to_broadcast([128, 8, D]),
                                                    op=ALU.mult), r=["wop", "g1bc"], w=["wop"])
            lin4 = lin[:].rearrange("p (g k) n -> p g k n", g=4)
            ps4 = psbc[:].rearrange("p (g n) -> p g n", g=4).unsqueeze(2).to_broadcast([128, 4, 2, 256])
            tk.op("dve", lambda e: e.tensor_tensor(out=lin4, in0=lin4, in1=ps4, op=ALU.mult), r=["lin", "psbc"], w=["lin"])
            xt4 = [sb(p4, f"xt4{i}", [128, D]) for i in range(2)]
            at4 = [sb(p4, f"at4{i}", [128, 4, 128], BF16) for i in range(2)]
            atk = [sb(p4, f"atk{i}", [128, 512], BF16) for i in range(2)]
            yT4 = [sb(p4, f"yT4{i}", [128, 4, 128], BF16) for i in range(2)]
            tfa = sb(p4, "tfa", [128, D])
            tfb = sb(p4, "tfb", [128, D])
            x1 = [sb(p4, f"x1{i}", [128, D]) for i in range(3)]
            sta = sb(p4, "sta", [128, 4])
            stb = sb(p4, "stb", [128, 4])
            junka = sb(p4, "junka", [128, D], BF16)
            junkb = sb(p4, "junkb", [128, D], BF16)
            hb4 = sb(p4, "hb4", [128, D], BF16)
            hT4 = sb(p4, "hT4", [128, 8, 128], BF16)
            ub = [sb(p4, f"ub{i}", [128, D], BF16) for i in range(4)]
            sg = [sb(p4, f"sg{i}", [128, D], BF16) for i in range(3)]
            mTb = sb(p4, "mTb", [128, 8, 128], BF16)
            pb4 = sb(p4, "pb4", [128, D], BF16)
            pT4 = sb(p4, "pT4", [128, 8, 128], BF16)
            x2 = sb(p4, "x2", [128, D])
            ot = [sb(p4, f"ot{i}", [128, D]) for i in range(2)]
            PA = ps(p4, "PA", [128, 1024])
            PB = ps(p4, "PB", [128, 1024])
            PC = ps(p4, "PC", [128, 1024])
            PD = ps(p4, "PD", [128, 512])
            TP4 = ps(p4, "TP4", [128, 1024], BF16)

            def bundle2(ops):
                a_, b_ = ops[-2], ops[-1]
                ops[-2:] = [lambda: (a_(), b_())]

            def rms4(ops, src, name_r, st_, stn, jk, jkn):
                ops.append(lambda: tk.op("act", lambda e: e.activation(out=jk[:], in_=src, func=AF.Square,
                                                                         accum_out=st_[:, 0:1]), r=[name_r], w=[jkn, stn]))
                ops.append(lambda: tk.op("act", lambda e: e.activation(out=st_[:, 1:2], in_=st_[:, 0:1], func=AF.Sqrt,
                                                                         scale=1.0 / D, bias=EPS), r=[stn], w=[stn]))
                ops.append(lambda: tk.op("dve", lambda e: e.reciprocal(out=st_[:, 2:3], in_=st_[:, 1:2]), r=[stn], w=[stn]))

            def stage_a(j):
                ops = []
                s_ = j % 2
                r_ = j % 3
                u_ = j % 4
                qcols = slice(j * 128, (j + 1) * 128)
                ops.append(lambda: tk.dma("sp", f"xt4{s_}", xt4[s_][:], xs[j * 128:(j + 1) * 128, :], w=[f"xt4{s_}"]))
                ops.append(lambda: tk.dma("sp", f"at4{s_}", atk[s_][:], AT2[j], w=[f"atk{s_}"]))
                ops.append(lambda: tk.grp("pe", [lambda e, k=k: e.transpose(TP4[:, k * 128:(k + 1) * 128],
                                                                             atk[s_][:, k * 128:(k + 1) * 128], ident[:])
                                                 for k in range(4)], r=[f"atk{s_}", "ident"], w=["TP4"]))
                ops.append(lambda: tk.op("act", lambda e: e.copy(out=at4[s_][:].rearrange("p k t -> p (k t)"),
                                                                 in_=TP4[:, 0:512]), r=["TP4"], w=[f"at4{s_}"]))
                bundle2(ops)
                ops.append(lambda: tk.dma("sp", f"yT4{s_}", yT4[s_][:], YT[:, qcols].rearrange("(c p) t -> p c t", p=128),
                                          w=[f"yT4{s_}"]))
                fns = []
                for hf in range(2):
                    for k in range(8):
                        lt = at4[s_][:, k, :] if k < 4 else yT4[s_][:, k - 4, :]
                        fns.append(lambda e, k=k, hf=hf, lt=lt: e.matmul(PA[:, hf * 512:(hf + 1) * 512], lhsT=lt,
                                                                        rhs=wo[:, k, hf * 512:(hf + 1) * 512],
                                                                        start=(k == 0), stop=(k == 7)))
                ops.append(lambda: tk.grp("pe", fns, r=[f"at4{s_}", f"yT4{s_}", "wo"], w=["PA"]))
                ops.append(lambda: tk.op("dve", lambda e: e.tensor_tensor(out=x1[r_][:], in0=PA[:, :], in1=xt4[s_][:], op=ALU.add),
                                         r=["PA", f"xt4{s_}"], w=[f"x1{r_}"]))
                rms4(ops, x1[r_][:], f"x1{r_}", sta, "sta", junka, "junka")
                ops.append(lambda: tk.op("dve", lambda e: e.scalar_tensor_tensor(
                    out=tfa[:], in0=x1[r_][:], scalar=sta[:, 2:3], in1=A1bc[:], op0=ALU.mult, op1=ALU.mult),
                    r=[f"x1{r_}", "sta", "A1bc"], w=["tfa"]))
                ops.append(lambda: tk.op("pool", lambda e: e.tensor_tensor(out=hb4[:], in0=tfa[:], in1=B1bc[:], op=ALU.add),
                                         r=["tfa", "B1bc"], w=["hb4"]))
                ops.append(lambda: tk.grp("pe", [lambda e, k=k: e.transpose(TP4[:, k * 128:(k + 1) * 128],
                                                                             hb4[:, k * 128:(k + 1) * 128], ident[:])
                                                 for k in range(8)], r=["hb4", "ident"], w=["TP4"]))
                ops.append(lambda: tk.op("act", lambda e: e.copy(out=hT4[:].rearrange("p k t -> p (k t)"), in_=TP4[:, :]),
                                         r=["TP4"], w=["hT4"]))
                bundle2(ops)
                for (pt, name, c0) in ((PB, "PB", 0), (PC, "PC", D)):
                    fns2 = []
                    for hf in range(2):
                        for k in range(8):
                            fns2.append(lambda e, k=k, hf=hf, pt=pt, c0=c0: e.matmul(
                                pt[:, hf * 512:(hf + 1) * 512], lhsT=hT4[:, k, :],
                                rhs=wip[:, k, c0 + hf * 512:c0 + (hf + 1) * 512], start=(k == 0), stop=(k == 7)))
                    ops.append(lambda fns2=fns2, name=name: tk.grp("pe", fns2, r=["hT4", "wip"], w=[name]))
                ops.append(lambda: tk.op("dve", lambda e: e.tensor_copy(out=ub[u_][:], in_=PB[:, :]), r=["PB"], w=[f"ub{u_}"]))
                ops.append(lambda: tk.op("act", lambda e: e.activation(out=sg[r_][:], in_=PC[:, :], func=AF.Silu),
                                         r=["PC"], w=[f"sg{r_}"]))
                return ops

            def stage_b(i):
                ops = []
                r_ = i % 3
                o_ = i % 2
                var = 0 if i == 0 else 1
                nbs = [1, 2] if i == 0 else [0, 1, 2]
                for hf in range(2):
                    fns = []
                    for c in range(4 * hf, 4 * hf + 4):
                        g = c // 2
                        for n_i, nb in enumerate(nbs):
                            ubn = ub[(i + nb - 1) % 4]
                            last_ = (n_i == len(nbs) - 1)
                            fns.append(lambda e, c=c, g=g, nb=nb, ubn=ubn, n_i=n_i, hf=hf, last_=last_: e.matmul(
                                PD[:, (c - 4 * hf) * 128:(c - 4 * hf + 1) * 128], lhsT=ubn[:, c * 128:(c + 1) * 128],
                                rhs=poolm[:, var * 12 + g * 3 + nb, :], start=(n_i == 0), stop=(last_ and var == 1)))
                            if var == 0:
                                fns.append(lambda e, c=c, g=g, nb=nb, ubn=ubn, hf=hf, last_=last_: e.matmul(
                                    PD[:, (c - 4 * hf) * 128:(c - 4 * hf + 1) * 128], lhsT=ubn[:, c * 128:(c + 1) * 128],
                                    rhs=poolm[:, 24 + g * 3 + nb, :], start=False, stop=last_))
                    ops.append(lambda fns=fns: tk.grp("pe", fns, r=[f"ub{(i + nb - 1) % 4}" for nb in nbs] + ["poolm"], w=["PD"]))
                    ops.append(lambda hf=hf: tk.op("act", lambda e: e.copy(
                        out=mTb[:, 4 * hf:4 * hf + 4, :].rearrange("p k t -> p (k t)"), in_=PD[:, :]), r=["PD"], w=["mTb"]))
                for hf in range(2):
                    fns = []
                    for g in range(2 * hf, 2 * hf + 2):
                        for kk in range(2):
                            fns.append(lambda e, g=g, kk=kk, hf=hf: e.matmul(
                                PD[:, (g - 2 * hf) * 256:(g - 2 * hf + 1) * 256], lhsT=mTb[:, g * 2 + kk, :],
                                rhs=lin[:, g * 2 + kk, :], start=(kk == 0), stop=(kk == 1)))
                    hs = slice(hf * 512, (hf + 1) * 512)
                    ops.append(lambda fns=fns: tk.grp("pe", fns, r=["mTb", "lin"], w=["PD"]))
                    ops.append(lambda hs=hs: tk.op("dve", lambda e: e.tensor_tensor(out=pb4[:, hs], in0=PD[:, :], in1=sg[r_][:, hs],
                                                                                   op=ALU.mult), r=["PD", f"sg{r_}"], w=["pb4"]))
                ops.append(lambda: tk.grp("pe", [lambda e, k=k: e.transpose(TP4[:, k * 128:(k + 1) * 128],
                                                                             pb4[:, k * 128:(k + 1) * 128], ident[:])
                                                 for k in range(8)], r=["pb4", "ident"], w=["TP4"]))
                ops.append(lambda: tk.op("act", lambda e: e.copy(out=pT4[:].rearrange("p k t -> p (k t)"), in_=TP4[:, :]),
                                         r=["TP4"], w=["pT4"]))
                bundle2(ops)
                for hf in range(2):
                    hs = slice(hf * 512, (hf + 1) * 512)
                    fns = [lambda e, k=k, hs=hs: e.matmul(PD[:, :], lhsT=pT4[:, k, :], rhs=wop[:, k, hs],
                                                          start=(k == 0), stop=(k == 7)) for k in range(8)]
                    ops.append(lambda fns=fns: tk.grp("pe", fns, r=["pT4", "wop"], w=["PD"]))
                    ops.append(lambda hs=hs: tk.op("dve", lambda e: e.tensor_tensor(out=x2[:, hs], in0=PD[:, :], in1=x1[r_][:, hs],
                                                                                   op=ALU.add), r=["PD", f"x1{r_}"], w=["x2"]))
                rms4(ops, x2[:], "x2", stb, "stb", junkb, "junkb")
                ops.append(lambda: tk.op("dve", lambda e: e.scalar_tensor_tensor(
                    out=ot[o_][:], in0=x2[:], scalar=stb[:, 2:3], in1=fnbc[:], op0=ALU.mult, op1=ALU.mult),
                    r=["x2", "stb", "fnbc"], w=[f"ot{o_}"]))
                ops.append(lambda: tk.dma("sp", f"ot{o_}", out_d[i * 128:(i + 1) * 128, :], ot[o_][:], r=[f"ot{o_}"],
                                          w=[("out", i)]))
                return ops

            def interleave(a, b):
                na, nb_ = len(a), len(b)
                ia = ib = 0
                while ia < na or ib < nb_:
                    if ib >= nb_ or (ia < na and ia * nb_ <= ib * na):
                        a[ia](); ia += 1
                    else:
                        b[ib](); ib += 1

            for f_ in stage_a(0):
                f_()
            for f_ in stage_a(1):
                f_()
            for j in range(2, NOWN):
                interleave(stage_a(j), stage_b(j - 2))
            for f_ in stage_b(NOWN - 2):
                f_()
        tk.barrier()
    return nc


POOL_WINDOWS = (2, 4, 8, 16)


def _consts(flip):
    bf = ml_dtypes.bfloat16
    ident = np.eye(128, dtype=np.float32).astype(bf)
    rope = np.zeros((NSLOT, 128, 64), np.float32)
    rope[0:2, :, 0:32] = 1.0
    tl = np.arange(SEQ)
    t = (SEQ - 1 - tl) if flip else tl
    row = (t // 64).astype(np.float32)
    col = (t % 64).astype(np.float32)
    inv = (1.0 / (10000.0 ** (np.arange(0, 16, 2, dtype=np.float32) / 16.0))).astype(np.float32)
    ang = np.concatenate([row[:, None] * inv, col[:, None] * inv], axis=-1).astype(np.float32)
    cos, sin = np.cos(ang), np.sin(ang)
    tab = np.concatenate([cos, cos, -sin, sin], axis=-1).astype(np.float32)
    rope[2:] = tab.reshape(NT, 128, 64)
    k = np.arange(128)
    masks = np.zeros((128, 2, 128), np.float32)
    masks[:, 0, :] = (k[:, None] <= k[None, :])
    masks[:, 1, :] = (k[:, None] >= k[None, :])
    negm = np.zeros((128, 2, 8, 128), np.float32)
    negm[:, 0] = np.where(k[:, None] <= k[None, :], 0.0, NEG)[:, None, :]
    negm[:, 1] = np.where(k[:, None] >= k[None, :], 0.0, NEG)[:, None, :]
    negm = negm.reshape(128, 2, 1024).astype(bf)
    poolm = np.zeros((128, 2, 4, 3, 128), np.float32)
    for var in range(2):
        for g, w in enumerate(POOL_WINDOWS):
            if flip:
                lo_off, hi_off = -(w - w // 2 - 1), w // 2
            else:
                lo_off, hi_off = -(w // 2), (w - w // 2 - 1)
            for tt in range(128):
                lo, hi = tt + lo_off, tt + hi_off
                if var == 0:
                    lo = max(lo, 0)
                cnt = hi - lo + 1
                for sidx in range(lo, hi + 1):
                    nb = 1 + (sidx // 128)
                    poolm[sidx % 128, var, g, nb, tt] += 1.0 / cnt
                poolm[tt, var, g, 1, tt] -= 1.0
    pm_hi = poolm.astype(bf)
    pm_lo = (poolm - pm_hi.astype(np.float32)).astype(bf)
    poolm = np.concatenate([pm_hi.reshape(128, 24, 128), pm_lo[:, 0].reshape(128, 12, 128)], axis=1)
    return dict(ident=ident, rope=rope, masks=masks, negm=negm, poolm=poolm)


def _in_maps(inp):
    f = lambda a: np.ascontiguousarray(np.asarray(a, dtype=np.float32))
    x, c, ctx, c_ctx = f(inp["x"]), f(inp["c"]), f(inp["ctx"]), f(inp["c_ctx"])
    consts = [_consts(False), _consts(True)]
    w_in = f(inp["w_in_mix"])[0]
    w_in_f = w_in.copy()
    w_in_f[:, 2464:2472] = w_in[:, 2472:2480]
    w_in_f[:, 2472:2480] = w_in[:, 2464:2472]
    ssdp = np.stack([f(inp["a_log"])[0].reshape(16), f(inp["dt_bias"])[0].reshape(16),
                     f(inp["d_skip"])[0].reshape(16)])
    ssdp_f = np.concatenate([ssdp[:, 8:16], ssdp[:, 0:8]], axis=1)
    conv_w = f(inp["conv_w"])[0]
    shared = dict(
        mod_w=f(inp["mod_w"]), mod_b=f(inp["mod_b"]), norm_w=f(inp["norm_w"]),
        q_norm=f(inp["q_norm"]).reshape(1, 384), w_uq=f(inp["w_uq"])[0],
        kv_norm=f(inp["kv_norm"]).reshape(1, 256), w_ukv=f(inp["w_ukv"])[0],
        conv_b=f(inp["conv_b"]).reshape(1, 768), ssd_norm=f(inp["ssd_norm"]).reshape(1, 512),
        w_out_mix=f(inp["w_out_mix"])[0], w_in_pool=f(inp["w_in_pool"])[0], pool_lin=f(inp["pool_lin"])[0],
        pool_scale=f(inp["pool_scale"]).reshape(1, D), w_out_pool=f(inp["w_out_pool"])[0],
        final_norm=f(inp["final_norm"]).reshape(1, D))
    maps = []
    for core in range(8):
        b, half = core // 2, core % 2
        flip = half == 1
        m = dict(shared)
        if _SMALL:
            m.pop("mod_w")
        xb = x[b][:SEQ]
        m["xs"] = np.ascontiguousarray(xb[::-1]) if flip else np.ascontiguousarray(xb)
        m["ctx"] = np.ascontiguousarray(ctx[b][::-1]) if flip else ctx[b]
        m["cv"] = np.ascontiguousarray(np.concatenate([c[b].reshape(8, 128).T, c_ctx.reshape(8, 128).T], axis=1))
        m["w_in"] = w_in_f if flip else w_in
        cwl = conv_w[::-1] if flip else conv_w
        m["conv_w"] = np.ascontiguousarray(cwl)
        cb = f(inp["conv_b"]).reshape(768)
        m["cwt"] = np.ascontiguousarray(np.concatenate([cwl, cb[None, :]], 0).reshape(4, 6, 128).transpose(2, 1, 0))
        m["ssdp"] = np.ascontiguousarray(ssdp_f if flip else ssdp)
        m.update(consts[half])
        maps.append(m)
    return maps


_NC = {}


def kernel(**inputs):
    if "nc" not in _NC:
        _NC["nc"] = build(stop=float(os.environ.get("KSTOP", "99")))
    maps = _in_maps(inputs)
    res = run_bass_kernel_spmd(_NC["nc"], maps, core_ids=list(range(8)))
    out = np.zeros((4, SEQ, D), np.float32)
    for core in range(8):
        b, half = core // 2, core % 2
        o = np.asarray(res.results[core]["out"], dtype=np.float32)
        if half == 0:
            out[b, 0:NOUT * 128] = o
        else:
            out[b, NOUT * 128:] = o[::-1]
    return out
```

```python
import numpy as np
import ml_dtypes
from contextlib import ExitStack
import concourse.bass as bass
import concourse.mybir as mybir
from concourse.bass_utils import run_bass_kernel_spmd

F32 = mybir.dt.float32
BF16 = mybir.dt.bfloat16
AF = mybir.ActivationFunctionType
ALU = mybir.AluOpType

import os
_CUT = float(os.environ.get('KCUT', '99'))
_CUT3 = float(os.environ.get('KCUT3', '99'))
_CUT2 = float(os.environ.get('KCUT2', '99'))
_SMALL = os.environ.get("KSMALL") == "1"
D = 1024
NT = 4 if _SMALL else 64
NOWN = 3 if _SMALL else 33
NOUT = NOWN - 1
SEQ = NT * 128
NSLOT = NT + 2
NKEY = NSLOT * 128
NQ = NOWN * 128
EPS = 1e-6
QSCALE = 96 ** -0.5
NEG = -30000.0
VP = 66


class Trk:
    def __init__(self, nc, es):
        self.nc = nc
        self.es = es
        self.engs = {"pe": nc.tensor, "act": nc.scalar, "dve": nc.vector, "pool": nc.gpsimd, "sp": nc.sync}
        self.esem = {}
        self.cnt = {}
        self.seen = {e: {} for e in self.engs}
        self.last_w = {}
        self.readers = {}
        self.dsem = {}
        self.dcnt = {}
        self.nsem = 0
        self.epoch = 0
        self.rec = None
        self.scratch = set()
        self._bundle = None
        self.excl = set()
        self.new_epoch()

    def _newsem(self, name):
        self.nsem += 1
        return self.es.enter_context(self.nc.semaphore(f"{name}_{self.nsem}"))

    def new_epoch(self):
        self.epoch += 1
        if not self.esem:
            for e in self.engs:
                self.esem[e] = self._newsem(f"e{e}")
                self.cnt[e] = 0
        self.keymap = {}
        self.last_w = {}
        self.readers = {}

    def _wait(self, eng, tok):
        kind, key, val = tok
        if kind == "e":
            if key == eng and eng == "pe":
                return
            sem = self.esem[key]
        else:
            sem = self.dsem[key]
        sk = (kind, key)
        if self.seen[eng].get(sk, 0) >= val:
            return
        self.seen[eng][sk] = val
        self.engs[eng].wait_ge(sem, val)

    def _deps(self, eng, r, w):
        deps = []
        for b in r:
            t = self.last_w.get(b)
            if t:
                deps.append(t)
        for b in w:
            t = self.last_w.get(b)
            if t:
                deps.append(t)
            deps.extend(self.readers.get(b, ()))
        for t in deps:
            self._wait(eng, t)

    def _commit(self, tok, r, w):
        for b in w:
            self.last_w[b] = tok
            self.readers[b] = []
        for b in r:
            if b not in w:
                self.readers.setdefault(b, []).append(tok)

    def _x(self, r, w):
        xr = [b for b in r if b in self.excl and b not in w]
        return (r, list(w) + xr) if xr else (r, w)

    def op(self, eng, fn, r=(), w=()):
        if self.rec is not None:
            self._rec_add(lambda: self.op(eng, fn, r, w), r, w)
            return
        r, w = self._x(r, w)
        self._deps(eng, r, w)
        inst = fn(self.engs[eng])
        self.cnt[eng] += 1
        inst.then_inc(self.esem[eng], 1)
        self._commit(("e", eng, self.cnt[eng]), r, w)

    def grp(self, eng, fns, r=(), w=()):
        if self.rec is not None:
            self._rec_add(lambda: self.grp(eng, fns, r, w), r, w)
            return
        r, w = self._x(r, w)
        self._deps(eng, r, w)
        inst = None
        for fn in fns:
            inst = fn(self.engs[eng])
        self.cnt[eng] += 1
        inst.then_inc(self.esem[eng], 1)
        self._commit(("e", eng, self.cnt[eng]), r, w)

    def _rec_add(self, clo, r, w):
        hit_w = any(b in self.scratch for b in w) and not any(b in self.scratch for b in r)
        hit_r = any(b in self.scratch for b in r)
        if self._bundle is not None:
            self._bundle.append(clo)
            if hit_r:
                inner, self._bundle = self._bundle, None
                self.rec.append(lambda: [g() for g in inner])
            return
        if hit_w:
            self._bundle = [clo]
            return
        self.rec.append(clo)

    def atomic(self, f):
        if self.rec is None:
            f()
            return
        outer, self.rec = self.rec, []
        f()
        inner, self.rec = self.rec, outer
        self.rec.append(lambda: [g() for g in inner])

    def record(self, f, *a):
        assert self.rec is None
        self.rec = []
        f(*a)
        assert self._bundle is None
        ops, self.rec = self.rec, None
        return ops

    @staticmethod
    def interleave(a, b):
        na, nb_ = len(a), len(b)
        ia = ib = 0
        while ia < na or ib < nb_:
            if ib >= nb_ or (ia < na and ia * nb_ <= ib * na):
                a[ia](); ia += 1
            else:
                b[ib](); ib += 1

    def dma(self, q, key, out, in_, r=(), w=(), **kw):
        if self.rec is not None:
            self._rec_add(lambda: self.dma(q, key, out, in_, r, w, **kw), r, w)
            return
        km = self.keymap.setdefault(q, {})
        key = (q, km.setdefault(key, len(km)))
        if key not in self.dsem:
            self.dsem[key] = self._newsem(f"d{key[0]}{key[1]}")
            self.dcnt[key] = 0
        self._deps(q, r, w)
        self.engs[q].dma_start(out=out, in_=in_, **kw).then_inc(self.dsem[key], 16)
        self.dcnt[key] += 16
        self._commit(("d", key, self.dcnt[key]), r, w)

    def barrier(self):
        for e in self.engs:
            for e2 in self.engs:
                if e2 != e and self.cnt[e2] > 0:
                    self._wait(e, ("e", e2, self.cnt[e2]))
            for key, v in self.dcnt.items():
                self._wait(e, ("d", key, v))
        self.new_epoch()


def build(dbg=False, stop=99):
    nc = bass.Bass("TRN2", target_bir_lowering=False)

    def din(name, shape, dt=F32):
        return nc.dram_tensor(name, list(shape), dt, kind="ExternalInput").ap()

    def dscr(name, shape, dt=BF16):
        kind = "ExternalOutput" if (dbg and name in DBG) else "Internal"
        return nc.dram_tensor(name, list(shape), dt, kind=kind).ap()

    DBG = {"QT", "KT", "KPE", "VS", "ZS", "XBC", "MODS", "DTS", "AT2", "YT", "X1"}

    xs = din("xs", [SEQ, D])
    ctx = din("ctx", [256, D])
    cv = din("cv", [128, 16])
    mod_w = None if _SMALL else din("mod_w", [2, D, 3 * D])
    mod_b = din("mod_b", [2, 3 * D])
    norm_w = din("norm_w", [2, D])
    w_in = din("w_in", [D, 2480])
    q_norm = din("q_norm", [1, 384])
    w_uq = din("w_uq", [384, 768])
    kv_norm = din("kv_norm", [1, 256])
    w_ukv = din("w_ukv", [256, 1024])
    conv_w = din("conv_w", [3, 768])
    conv_b = din("conv_b", [1, 768])
    ssdp = din("ssdp", [3, 16])
    ssd_norm = din("ssd_norm", [1, 512])
    w_out_mix = din("w_out_mix", [D, D])
    w_in_pool = din("w_in_pool", [D, 2 * D])
    pool_lin = din("pool_lin", [4, 256, 256])
    pool_scale = din("pool_scale", [1, D])
    w_out_pool = din("w_out_pool", [D, D])
    final_norm = din("final_norm", [1, D])
    ident_d = din("ident", [128, 128], BF16)
    rope_d = din("rope", [NSLOT, 128, 64])
    masks_d = din("masks", [128, 2, 128])
    negm_d = din("negm", [128, 2, 8 * 128], BF16)
    poolm_d = din("poolm", [128, 36, 128], BF16)
    cwt_d = din("cwt", [128, 6, 4])
    out_d = nc.dram_tensor("out", [NOUT * 128, D], F32, kind="ExternalOutput").ap()

    MODS = din("MODS", [3, 3 * D]) if _SMALL else dscr("MODS", [3, 3 * D], F32)
    QT = dscr("QT", [768, NQ])
    KT = dscr("KT", [512, NKEY])
    KPE = dscr("KPE", [32, NKEY])
    GA = dscr("GA", [512, NQ])
    ZS = dscr("ZS", [NOWN, 128, 512])
    XBC = dscr("XBC", [768, NKEY])
    AT = dscr("AT", [8, 64, NQ])
    YT = dscr("YT", [512, NQ])

    with ExitStack() as es:
        tk = Trk(nc, es)

        def sb(st, name, shape, dt=F32):
            return st.enter_context(nc.sbuf_tensor("s_" + name, list(shape), dt))

        def ps(st, name, shape, dt=F32):
            tk.excl.add(name)
            return st.enter_context(nc.psum_tensor("p_" + name, list(shape), dt))

        ident = sb(es, "ident", [128, 128], BF16)
        DT = sb(es, "DT", [128, NSLOT, 16])
        tk.dma("sp", "c0", ident[:], ident_d[:, :], w=["ident"])
        p13 = es.enter_context(ExitStack())
        Vres = sb(p13, "Vres", [128, NSLOT, 8, VP], BF16)
        GAres = sb(p13, "GAres", [128, NOWN, 512], BF16)

        with ExitStack() as p0:
          if not _SMALL:
              cvt = sb(p0, "cvt", [128, 16])
              scb = sb(p0, "scb", [128, 16], BF16)
              mw = sb(p0, "mw", [128, 8, 3 * D], BF16)
              mrow = sb(p0, "mrow", [1, 3 * D])
              mbrow = sb(p0, "mbrow", [1, 3 * D])
              pm = ps(p0, "pm", [128, 512])
              tk.dma("sp", "c1", cvt[:], cv[:, :], w=["cvt"])
              tk.op("act", lambda e: e.activation(out=scb[:], in_=cvt[:], func=AF.Silu), r=["cvt"], w=["scb"])
              vi = 0
              for layer in range(2):
                  tk.dma("pool", "mw", mw[:], mod_w[layer].rearrange("(k p) n -> p k n", p=128), w=["mw"])
                  for var in ([0, 1] if layer == 0 else [0]):
                      tk.dma("sp", "c2", mbrow[:], mod_b[layer:layer + 1, :], w=["mbrow"])
                      for jb in range(6):
                          fns = []
                          for k in range(8):
                              fns.append(lambda e, k=k, jb=jb, var=var: e.matmul(
                                  pm[0:1, :], lhsT=scb[:, var * 8 + k:var * 8 + k + 1],
                                  rhs=mw[:, k, jb * 512:(jb + 1) * 512], start=(k == 0), stop=(k == 7)))
                          tk.grp("pe", fns, r=["scb", "mw"], w=["pm"])
                          tk.op("dve", lambda e, jb=jb: e.tensor_tensor(
                              out=mrow[:, jb * 512:(jb + 1) * 512], in0=pm[0:1, :],
                              in1=mbrow[:, jb * 512:(jb + 1) * 512], op=ALU.add),
                              r=["pm", "mbrow"], w=["mrow"])
                      tk.dma("sp", "c3", MODS[vi:vi + 1, :], mrow[:], r=["mrow"], w=[("MODS", vi)])
                      vi += 1
        tk.barrier()
        if stop < 1:
            return nc

        with ExitStack() as p1:
            win = sb(p1, "win", [128, 8, 2480], BF16)
            wuq = sb(p1, "wuq", [128, 3, 768], BF16)
            wukv = sb(p1, "wukv", [128, 2, 1024], BF16)
            tk.dma("pool", "w1", win[:], w_in.rearrange("(k p) n -> p k n", p=128), w=["win"])
            tk.dma("pool", "w2", wuq[:], w_uq.rearrange("(k p) n -> p k n", p=128), w=["wuq"])
            tk.dma("pool", "w3", wukv[:], w_ukv.rearrange("(k p) n -> p k n", p=128), w=["wukv"])
            tf = sb(p1, "tf", [128, D])
            nwbc = tf
            Abc = [sb(p1, f"Abc{v}", [128, D]) for v in range(2)]
            Bbc = [sb(p1, f"Bbc{v}", [128, D]) for v in range(2)]
            qnbc = sb(p1, "qnbc", [128, 384])
            kvnbc = sb(p1, "kvnbc", [128, 256])
            dtbbc = sb(p1, "dtbbc", [128, 16])
            tk.dma("sp", "c4", nwbc[:], norm_w[0:1, :].partition_broadcast(128), w=["tf"])
            tk.dma("sp", "c5", qnbc[:], q_norm[0:1, :].partition_broadcast(128), w=["qnbc"])
            tk.dma("sp", "c6", kvnbc[:], kv_norm[0:1, :].partition_broadcast(128), w=["kvnbc"])
            tk.dma("sp", "c7", dtbbc[:], ssdp[1:2, :].partition_broadcast(128), w=["dtbbc"])
            for v in range(2):
                tk.dma("sp", f"c8{v}", Bbc[v][:], MODS[v:v + 1, 0:D].partition_broadcast(128),
                       r=[("MODS", v)], w=[f"Bbc{v}"])
                tk.dma("sp", f"c9{v}", Abc[v][:], MODS[v:v + 1, D:2 * D].partition_broadcast(128),
                       r=[("MODS", v)], w=[f"Abc{v}"])
                tk.op("dve", lambda e, v=v: e.scalar_tensor_tensor(
                    out=Abc[v][:], in0=Abc[v][:], scalar=1.0, in1=nwbc[:], op0=ALU.add, op1=ALU.mult),
                    r=[f"Abc{v}", "tf"], w=[f"Abc{v}"])

            NB = 2
            xt = [sb(p1, f"xt{i}", [128, D]) for i in range(NB)]
            ropet = [sb(p1, f"ropet{i}", [128, 64]) for i in range(3)]
            junk = sb(p1, "junk", [128, D], BF16)
            stat = [sb(p1, f"stat{i}", [128, 12]) for i in range(NB)]
            hb = sb(p1, "hb", [128, D], BF16)
            hT = [sb(p1, f"hT{i}", [128, 8, 128], BF16) for i in range(NB)]
            kvn = sb(p1, "kvn", [128, 256], BF16)
            kvnT = sb(p1, "kvnT", [128, 2, 128], BF16)
            knb = sb(p1, "knb", [128, 512], BF16)
            knT_1 = sb(p1, "knT0", [128, 4, 128], BF16)
            knT = [knT_1, knT_1]
            kf = sb(p1, "kf", [128, 32])
            kr = sb(p1, "kr", [128, 64])
            kpb = sb(p1, "kpb", [128, 32], BF16)
            kpT = [sb(p1, f"kpT{i}", [32, 128], BF16) for i in range(NB)]
            dtt = sb(p1, "dtt", [128, 16])
            qan = sb(p1, "qan", [128, 384], BF16)
            qanT = sb(p1, "qanT", [128, 3, 128], BF16)
            qb = sb(p1, "qb", [128, 8, 96], BF16)
            qf = sb(p1, "qf", [128, 8, 32])
            qr = sb(p1, "qr", [128, 8, 64])
            qT_1 = sb(p1, "qT0", [128, 6, 128], BF16)
            qT = [qT_1, qT_1]
            zs_1 = sb(p1, "zs0", [128, 512], BF16)
            zs = [zs_1, zs_1]
            xbT = [sb(p1, f"xbT{i}", [128, 6, 128], BF16) for i in range(NB)]
            TP = ps(p1, "TP", [128, 1024], BF16)
            TPF = TP
            MB = ps(p1, "MB", [128, 512])
            MA = ps(p1, "MA", [128, 512])
            MZ = ps(p1, "MZ", [128, 512])
            W0 = ps(p1, "W0", [128, 1024])
            F0 = ps(p1, "F0", [128, 512])
            F1 = ps(p1, "F1", [128, 512])
            tk.op("pool", lambda e: e.memset(Vres[:].rearrange("p j h c -> p (j h c)"), 1.0), w=["Vres"])

            C_QA, C_KVA, C_KPE, C_GA, C_Z, C_XBC, C_DT = 0, 384, 640, 672, 1184, 1696, 2464

            def rms_stats(src_ap, n, st_ap, rb, wb, name):
                tk.op("act", lambda e: e.activation(out=junk[:, 0:n], in_=src_ap, func=AF.Square,
                                                    accum_out=st_ap[:, 0:1]), r=rb, w=["junk", wb])
                tk.op("act", lambda e: e.activation(out=st_ap[:, 1:2], in_=st_ap[:, 0:1], func=AF.Sqrt,
                                                    scale=1.0 / n, bias=EPS), r=[wb], w=[wb])
                tk.op("dve", lambda e: e.reciprocal(out=st_ap[:, 2:3], in_=st_ap[:, 1:2]), r=[wb], w=[wb])

            def rope_ops(src, dst, tmp, rt, nh, rb, wb, tb):
                c2 = rt[:, 0:32].unsqueeze(1).to_broadcast([128, nh, 32])
                s2a = rt[:, 32:48].unsqueeze(1).to_broadcast([128, nh, 16])
                s2b = rt[:, 48:64].unsqueeze(1).to_broadcast([128, nh, 16])
                tk.op("dve", lambda e: e.tensor_tensor(out=tmp[:, :, 0:32], in0=src, in1=c2, op=ALU.mult),
                      r=rb, w=[tb])
                tk.op("dve", lambda e: e.tensor_tensor(out=tmp[:, :, 32:48], in0=src[:, :, 16:32], in1=s2a,
                                                       op=ALU.mult), r=rb, w=[tb])
                tk.op("dve", lambda e: e.tensor_tensor(out=tmp[:, :, 48:64], in0=src[:, :, 0:16], in1=s2b,
                                                       op=ALU.mult), r=rb, w=[tb])
                tk.op("dve", lambda e: e.tensor_tensor(out=dst, in0=tmp[:, :, 0:32], in1=tmp[:, :, 32:64],
                                                       op=ALU.add), r=[tb], w=wb)

            def kind_of(j):
                return "ctx" if j < 2 else ("own" if j - 2 < NOWN else "oth")

            def front(j):
                s = j % NB
                kind = kind_of(j)
                v = 1 if kind == "ctx" else 0
                src = ctx[j * 128:(j + 1) * 128, :] if kind == "ctx" else xs[(j - 2) * 128:(j - 1) * 128, :]
                tk.dma("sp", f"xt{s}", xt[s][:], src, w=[f"xt{s}"])
                tk.dma("sp", f"rp{j % 3}", ropet[j % 3][:], rope_d[j], w=[f"ropet{j % 3}"])
                st = stat[s]
                rms_stats(xt[s][:], D, st, [f"xt{s}"], f"stat{s}a", "x")
                tk.op("dve", lambda e: e.scalar_tensor_tensor(
                    out=tf[:], in0=xt[s][:], scalar=st[:, 2:3], in1=Abc[v][:], op0=ALU.mult, op1=ALU.mult),
                    r=[f"xt{s}", f"stat{s}a", f"Abc{v}"], w=["tf"])
                tk.op("pool", lambda e: e.tensor_tensor(out=hb[:], in0=tf[:], in1=Bbc[v][:], op=ALU.add),
                      r=["tf", f"Bbc{v}"], w=["hb"])
                def _tr():
                    tk.grp("pe", [lambda e, k=k: e.transpose(TPF[:, k * 128:(k + 1) * 128], hb[:, k * 128:(k + 1) * 128],
                                                              ident[:]) for k in range(8)],
                           r=["hb", "ident"], w=["TP"])
                    tk.op("act", lambda e: e.copy(out=hT[s][:].rearrange("p k t -> p (k t)"), in_=TPF[:, :]),
                          r=["TP"], w=[f"hT{s}"])
                tk.atomic(_tr)

            def mm_all(j, part):
                s = j % NB
                kind = kind_of(j)
                hTs = hT[s]

                def tok_major(pt, c0, width, col0, name):
                    tk.grp("pe", [lambda e, k=k: e.matmul(pt[:, c0:c0 + width], lhsT=hTs[:, k, :],
                                                          rhs=win[:, k, col0:col0 + width],
                                                          start=(k == 0), stop=(k == 7)) for k in range(8)],
                           r=[f"hT{s}", "win"], w=[name])

                def feat_major(pt, c0, col0, name):
                    tk.grp("pe", [lambda e, k=k: e.matmul(pt[:, c0:c0 + 128], lhsT=win[:, k, col0:col0 + 128],
                                                          rhs=hTs[:, k, :],
                                                          start=(k == 0), stop=(k == 7)) for k in range(8)],
                           r=[f"hT{s}", "win"], w=[name])

                if part == 0:
                    tok_major(MB, 0, 288, C_KVA, "MB")
                    tok_major(MB, 288, 16, C_DT, "MB")
                    feat_major(MB, 304, C_XBC + 512, "MB")
                    return
                if part == 1:
                    if kind == "own":
                        tok_major(MA, 0, 384, C_QA, "MA")
                        feat_major(MA, 384, C_XBC + 640, "MA")
                    for c in range(2):
                        feat_major(F1, c * 128, C_XBC + c * 128, "F1")
                    return
                for c in range(2, 4):
                    feat_major(F1, c * 128, C_XBC + c * 128, "F1")
                if kind == "own":
                    tok_major(MZ, 0, 512, C_Z, "MZ")
                    tok_major(F0, 0, 512, C_GA, "F0")

            def chains_a1(j):
                s = j % NB
                st = stat[s]
                tk.op("dve", lambda e: e.tensor_copy(out=xbT[s][:, 4, :], in_=MB[:, 304:432]),
                      r=["MB"], w=[f"xbT{s}"])
                rms_stats(MB[:, 0:256], 256, st[:, 3:6], ["MB"], f"stat{s}b", "kv")
                tk.op("dve", lambda e: e.scalar_tensor_tensor(
                    out=kvn[:], in0=MB[:, 0:256], scalar=st[:, 5:6], in1=kvnbc[:], op0=ALU.mult, op1=ALU.mult),
                    r=["MB", f"stat{s}b", "kvnbc"], w=["kvn"])
                tk.op("act", lambda e: e.copy(out=kf[:], in_=MB[:, 256:288]), r=["MB"], w=["kf"])
                tk.op("dve", lambda e: e.tensor_tensor(out=dtt[:], in0=MB[:, 288:304], in1=dtbbc[:], op=ALU.add),
                      r=["MB", "dtbbc"], w=["dtt"])

            def chains_a2(j):
                s = j % NB
                kind = kind_of(j)
                st = stat[s]
                jq = j - 2
                nxc = 6 if kind == "own" else 5
                tk.op("act", lambda e: e.copy(out=xbT[s][:, 0:4, :].rearrange("p k t -> p (k t)"), in_=F1[:, :]),
                      r=["F1"], w=[f"xbT{s}"])
                if kind == "own":
                    tk.op("dve", lambda e: e.tensor_copy(out=xbT[s][:, 5, :], in_=MA[:, 384:512]),
                          r=["MA"], w=[f"xbT{s}"])
                tk.dma("sp", f"xb{s}", XBC[0:nxc * 128, j * 128:(j + 1) * 128].rearrange("(c p) t -> p c t", p=128),
                       xbT[s][:, 0:nxc, :], r=[f"xbT{s}"], w=[("XBC", j)])
                if kind == "own":
                    rms_stats(MA[:, 0:384], 384, st[:, 6:9], ["MA"], f"stat{s}c", "q")
                    tk.op("dve", lambda e: e.scalar_tensor_tensor(
                        out=qan[:], in0=MA[:, 0:384], scalar=st[:, 8:9], in1=qnbc[:], op0=ALU.mult, op1=ALU.mult),
                        r=["MA", f"stat{s}c", "qnbc"], w=["qan"])
                    tk.op("act", lambda e: e.activation(out=zs[s][:], in_=MZ[:, :], func=AF.Silu), r=["MZ"], w=["zs0"])
                    tk.dma("sp", "zs0", ZS[jq], zs[s][:], r=["zs0"], w=[("ZS", jq)])
                    tk.op("act", lambda e: e.activation(out=GAres[:, jq, :], in_=F0[:, :], func=AF.Silu),
                          r=["F0"], w=[("GAres", jq)])

            def chains_b1(j):
                s = j % NB
                kind = kind_of(j)
                st = stat[s]
                jq = j - 2
                tk.grp("pe", [lambda e, k=k: e.transpose(TP[:, k * 128:(k + 1) * 128], kvn[:, k * 128:(k + 1) * 128],
                                                          ident[:]) for k in range(2)],
                       r=["kvn", "ident"], w=["TP"])
                tk.op("dve", lambda e: e.tensor_copy(out=kvnT[:].rearrange("p k t -> p (k t)"), in_=TP[:, 0:256]),
                      r=["TP"], w=["kvnT"])
                fns = []
                for half in range(2):
                    for k in range(2):
                        fns.append(lambda e, k=k, half=half: e.matmul(
                            W0[:, half * 512:(half + 1) * 512], lhsT=kvnT[:, k, :],
                            rhs=wukv[:, k, half * 512:(half + 1) * 512], start=(k == 0), stop=(k == 1)))
                tk.grp("pe", fns, r=["kvnT", "wukv"], w=["W0"])
                kv3 = W0[:, :].rearrange("p (h c) -> p h c", h=8)
                tk.op("act", lambda e: e.copy(out=Vres[:, j, :, 0:64], in_=kv3[:, :, 64:128]),
                      r=["W0", "Vres"], w=[("V", j)])
                tk.op("dve", lambda e: e.tensor_copy(out=knb[:].rearrange("p (h c) -> p h c", h=8),
                                                     in_=kv3[:, :, 0:64]), r=["W0"], w=["knb"])

            def chains_b2(j):
                s = j % NB
                kind = kind_of(j)
                st = stat[s]
                jq = j - 2
                if kind == "own":
                    tk.grp("pe", [lambda e, k=k: e.transpose(TP[:, k * 128:(k + 1) * 128],
                                                              qan[:, k * 128:(k + 1) * 128], ident[:]) for k in range(3)],
                           r=["qan", "ident"], w=["TP"])
                    tk.op("dve", lambda e: e.tensor_copy(out=qanT[:].rearrange("p k t -> p (k t)"), in_=TP[:, 0:384]),
                          r=["TP"], w=["qanT"])
                    fns = []
                    for (c0, wd) in ((0, 512), (512, 256)):
                        for k in range(3):
                            fns.append(lambda e, k=k, c0=c0, wd=wd: e.matmul(
                                W0[:, c0:c0 + wd], lhsT=qanT[:, k, :], rhs=wuq[:, k, c0:c0 + wd],
                                start=(k == 0), stop=(k == 2)))
                    tk.grp("pe", fns, r=["qanT", "wuq"], w=["W0"])
                tk.grp("pe", [lambda e, k=k: e.transpose(TP[:, k * 128:(k + 1) * 128], knb[:, k * 128:(k + 1) * 128],
                                                          ident[:]) for k in range(4)],
                       r=["knb", "ident"], w=["TP"])
                tk.op("act", lambda e: e.copy(out=knT[s][:].rearrange("p k t -> p (k t)"), in_=TP[:, 0:512]),
                      r=["TP"], w=["knT0"])
                tk.dma("sp", f"kt{s}", KT[:, j * 128:(j + 1) * 128].rearrange("(c p) t -> p c t", p=128),
                       knT[s][:], r=["knT0"], w=[("KT", j)])
                rope_ops(kf[:].unsqueeze(1), kpb[:].unsqueeze(1), kr[:].unsqueeze(1), ropet[j % 3], 1,
                         ["kf", f"ropet{j % 3}"], ["kpb"], "kr")
                tk.op("pe", lambda e: e.transpose(TP[0:32, 0:128], kpb[:, :], ident[:]),
                      r=["kpb", "ident"], w=["TP"])
                tk.op("dve", lambda e: e.tensor_copy(out=kpT[s][:], in_=TP[0:32, 0:128]), r=["TP"], w=[f"kpT{s}"])
                tk.dma("sp", f"kp{s}", KPE[:, j * 128:(j + 1) * 128], kpT[s][:], r=[f"kpT{s}"], w=[("KPE", j)])
                tk.op("act", lambda e: e.activation(out=dtt[:], in_=dtt[:], func=AF.Exp), r=["dtt"], w=["dtt"])
                tk.op("act", lambda e: e.activation(out=DT[:, j, :], in_=dtt[:], func=AF.Ln, bias=1.0),
                      r=["dtt"], w=[("DT", j)])
                if kind != "own":
                    return
                q3 = W0[:, 0:768].rearrange("p (h c) -> p h c", h=8)
                tk.op("act", lambda e: e.activation(out=qb[:, :, 0:64], in_=q3[:, :, 0:64], func=AF.Copy,
                                                    scale=QSCALE), r=["W0"], w=["qb"])
                tk.op("act", lambda e: e.activation(out=qf[:], in_=q3[:, :, 64:96], func=AF.Copy, scale=QSCALE),
                      r=["W0"], w=["qf"])
                rope_ops(qf[:], qb[:, :, 64:96], qr[:], ropet[j % 3], 8, ["qf", f"ropet{j % 3}"], ["qb"], "qr")
                qb2 = qb[:].rearrange("p h c -> p (h c)")
                tk.grp("pe", [lambda e, k=k: e.transpose(TP[:, k * 128:(k + 1) * 128], qb2[:, k * 128:(k + 1) * 128],
                                                          ident[:]) for k in range(6)],
                       r=["qb", "ident"], w=["TP"])
                tk.op("act", lambda e: e.copy(out=qT[s][:].rearrange("p k t -> p (k t)"), in_=TP[:, 0:768]),
                      r=["TP"], w=["qT0"])
                tk.dma("sp", f"qt{s}", QT[:, jq * 128:(jq + 1) * 128].rearrange("(c p) t -> p c t", p=128),
                       qT[s][:], r=["qT0"], w=[("QT", jq)])

            def mm_front(j):
                if j < NSLOT:
                    mm_all(j)
                if j + 1 < NSLOT:
                    front(j + 1)

            def mm_front(j):
                if j < NSLOT:
                    mm_all(j)
                if j + 1 < NSLOT:
                    front(j + 1)

            front(0)
            for j in range(NSLOT):
                mm_all(j, 0)
                chains_a1(j)
                mm_all(j, 1)
                chains_b1(j)
                mm_all(j, 2)
                if j + 1 < NSLOT:
                    front(j + 1)
                chains_a2(j)
                chains_b2(j)
            if dbg:
                VSd = dscr("VS", [128, NSLOT * 8 * VP])
                tk.dma("sp", "vsd", VSd[:, :], Vres[:].rearrange("p j h c -> p (j h c)"),
                       r=[("V", j) for j in range(NSLOT)], w=["VSd"])
                DTS = dscr("DTS", [128, NSLOT * 16], F32)
                tk.dma("sp", "dts", DTS[:, :], DT[:].rearrange("p j c -> p (j c)"),
                       r=[("DT", j) for j in range(NSLOT)], w=["DTS"])
        tk.barrier()
        if stop < 2:
            return nc

        AT2 = dscr("AT2", [NOWN, 128, 512])
        with ExitStack() as p3:
            KH = [sb(p3, f"KH{i}", [96, NKEY], BF16) for i in range(2)]
            QH = [sb(p3, f"QH{i}", [96, NQ], BF16) for i in range(2)]
            ATres = sb(p3, "ATres", [128, NOWN, 512], BF16)
            NPT = 3
            PT = [sb(p3, f"PT{i}", [128, 2, 512], BF16) for i in range(3)]
            zb = sb(p3, "zb", [128, 512], BF16)
            rcp = sb(p3, "rcp", [128, 4])
            t1 = sb(p3, "t1", [128, 4, 64])
            ST = [ps(p3, f"ST{i}", [128, 1024]) for i in range(NPT)]
            OT = [ps(p3, f"OT{i}", [128, 512]) for i in range(2)]
            tk.op("pool", lambda e: e.memset(zb[:], 0.0), w=["zb"])
            for i in range(2):
                tk.dma("sp", f"khr{i}", KH[i][64:96, :], KPE[:, :], w=[f"KHr{i}"])
            blocks = [(b * 4, 4) for b in range(NOWN // 4)]
            if NOWN % 4:
                blocks.append(((NOWN // 4) * 4, NOWN % 4))

            def load_head(h):
                i = h % 2
                tk.dma("sp", f"kh{i}", KH[i][0:64, :], KT[h * 64:(h + 1) * 64, :], w=[f"KH{i}"])
                tk.dma("sp", f"qh{i}", QH[i][:, :], QT[h * 96:(h + 1) * 96, :], w=[f"QH{i}"])

            load_head(0)
            bi = 0
            for h in range(8):
                i = h % 2
                if h + 1 < 8:
                    load_head(h + 1)
                for (t0, nt) in blocks:
                    q0, qw = t0 * 128, nt * 128
                    ob = bi % 2
                    bi += 1
                    tk.op("pe", lambda e: e.matmul(OT[ob][:, :], lhsT=zb[:, 0:128], rhs=zb[:, :], start=True, stop=True),
                          r=["zb"], w=[f"OT{ob}"])
                    NPAIR = NSLOT // 2

                    def S(pk):
                        b_ = pk % NPT
                        tk.grp("pe", [lambda e, u=u: e.matmul(ST[b_][:, u * 512:u * 512 + qw],
                                                              lhsT=KH[i][0:96, (2 * pk + u) * 128:(2 * pk + u + 1) * 128],
                                                              rhs=QH[i][0:96, q0:q0 + qw], start=True, stop=True)
                                      for u in range(2)],
                               r=[f"KH{i}", f"KHr{i}", f"QH{i}"], w=[f"ST{b_}"])

                    S(0)
                    S(1)
                    for pk in range(NPAIR):
                        b_ = pk % NPT
                        pb_ = pk % 3
                        if pk + 2 < NPAIR:
                            S(pk + 2)
                        tk.op("act", lambda e: e.activation(
                            out=PT[pb_][:, :, 0:qw], in_=ST[b_][:, :].rearrange("p (u c) -> p u c", u=2)[:, :, 0:qw],
                            func=AF.Exp), r=[f"ST{b_}"], w=[f"PT{pb_}"])
                        tk.grp("pe", [lambda e, qt=qt, u=u: e.matmul(OT[ob][:, qt * 128:qt * 128 + 65],
                                                                      lhsT=PT[pb_][:, u, qt * 128:(qt + 1) * 128],
                                                                      rhs=Vres[:, 2 * pk + u, h, 0:65], start=False,
                                                                      stop=True, skip_group_check=True)
                                      for u in range(2) for qt in range(nt)], r=[f"PT{pb_}"], w=[f"OT{ob}"])
                    O3 = OT[ob][:, :].rearrange("p (t c) -> p t c", t=4)
                    tk.op("dve", lambda e: e.reciprocal(out=rcp[:, 0:nt], in_=O3[:, 0:nt, 64]), r=[f"OT{ob}"], w=["rcp"])
                    tk.op("dve", lambda e: e.tensor_tensor(out=t1[:, 0:nt, :], in0=O3[:, 0:nt, 0:64],
                                                           in1=rcp[:, 0:nt].unsqueeze(2).to_broadcast([128, nt, 64]),
                                                           op=ALU.mult), r=[f"OT{ob}", "rcp"], w=["t1"])
                    tk.op("pool", lambda e: e.tensor_tensor(out=ATres[:, t0:t0 + nt, h * 64:(h + 1) * 64], in0=t1[:, 0:nt, :],
                                                            in1=GAres[:, t0:t0 + nt, h * 64:(h + 1) * 64], op=ALU.mult),
                          r=["t1"], w=["ATres"])
            tk.dma("sp", "at2", AT2.rearrange("j p c -> p j c"), ATres[:], r=["ATres"], w=["AT2"])
        tk.barrier()
        p13.close()
        if stop < 3:
            return nc

        with ExitStack() as p2:
            U5 = sb(p2, "U5", [128, 5, NKEY], BF16)
            UC = sb(p2, "UC", [128, NQ], BF16)
            cw = sb(p2, "cw", [128, 6, 4])
            masks = sb(p2, "masks", [128, 2, 128])
            negm = sb(p2, "negm", [128, 2, 1024], BF16)
            onesf = sb(p2, "onesf2", [128, 128])
            Abc16 = sb(p2, "Abc16", [128, 16])
            skip16 = sb(p2, "skip16", [128, 16])
            skip8 = sb(p2, "skip8", [128, 8])
            ssdnbc = sb(p2, "ssdnbc", [128, 512])
            Hs = [sb(p2, f"Hs{d}", [128, 256]) for d in range(2)]
            Hb = [sb(p2, f"Hb{d}", [128, 512], BF16) for d in range(2)]
            Cblk = sb(p2, "Cblk", [128, 256], BF16)
            Yb = sb(p2, "Yb", [128, NOWN, 512], BF16)
            a8 = sb(p2, "a8", [128, 8])
            acs16 = sb(p2, "acs16", [128, 16])
            ahl = sb(p2, "ahl", [128, 16], BF16)
            masksb = sb(p2, "masksb", [128, 2, 128], BF16)
            w8 = sb(p2, "w8", [128, 8])
            acs8s = sb(p2, "acs8s", [128, 8])
            diff = sb(p2, "diff", [128, 1024])
            Ee = sb(p2, "Ee", [128, 1024])
            xd = sb(p2, "xd", [128, 512], BF16)
            xdw = sb(p2, "xdw", [128, 512], BF16)
            Bm = sb(p2, "Bm", [128, 128], BF16)
            MT = sb(p2, "MT", [128, 1024], BF16)
            ty = sb(p2, "ty", [128, 512])
            ysum = sb(p2, "ysum", [128, 512])
            zt = [sb(p2, f"zt{i}", [128, 512], BF16) for i in range(2)]
            st2 = sb(p2, "st2", [128, 4])
            junk2 = sb(p2, "junk2", [128, 512], BF16)
            yn = sb(p2, "yn", [128, 512], BF16)
            ynT = [sb(p2, f"ynT{i}", [128, 4, 128], BF16) for i in range(2)]
            TPs = ps(p2, "TPs", [128, 1024], BF16)
            ACS = ps(p2, "ACS", [128, 1024])
            G8 = ps(p2, "G8", [128, 512])
            Yp = ps(p2, "Yp", [128, 512])
            YO = ps(p2, "YO", [128, 512])
            SP = ps(p2, "SP", [128, 512])

            tk.dma("sp", "k0", cw[:], cwt_d[:, :, :], w=["cw"])
            tk.dma("sp", "k1", masks[:], masks_d[:, :, :], w=["masks"])
            tk.dma("sp", "k2", negm[:], negm_d[:, :, :], w=["negm"])
            tk.dma("sp", "k3", Abc16[:], ssdp[0:1, :].partition_broadcast(128), w=["Abc16"])
            tk.dma("sp", "k4", skip16[:], ssdp[2:3, :].partition_broadcast(128), w=["skip16"])
            tk.dma("sp", "k5", ssdnbc[:], ssd_norm[0:1, :].partition_broadcast(128), w=["ssdnbc"])
            tk.op("pool", lambda e: e.memset(onesf[:], 1.0), w=["onesf"])
            tk.op("dve", lambda e: e.tensor_copy(out=masksb[:], in_=masks[:]), r=["masks"], w=["masksb"])
            tk.op("act", lambda e: e.activation(out=Abc16[:], in_=Abc16[:], func=AF.Exp), r=["Abc16"], w=["Abc16"])
            tk.op("dve", lambda e: e.tensor_scalar(out=Abc16[:], in0=Abc16[:], scalar1=-1.0, scalar2=None,
                                                   op0=ALU.mult), r=["Abc16"], w=["Abc16"])
            tk.op("dve", lambda e: e.tensor_tensor(out=skip8[:], in0=skip16[:, 0:8], in1=skip16[:, 8:16], op=ALU.add),
                  r=["skip16"], w=["skip8"])
            for d in range(2):
                tk.op("pool", lambda e, d=d: e.memset(Hs[d][:], 0.0), w=[f"Hs{d}"])
                tk.op("pool", lambda e, d=d: e.memset(Hb[d][:], 0.0), w=[f"Hb{d}"])
            tk.op("pool", lambda e: e.memset(Cblk[:], 0.0), w=["Cblk"])

            pc = p2.enter_context(ExitStack())
            raw = [sb(pc, f"raw{i}", [128, 6, 1026], BF16) for i in range(2)]
            cacc = [sb(pc, f"cacc{i}", [128, 1024]) for i in range(2)]
            spans = [(0, 256, 5, 0, 256)]
            a = 256
            while a < 256 + NQ:
                n = min(1024, 256 + NQ - a)
                spans.append((a, n, 6, 256, NKEY))
                a += n
            while a < NKEY:
                n = min(1024, NKEY - a)
                spans.append((a, n, 5, 256, NKEY))
                a += n
            CEND = 256 + NQ
            if _CUT2 <= 1:
                spans = []
            for si, (a, n, nch, s0, s1) in enumerate(spans):
                rs = si % 2
                rw = raw[rs]
                tk.op("pool", lambda e: e.memset(rw[:, :, 0:1], 0.0), w=[f"raw{rs}"])
                tk.op("pool", lambda e: e.memset(rw[:, :, n + 1:n + 2], 0.0), w=[f"raw{rs}"])
                lo, hi = max(a - 1, s0), min(a + n + 1, s1)
                tk.dma("sp", f"raw{rs}", rw[:, 0:5, lo - (a - 1):hi - (a - 1)],
                       XBC[0:640, lo:hi].rearrange("(c p) t -> p c t", p=128), w=[f"raw{rs}"])
                if nch == 6:
                    hic = min(a + n + 1, CEND)
                    tk.dma("sp", f"rawc{rs}", rw[:, 5, lo - (a - 1):hic - (a - 1)],
                           XBC[640:768, lo:hic], w=[f"raw{rs}"])
                for c in range(nch):
                    ca = cacc[c % 2]
                    cn = f"cacc{c % 2}"
                    tk.op("dve", lambda e: e.tensor_scalar(out=ca[:, 0:n], in0=rw[:, c, 1:n + 1], scalar1=cw[:, c, 1:2],
                                                           scalar2=cw[:, c, 3:4], op0=ALU.mult, op1=ALU.add),
                          r=[f"raw{rs}", "cw"], w=[cn])
                    tk.op("dve", lambda e: e.scalar_tensor_tensor(out=ca[:, 0:n], in0=rw[:, c, 0:n], scalar=cw[:, c, 0:1],
                                                                  in1=ca[:, 0:n], op0=ALU.mult, op1=ALU.add),
                          r=[f"raw{rs}", "cw", cn], w=[cn])
                    tk.op("dve", lambda e: e.scalar_tensor_tensor(out=ca[:, 0:n], in0=rw[:, c, 2:n + 2], scalar=cw[:, c, 2:3],
                                                                  in1=ca[:, 0:n], op0=ALU.mult, op1=ALU.add),
                          r=[f"raw{rs}", "cw", cn], w=[cn])
                    dst = U5[:, c, a:a + n] if c < 5 else UC[:, a - 256:a - 256 + n]
                    tk.op("act", lambda e: e.activation(out=dst, in_=ca[:, 0:n], func=AF.Silu), r=[cn], w=["U"])

            tk.barrier()
            pc.close()
            dec2 = [sb(p2, f"dec{i}", [128, 4]) for i in range(2)]
            ea82 = [sb(p2, f"ea8{i}", [128, 8]) for i in range(2)]
            ydg = [sb(p2, f"ydg{i}", [128, 512]) for i in range(2)]
            sps = [sb(p2, f"sps{i}", [128, 256]) for i in range(2)]
            tsk2 = [sb(p2, f"tsk{i}", [128, 512]) for i in range(2)]

            a_all = sb(p2, "a_all", [128, NSLOT, 16])
            ahi = sb(p2, "ahi", [128, NSLOT, 16], BF16)
            alo = sb(p2, "alo", [128, NSLOT, 16], BF16)
            acs_all = sb(p2, "acs_all", [128, NSLOT, 16])
            ea_all = sb(p2, "ea_all", [128, NSLOT, 16])
            w_all = sb(p2, "w_all", [128, NSLOT, 16])
            dec_all = sb(p2, "dec_all", [128, NSLOT, 2, 4])
            NH = NSLOT // 2
            tk.op("dve", lambda e: e.tensor_tensor(out=a_all[:], in0=DT[:], in1=Abc16[:].unsqueeze(1).to_broadcast([128, NSLOT, 16]),
                                                   op=ALU.mult), r=["Abc16"], w=["a_all"])
            tk.op("act", lambda e: e.copy(out=ahi[:], in_=a_all[:]), r=["a_all"], w=["ahi"])
            tk.op("dve", lambda e: e.tensor_tensor(out=alo[:], in0=a_all[:], in1=ahi[:], op=ALU.subtract),
                  r=["a_all", "ahi"], w=["alo"])
            for d in range(2):
                tot = [Yp, YO]
                totn = ["Yp", "YO"]
                for hf in range(2):
                    sl = slice(hf * NH, (hf + 1) * NH)
                    rhs_ = a_all[:, sl, d * 8:(d + 1) * 8]
                    tk.op("pe", lambda e, hf=hf, rhs_=rhs_: e.matmul(ACS[:, hf * 512:hf * 512 + NH * 8], lhsT=masks[:, d, :],
                                                                     rhs=rhs_, start=True, stop=True),
                          r=["masks", "a_all"], w=["ACS"])
                    tk.op("pe", lambda e, hf=hf, rhs_=rhs_: e.matmul(tot[hf][:, 0:NH * 8], lhsT=onesf[:, :], rhs=rhs_,
                                                                     start=True, stop=True),
                          r=["onesf", "a_all"], w=[totn[hf]])
                for hf in range(2):
                    sl = slice(hf * NH, (hf + 1) * NH)
                    acs_v = ACS[:, hf * 512:hf * 512 + NH * 8].rearrange("p (s h) -> p s h", h=8)
                    tot_v = tot[hf][:, 0:NH * 8].rearrange("p (s h) -> p s h", h=8)
                    tk.op("act", lambda e, sl=sl, acs_v=acs_v: e.copy(out=acs_all[:, sl, d * 8:(d + 1) * 8], in_=acs_v),
                          r=["ACS"], w=["acs_all"])
                    tk.op("dve", lambda e, sl=sl, tot_v=tot_v: e.tensor_tensor(out=w_all[:, sl, d * 8:(d + 1) * 8], in0=tot_v,
                                                                               in1=acs_all[:, sl, d * 8:(d + 1) * 8],
                                                                               op=ALU.subtract),
                          r=[totn[hf], "acs_all"], w=["w_all"])
                    for g in range(2):
                        tk.op("act", lambda e, sl=sl, tot_v=tot_v, g=g: e.activation(
                            out=dec_all[g * 64:(g + 1) * 64, sl, d, :], in_=tot_v[g * 64:(g + 1) * 64, :, g * 4:(g + 1) * 4],
                            func=AF.Exp), r=[totn[hf]], w=["dec_all"])
            tk.op("act", lambda e: e.activation(out=w_all[:], in_=w_all[:], func=AF.Exp), r=["w_all"], w=["w_all"])
            tk.op("act", lambda e: e.activation(out=ea_all[:], in_=acs_all[:], func=AF.Exp), r=["acs_all"], w=["ea_all"])

            def ssd_front(j, d, need_out, final, par):
                cols = slice(j * 128, (j + 1) * 128)
                jq = j - 2
                qcols = slice(jq * 128, (jq + 1) * 128)
                last = 127 if d == 0 else 0
                dt8 = DT[:, j, d * 8:(d + 1) * 8]
                tk.grp("pe", [lambda e, c=c: e.transpose(TPs[:, c * 128:(c + 1) * 128], U5[:, c, cols], ident[:])
                              for c in range(5)], r=["U", "ident"], w=["TPs"])
                acs8v = acs_all[:, j, d * 8:(d + 1) * 8]
                wsrc = w_all[:, j, d * 8:(d + 1) * 8]
                if need_out:
                    fns = []
                    for h_ in range(8):
                        for u, src_ in enumerate((ahi, alo)):
                            fns.append(lambda e, h_=h_, u=u, src_=src_: e.matmul(
                                ACS[:, h_ * 128:(h_ + 1) * 128],
                                lhsT=src_[:, j, d * 8 + h_:d * 8 + h_ + 1].to_broadcast([128, 128]),
                                rhs=masksb[:, d, :], start=(h_ % 4 == 0 and u == 0), stop=False, skip_group_check=True))
                    for hf in range(2):
                        fns.append(lambda e, hf=hf: e.matmul(ACS[:, hf * 512:(hf + 1) * 512], lhsT=ident[:, :],
                                                             rhs=negm[:, d, hf * 512:(hf + 1) * 512], start=False, stop=True,
                                                             skip_group_check=True))
                    tk.grp("pe", fns, r=["ahi", "alo", "masksb", "negm", "ident"], w=["ACS"])
                    ACS3 = ACS[:, :].rearrange("p (h l) -> p h l", h=8)
                    tk.op("dve", lambda e: e.tensor_tensor(out=diff[:].rearrange("p (h l) -> p h l", h=8), in0=ACS3,
                                                           in1=acs8v.unsqueeze(2).to_broadcast([128, 8, 128]),
                                                           op=ALU.subtract), r=["ACS", "acs_all"], w=["diff"])
                    tk.op("act", lambda e: e.activation(out=Ee[:], in_=diff[:], func=AF.Exp), r=["diff"], w=["Ee"])
                xs3 = TPs[:, 0:512].rearrange("p (h c) -> p h c", h=8)
                tk.op("dve", lambda e: e.tensor_tensor(out=xd[:].rearrange("p (h c) -> p h c", h=8), in0=xs3,
                                                       in1=dt8.unsqueeze(2).to_broadcast([128, 8, 64]), op=ALU.mult),
                      r=["TPs"], w=["xd"])
                tk.op("act", lambda e: e.copy(out=Bm[:], in_=TPs[:, 512:640]), r=["TPs"], w=["Bm"])
                if final:
                    tk.op("dve", lambda e: e.tensor_tensor(out=tsk2[par][:].rearrange("p (h c) -> p h c", h=8), in0=xs3,
                                                           in1=skip8[:].unsqueeze(2).to_broadcast([128, 8, 64]),
                                                           op=ALU.mult), r=["TPs", "skip8"], w=[f"tsk{par}"])
                tk.op("pool", lambda e: e.tensor_tensor(out=xdw[:].rearrange("p (h c) -> p h c", h=8),
                                                        in0=xd[:].rearrange("p (h c) -> p h c", h=8),
                                                        in1=wsrc.unsqueeze(2).to_broadcast([128, 8, 64]),
                                                        op=ALU.mult), r=["xd", "w_all"], w=["xdw"])
                tk.op("pe", lambda e: e.matmul(SP[:, :], lhsT=Bm[:, :], rhs=xdw[:, :], start=True, stop=True),
                      r=["Bm", "xdw"], w=["SP"])
                for g in range(2):
                    tk.op("act", lambda e, g=g: e.copy(out=sps[par][g * 64:(g + 1) * 64, :],
                                                       in_=SP[g * 64:(g + 1) * 64, g * 256:(g + 1) * 256]),
                          r=["SP"], w=[f"sps{par}"])
                if need_out:
                    for g in range(2):
                        tk.op("pool", lambda e, g=g: e.tensor_copy(out=Cblk[g * 64:(g + 1) * 64, g * 128:(g + 1) * 128],
                                                                  in_=UC[g * 64:(g + 1) * 64, qcols]),
                              r=["U"], w=["Cblk"])
                    tk.op("pe", lambda e: e.matmul(G8[:, 0:256], lhsT=U5[:, 4, cols], rhs=Cblk[:, :],
                                                   start=True, stop=True), r=["U", "Cblk"], w=["G8"])
                    tk.op("dve", lambda e: e.tensor_tensor(
                        out=MT[:].rearrange("p (g r l) -> p g r l", g=2, r=4),
                        in0=Ee[:].rearrange("p (g r l) -> p g r l", g=2, r=4),
                        in1=G8[:, 0:256].rearrange("p (g l) -> p g l", g=2).unsqueeze(2).to_broadcast([128, 2, 4, 128]),
                        op=ALU.mult), r=["Ee", "G8"], w=["MT"])
                    tk.grp("pe", [lambda e, h=h: e.matmul(Yp[:, h * 64:(h + 1) * 64], lhsT=MT[:, h * 128:(h + 1) * 128],
                                                          rhs=xd[:, h * 64:(h + 1) * 64], start=True, stop=True)
                                  for h in range(8)], r=["MT", "xd"], w=["Yp"])
                    tk.op("act", lambda e: e.copy(out=ydg[par][:], in_=Yp[:, :]), r=["Yp"], w=[f"ydg{par}"])

            def ssd_back(j, d, need_out, final, par):
                jq = j - 2
                qcols = slice(jq * 128, (jq + 1) * 128)
                if need_out:
                    tk.op("pe", lambda e: e.matmul(YO[:, :], lhsT=UC[:, qcols], rhs=Hb[d][:, :], start=True, stop=True),
                          r=["U", f"Hb{d}"], w=["YO"])
                    tk.op("dve", lambda e: e.tensor_tensor(out=ty[:].rearrange("p (h c) -> p h c", h=8),
                                                           in0=YO[:, :].rearrange("p (h c) -> p h c", h=8),
                                                           in1=ea_all[:, j, d * 8:(d + 1) * 8].unsqueeze(2).to_broadcast([128, 8, 64]),
                                                           op=ALU.mult), r=["YO", "ea_all"], w=["ty"])
                H3 = Hs[d][:].rearrange("p (r c) -> p r c", r=4)
                tk.op("dve", lambda e: e.tensor_tensor(out=H3, in0=H3,
                                                       in1=dec_all[:, j, d, :].unsqueeze(2).to_broadcast([128, 4, 64]),
                                                       op=ALU.mult), r=[f"Hs{d}", "dec_all"], w=[f"Hs{d}"])
                tk.op("dve", lambda e: e.tensor_tensor(out=Hs[d][:], in0=Hs[d][:], in1=sps[par][:], op=ALU.add),
                      r=[f"Hs{d}", f"sps{par}"], w=[f"Hs{d}"])
                for g in range(2):
                    tk.op("act", lambda e, g=g: e.copy(out=Hb[d][g * 64:(g + 1) * 64, g * 256:(g + 1) * 256],
                                                       in_=Hs[d][g * 64:(g + 1) * 64, :]), r=[f"Hs{d}"], w=[f"Hb{d}"])
                if not need_out:
                    return
                if not final:
                    tk.op("pool", lambda e: e.tensor_tensor(out=Yb[:, jq, :], in0=ty[:], in1=ydg[par][:], op=ALU.add),
                          r=["ty", f"ydg{par}"], w=[("Yb", jq)])
                    return
                zi = jq % 2
                tk.dma("sp", f"zt{zi}", zt[zi][:], ZS[jq], w=[f"zt{zi}"])
                tk.op("pool", lambda e: e.tensor_tensor(out=ysum[:], in0=ty[:], in1=ydg[par][:], op=ALU.add),
                      r=["ty", f"ydg{par}"], w=["ysum"])
                tk.op("pool", lambda e: e.tensor_tensor(out=ysum[:], in0=ysum[:], in1=Yb[:, jq, :], op=ALU.add),
                      r=["ysum", ("Yb", jq)], w=["ysum"])
                tk.op("pool", lambda e: e.tensor_tensor(out=ysum[:], in0=ysum[:], in1=tsk2[par][:], op=ALU.add),
                      r=["ysum", f"tsk{par}"], w=["ysum"])
                tk.op("pool", lambda e: e.tensor_tensor(out=ysum[:], in0=ysum[:], in1=zt[zi][:], op=ALU.mult),
                      r=["ysum", f"zt{zi}"], w=["ysum"])
                tk.op("act", lambda e: e.activation(out=junk2[:], in_=ysum[:], func=AF.Square,
                                                    accum_out=st2[:, 0:1]), r=["ysum"], w=["junk2", "st2"])
                tk.op("act", lambda e: e.activation(out=st2[:, 1:2], in_=st2[:, 0:1], func=AF.Sqrt,
                                                    scale=1.0 / 512, bias=EPS), r=["st2"], w=["st2"])
                tk.op("dve", lambda e: e.reciprocal(out=st2[:, 2:3], in_=st2[:, 1:2]), r=["st2"], w=["st2"])
                tk.op("dve", lambda e: e.scalar_tensor_tensor(out=yn[:], in0=ysum[:], scalar=st2[:, 2:3],
                                                              in1=ssdnbc[:], op0=ALU.mult, op1=ALU.mult),
                      r=["ysum", "st2", "ssdnbc"], w=["yn"])
                tk.grp("pe", [lambda e, c=c: e.transpose(YO[:, :].bitcast(BF16)[:, c * 128:(c + 1) * 128],
                                                          yn[:, c * 128:(c + 1) * 128], ident[:])
                              for c in range(4)], r=["yn", "ident"], w=["YO"])
                tk.op("act", lambda e: e.copy(out=ynT[zi][:].rearrange("p c t -> p (c t)"),
                                              in_=YO[:, :].bitcast(BF16)[:, 0:512]), r=["YO"], w=[f"ynT{zi}"])
                tk.dma("sp", f"yt{zi}", YT[:, qcols].rearrange("(c p) t -> p c t", p=128), ynT[zi][:],
                       r=[f"ynT{zi}"], w=[("YT", jq)])

            steps = [(1, 1, False, False), (0, 1, False, False)]
            steps += [(j, 1, (j - 2) < NOWN, False) for j in range(NSLOT - 1, 1, -1)]
            steps += [(0, 0, False, False), (1, 0, False, False)]
            steps += [(j, 0, True, True) for j in range(2, 2 + NOWN)]
            ssd_front(*steps[0], 0)
            for k, stp in enumerate(steps):
                fo = tk.record(ssd_front, *steps[k + 1], (k + 1) % 2) if k + 1 < len(steps) else []
                bo = tk.record(ssd_back, *stp, k % 2)
                tk.interleave(fo, bo)
        tk.barrier()
        if stop < 4:
            return nc

        with ExitStack() as p4:
            wo = sb(p4, "wo", [128, 8, D], BF16)
            wip = sb(p4, "wip", [128, 8, 2 * D], BF16)
            lin = sb(p4, "lin", [128, 8, 256], BF16)
            wop = sb(p4, "wop", [128, 8, D], BF16)
            poolm = sb(p4, "poolm", [128, 36, 128], BF16)
            tk.dma("pool", "w1", wo[:], w_out_mix.rearrange("(k p) n -> p k n", p=128), w=["wo"])
            tk.dma("pool", "w2", wip[:], w_in_pool.rearrange("(k p) n -> p k n", p=128), w=["wip"])
            tk.dma("pool", "w3", lin[:], pool_lin.rearrange("g (kk p) n -> p (g kk) n", p=128), w=["lin"])
            tk.dma("pool", "w4", wop[:], w_out_pool.rearrange("(k p) n -> p k n", p=128), w=["wop"])
            tk.dma("sp", "c0", poolm[:], poolm_d[:, :, :], w=["poolm"])
            g0bc = sb(p4, "g0bc", [128, D])
            A1bc = sb(p4, "A1bc", [128, D])
            B1bc = sb(p4, "B1bc", [128, D])
            g1bc = sb(p4, "g1bc", [128, D])
            fnbc = sb(p4, "fnbc", [128, D])
            psbc = sb(p4, "psbc", [128, D])
            nw1 = sb(p4, "nw1", [128, D])
            tk.dma("sp", "c1", g0bc[:], MODS[0:1, 2 * D:3 * D].partition_broadcast(128), w=["g0bc"])
            tk.dma("sp", "c2", B1bc[:], MODS[2:3, 0:D].partition_broadcast(128), w=["B1bc"])
            tk.dma("sp", "c3", A1bc[:], MODS[2:3, D:2 * D].partition_broadcast(128), w=["A1bc"])
            tk.dma("sp", "c4", g1bc[:], MODS[2:3, 2 * D:3 * D].partition_broadcast(128), w=["g1bc"])
            tk.dma("sp", "c5", fnbc[:], final_norm[0:1, :].partition_broadcast(128), w=["fnbc"])
            tk.dma("sp", "c6", psbc[:], pool_scale[0:1, :].partition_broadcast(128), w=["psbc"])
            tk.dma("sp", "c7", nw1[:], norm_w[1:2, :].partition_broadcast(128), w=["nw1"])
            tk.op("dve", lambda e: e.scalar_tensor_tensor(out=A1bc[:], in0=A1bc[:], scalar=1.0, in1=nw1[:],
                                                          op0=ALU.add, op1=ALU.mult), r=["A1bc", "nw1"], w=["A1bc"])
            tk.op("dve", lambda e: e.tensor_tensor(out=wo[:], in0=wo[:], in1=g0bc[:].unsqueeze(1).to_broadcast([128, 8, D]),
                                                   op=ALU.mult), r=["wo", "g0bc"], w=["wo"])
            tk.op("pool", lambda e: e.tensor_tensor(out=wop[:], in0=wop[:], in1=g1bc[:].unsqueeze(1).to_broadcast([128, 8, D]),
                                                    op=ALU.mult), r=["wop", "g1bc"], w=["wop"])
            lin4 = lin[:].rearrange("p (g k) n -> p g k n", g=4)
            ps4 = psbc[:].rearrange("p (g n) -> p g n", g=4).unsqueeze(2).to_broadcast([128, 4, 2, 256])
            tk.op("dve", lambda e: e.tensor_tensor(out=lin4, in0=lin4, in1=ps4, op=ALU.mult), r=["lin", "psbc"], w=["lin"])
            xt4 = [sb(p4, f"xt4{i}", [128, D]) for i in range(2)]
            at4 = [sb(p4, f"at4{i}", [128, 4, 128], BF16) for i in range(2)]
            atk = [sb(p4, f"atk{i}", [128, 512], BF16) for i in range(2)]
            yT4 = [sb(p4, f"yT4{i}", [128, 4, 128], BF16) for i in range(2)]
            tfa = sb(p4, "tfa", [128, D])
            tfb = sb(p4, "tfb", [128, D])
            x1 = [sb(p4, f"x1{i}", [128, D]) for i in range(3)]
            sta = sb(p4, "sta", [128, 4])
            stb = sb(p4, "stb", [128, 4])
            junka = sb(p4, "junka", [128, D], BF16)
            junkb = sb(p4, "junkb", [128, D], BF16)
            hb4 = sb(p4, "hb4", [128, D], BF16)
            hT4 = sb(p4, "hT4", [128, 8, 128], BF16)
            ub = [sb(p4, f"ub{i}", [128, D], BF16) for i in range(4)]
            sg = [sb(p4, f"sg{i}", [128, D], BF16) for i in range(3)]
            mTb = sb(p4, "mTb", [128, 8, 128], BF16)
            pb4 = sb(p4, "pb4", [128, D], BF16)
            pT4 = sb(p4, "pT4", [128, 8, 128], BF16)
            x2 = sb(p4, "x2", [128, D])
            ot = [sb(p4, f"ot{i}", [128, D]) for i in range(2)]
            PA = ps(p4, "PA", [128, 1024])
            PB = ps(p4, "PB", [128, 1024])
            PC = ps(p4, "PC", [128, 1024])
            PD = ps(p4, "PD", [128, 512])
            TP4 = ps(p4, "TP4", [128, 1024], BF16)

            def bundle2(ops):
                a_, b_ = ops[-2], ops[-1]
                ops[-2:] = [lambda: (a_(), b_())]

            def rms4(ops, src, name_r, st_, stn, jk, jkn):
                ops.append(lambda: tk.op("act", lambda e: e.activation(out=jk[:], in_=src, func=AF.Square,
                                                                         accum_out=st_[:, 0:1]), r=[name_r], w=[jkn, stn]))
                ops.append(lambda: tk.op("act", lambda e: e.activation(out=st_[:, 1:2], in_=st_[:, 0:1], func=AF.Sqrt,
                                                                         scale=1.0 / D, bias=EPS), r=[stn], w=[stn]))
                ops.append(lambda: tk.op("dve", lambda e: e.reciprocal(out=st_[:, 2:3], in_=st_[:, 1:2]), r=[stn], w=[stn]))

            def stage_a(j):
                ops = []
                s_ = j % 2
                r_ = j % 3
                u_ = j % 4
                qcols = slice(j * 128, (j + 1) * 128)
                ops.append(lambda: tk.dma("sp", f"xt4{s_}", xt4[s_][:], xs[j * 128:(j + 1) * 128, :], w=[f"xt4{s_}"]))
                ops.append(lambda: tk.dma("sp", f"at4{s_}", atk[s_][:], AT2[j], w=[f"atk{s_}"]))
                ops.append(lambda: tk.grp("pe", [lambda e, k=k: e.transpose(TP4[:, k * 128:(k + 1) * 128],
                                                                             atk[s_][:, k * 128:(k + 1) * 128], ident[:])
                                                 for k in range(4)], r=[f"atk{s_}", "ident"], w=["TP4"]))
                ops.append(lambda: tk.op("act", lambda e: e.copy(out=at4[s_][:].rearrange("p k t -> p (k t)"),
                                                                 in_=TP4[:, 0:512]), r=["TP4"], w=[f"at4{s_}"]))
                bundle2(ops)
                ops.append(lambda: tk.dma("sp", f"yT4{s_}", yT4[s_][:], YT[:, qcols].rearrange("(c p) t -> p c t", p=128),
                                          w=[f"yT4{s_}"]))
                fns = []
                for hf in range(2):
                    for k in range(8):
                        lt = at4[s_][:, k, :] if k < 4 else yT4[s_][:, k - 4, :]
                        fns.append(lambda e, k=k, hf=hf, lt=lt: e.matmul(PA[:, hf * 512:(hf + 1) * 512], lhsT=lt,
                                                                        rhs=wo[:, k, hf * 512:(hf + 1) * 512],
                                                                        start=(k == 0), stop=(k == 7)))
                ops.append(lambda: tk.grp("pe", fns, r=[f"at4{s_}", f"yT4{s_}", "wo"], w=["PA"]))
                ops.append(lambda: tk.op("dve", lambda e: e.tensor_tensor(out=x1[r_][:], in0=PA[:, :], in1=xt4[s_][:], op=ALU.add),
                                         r=["PA", f"xt4{s_}"], w=[f"x1{r_}"]))
                rms4(ops, x1[r_][:], f"x1{r_}", sta, "sta", junka, "junka")
                ops.append(lambda: tk.op("dve", lambda e: e.scalar_tensor_tensor(
                    out=tfa[:], in0=x1[r_][:], scalar=sta[:, 2:3], in1=A1bc[:], op0=ALU.mult, op1=ALU.mult),
                    r=[f"x1{r_}", "sta", "A1bc"], w=["tfa"]))
                ops.append(lambda: tk.op("pool", lambda e: e.tensor_tensor(out=hb4[:], in0=tfa[:], in1=B1bc[:], op=ALU.add),
                                         r=["tfa", "B1bc"], w=["hb4"]))
                ops.append(lambda: tk.grp("pe", [lambda e, k=k: e.transpose(TP4[:, k * 128:(k + 1) * 128],
                                                                             hb4[:, k * 128:(k + 1) * 128], ident[:])
                                                 for k in range(8)], r=["hb4", "ident"], w=["TP4"]))
                ops.append(lambda: tk.op("act", lambda e: e.copy(out=hT4[:].rearrange("p k t -> p (k t)"), in_=TP4[:, :]),
                                         r=["TP4"], w=["hT4"]))
                bundle2(ops)
                for (pt, name, c0) in ((PB, "PB", 0), (PC, "PC", D)):
                    fns2 = []
                    for hf in range(2):
                        for k in range(8):
                            fns2.append(lambda e, k=k, hf=hf, pt=pt, c0=c0: e.matmul(
                                pt[:, hf * 512:(hf + 1) * 512], lhsT=hT4[:, k, :],
                                rhs=wip[:, k, c0 + hf * 512:c0 + (hf + 1) * 512], start=(k == 0), stop=(k == 7)))
                    ops.append(lambda fns2=fns2, name=name: tk.grp("pe", fns2, r=["hT4", "wip"], w=[name]))
                ops.append(lambda: tk.op("dve", lambda e: e.tensor_copy(out=ub[u_][:], in_=PB[:, :]), r=["PB"], w=[f"ub{u_}"]))
                ops.append(lambda: tk.op("act", lambda e: e.activation(out=sg[r_][:], in_=PC[:, :], func=AF.Silu),
                                         r=["PC"], w=[f"sg{r_}"]))
                return ops

            def stage_b(i):
                ops = []
                r_ = i % 3
                o_ = i % 2
                var = 0 if i == 0 else 1
                nbs = [1, 2] if i == 0 else [0, 1, 2]
                for hf in range(2):
                    fns = []
                    for c in range(4 * hf, 4 * hf + 4):
                        g = c // 2
                        for n_i, nb in enumerate(nbs):
                            ubn = ub[(i + nb - 1) % 4]
                            last_ = (n_i == len(nbs) - 1)
                            fns.append(lambda e, c=c, g=g, nb=nb, ubn=ubn, n_i=n_i, hf=hf, last_=last_: e.matmul(
                                PD[:, (c - 4 * hf) * 128:(c - 4 * hf + 1) * 128], lhsT=ubn[:, c * 128:(c + 1) * 128],
                                rhs=poolm[:, var * 12 + g * 3 + nb, :], start=(n_i == 0), stop=(last_ and var == 1)))
                            if var == 0:
                                fns.append(lambda e, c=c, g=g, nb=nb, ubn=ubn, hf=hf, last_=last_: e.matmul(
                                    PD[:, (c - 4 * hf) * 128:(c - 4 * hf + 1) * 128], lhsT=ubn[:, c * 128:(c + 1) * 128],
                                    rhs=poolm[:, 24 + g * 3 + nb, :], start=False, stop=last_))
                    ops.append(lambda fns=fns: tk.grp("pe", fns, r=[f"ub{(i + nb - 1) % 4}" for nb in nbs] + ["poolm"], w=["PD"]))
                    ops.append(lambda hf=hf: tk.op("act", lambda e: e.copy(
                        out=mTb[:, 4 * hf:4 * hf + 4, :].rearrange("p k t -> p (k t)"), in_=PD[:, :]), r=["PD"], w=["mTb"]))
                for hf in range(2):
                    fns = []
                    for g in range(2 * hf, 2 * hf + 2):
                        for kk in range(2):
                            fns.append(lambda e, g=g, kk=kk, hf=hf: e.matmul(
                                PD[:, (g - 2 * hf) * 256:(g - 2 * hf + 1) * 256], lhsT=mTb[:, g * 2 + kk, :],
                                rhs=lin[:, g * 2 + kk, :], start=(kk == 0), stop=(kk == 1)))
                    hs = slice(hf * 512, (hf + 1) * 512)
                    ops.append(lambda fns=fns: tk.grp("pe", fns, r=["mTb", "lin"], w=["PD"]))
                    ops.append(lambda hs=hs: tk.op("dve", lambda e: e.tensor_tensor(out=pb4[:, hs], in0=PD[:, :], in1=sg[r_][:, hs],
                                                                                   op=ALU.mult), r=["PD", f"sg{r_}"], w=["pb4"]))
                ops.append(lambda: tk.grp("pe", [lambda e, k=k: e.transpose(TP4[:, k * 128:(k + 1) * 128],
                                                                             pb4[:, k * 128:(k + 1) * 128], ident[:])
                                                 for k in range(8)], r=["pb4", "ident"], w=["TP4"]))
                ops.append(lambda: tk.op("act", lambda e: e.copy(out=pT4[:].rearrange("p k t -> p (k t)"), in_=TP4[:, :]),
                                         r=["TP4"], w=["pT4"]))
                bundle2(ops)
                for hf in range(2):
                    hs = slice(hf * 512, (hf + 1) * 512)
                    fns = [lambda e, k=k, hs=hs: e.matmul(PD[:, :], lhsT=pT4[:, k, :], rhs=wop[:, k, hs],
                                                          start=(k == 0), stop=(k == 7)) for k in range(8)]
                    ops.append(lambda fns=fns: tk.grp("pe", fns, r=["pT4", "wop"], w=["PD"]))
                    ops.append(lambda hs=hs: tk.op("dve", lambda e: e.tensor_tensor(out=x2[:, hs], in0=PD[:, :], in1=x1[r_][:, hs],
                                                                                   op=ALU.add), r=["PD", f"x1{r_}"], w=["x2"]))
                rms4(ops, x2[:], "x2", stb, "stb", junkb, "junkb")
                ops.append(lambda: tk.op("dve", lambda e: e.scalar_tensor_tensor(
                    out=ot[o_][:], in0=x2[:], scalar=stb[:, 2:3], in1=fnbc[:], op0=ALU.mult, op1=ALU.mult),
                    r=["x2", "stb", "fnbc"], w=[f"ot{o_}"]))
                ops.append(lambda: tk.dma("sp", f"ot{o_}", out_d[i * 128:(i + 1) * 128, :], ot[o_][:], r=[f"ot{o_}"],
                                          w=[("out", i)]))
                return ops

            def interleave(a, b):
                na, nb_ = len(a), len(b)
                ia = ib = 0
                while ia < na or ib < nb_:
                    if ib >= nb_ or (ia < na and ia * nb_ <= ib * na):
                        a[ia](); ia += 1
                    else:
                        b[ib](); ib += 1

            for f_ in stage_a(0):
                f_()
            for f_ in stage_a(1):
                f_()
            for j in range(2, NOWN):
                interleave(stage_a(j), stage_b(j - 2))
            for f_ in stage_b(NOWN - 2):
                f_()
        tk.barrier()
    return nc


POOL_WINDOWS = (2, 4, 8, 16)


def _consts(flip):
    bf = ml_dtypes.bfloat16
    ident = np.eye(128, dtype=np.float32).astype(bf)
    rope = np.zeros((NSLOT, 128, 64), np.float32)
    rope[0:2, :, 0:32] = 1.0
    tl = np.arange(SEQ)
    t = (SEQ - 1 - tl) if flip else tl
    row = (t // 64).astype(np.float32)
    col = (t % 64).astype(np.float32)
    inv = (1.0 / (10000.0 ** (np.arange(0, 16, 2, dtype=np.float32) / 16.0))).astype(np.float32)
    ang = np.concatenate([row[:, None] * inv, col[:, None] * inv], axis=-1).astype(np.float32)
    cos, sin = np.cos(ang), np.sin(ang)
    tab = np.concatenate([cos, cos, -sin, sin], axis=-1).astype(np.float32)
    rope[2:] = tab.reshape(NT, 128, 64)
    k = np.arange(128)
    masks = np.zeros((128, 2, 128), np.float32)
    masks[:, 0, :] = (k[:, None] <= k[None, :])
    masks[:, 1, :] = (k[:, None] >= k[None, :])
    negm = np.zeros((128, 2, 8, 128), np.float32)
    negm[:, 0] = np.where(k[:, None] <= k[None, :], 0.0, NEG)[:, None, :]
    negm[:, 1] = np.where(k[:, None] >= k[None, :], 0.0, NEG)[:, None, :]
    negm = negm.reshape(128, 2, 1024).astype(bf)
    poolm = np.zeros((128, 2, 4, 3, 128), np.float32)
    for var in range(2):
        for g, w in enumerate(POOL_WINDOWS):
            if flip:
                lo_off, hi_off = -(w - w // 2 - 1), w // 2
            else:
                lo_off, hi_off = -(w // 2), (w - w // 2 - 1)
            for tt in range(128):
                lo, hi = tt + lo_off, tt + hi_off
                if var == 0:
                    lo = max(lo, 0)
                cnt = hi - lo + 1
                for sidx in range(lo, hi + 1):
                    nb = 1 + (sidx // 128)
                    poolm[sidx % 128, var, g, nb, tt] += 1.0 / cnt
                poolm[tt, var, g, 1, tt] -= 1.0
    pm_hi = poolm.astype(bf)
    pm_lo = (poolm - pm_hi.astype(np.float32)).astype(bf)
    poolm = np.concatenate([pm_hi.reshape(128, 24, 128), pm_lo[:, 0].reshape(128, 12, 128)], axis=1)
    return dict(ident=ident, rope=rope, masks=masks, negm=negm, poolm=poolm)


def _in_maps(inp):
    f = lambda a: np.ascontiguousarray(np.asarray(a, dtype=np.float32))
    x, c, ctx, c_ctx = f(inp["x"]), f(inp["c"]), f(inp["ctx"]), f(inp["c_ctx"])
    consts = [_consts(False), _consts(True)]
    w_in = f(inp["w_in_mix"])[0]
    w_in_f = w_in.copy()
    w_in_f[:, 2464:2472] = w_in[:, 2472:2480]
    w_in_f[:, 2472:2480] = w_in[:, 2464:2472]
    ssdp = np.stack([f(inp["a_log"])[0].reshape(16), f(inp["dt_bias"])[0].reshape(16),
                     f(inp["d_skip"])[0].reshape(16)])
    ssdp_f = np.concatenate([ssdp[:, 8:16], ssdp[:, 0:8]], axis=1)
    conv_w = f(inp["conv_w"])[0]
    shared = dict(
        mod_w=f(inp["mod_w"]), mod_b=f(inp["mod_b"]), norm_w=f(inp["norm_w"]),
        q_norm=f(inp["q_norm"]).reshape(1, 384), w_uq=f(inp["w_uq"])[0],
        kv_norm=f(inp["kv_norm"]).reshape(1, 256), w_ukv=f(inp["w_ukv"])[0],
        conv_b=f(inp["conv_b"]).reshape(1, 768), ssd_norm=f(inp["ssd_norm"]).reshape(1, 512),
        w_out_mix=f(inp["w_out_mix"])[0], w_in_pool=f(inp["w_in_pool"])[0], pool_lin=f(inp["pool_lin"])[0],
        pool_scale=f(inp["pool_scale"]).reshape(1, D), w_out_pool=f(inp["w_out_pool"])[0],
        final_norm=f(inp["final_norm"]).reshape(1, D))
    maps = []
    for core in range(8):
        b, half = core // 2, core % 2
        flip = half == 1
        m = dict(shared)
        if _SMALL:
            m.pop("mod_w")
        xb = x[b][:SEQ]
        m["xs"] = np.ascontiguousarray(xb[::-1]) if flip else np.ascontiguousarray(xb)
        m["ctx"] = np.ascontiguousarray(ctx[b][::-1]) if flip else ctx[b]
        m["cv"] = np.ascontiguousarray(np.concatenate([c[b].reshape(8, 128).T, c_ctx.reshape(8, 128).T], axis=1))
        m["w_in"] = w_in_f if flip else w_in
        cwl = conv_w[::-1] if flip else conv_w
        m["conv_w"] = np.ascontiguousarray(cwl)
        cb = f(inp["conv_b"]).reshape(768)
        m["cwt"] = np.ascontiguousarray(np.concatenate([cwl, cb[None, :]], 0).reshape(4, 6, 128).transpose(2, 1, 0))
        m["ssdp"] = np.ascontiguousarray(ssdp_f if flip else ssdp)
        m.update(consts[half])
        maps.append(m)
    return maps


_NC = {}


def kernel(**inputs):
    if "nc" not in _NC:
        _NC["nc"] = build(stop=float(os.environ.get("KSTOP", "99")))
    maps = _in_maps(inputs)
    res = run_bass_kernel_spmd(_NC["nc"], maps, core_ids=list(range(8)))
    out = np.zeros((4, SEQ, D), np.float32)
    for core in range(8):
        b, half = core // 2, core % 2
        o = np.asarray(res.results[core]["out"], dtype=np.float32)
        if half == 0:
            out[b, 0:NOUT * 128] = o
        else:
            out[b, NOUT * 128:] = o[::-1]
    return out
```

```python
import numpy as np
import ml_dtypes
from contextlib import ExitStack
import concourse.bass as bass
import concourse.mybir as mybir
from concourse.bass_utils import run_bass_kernel_spmd

F32 = mybir.dt.float32
BF16 = mybir.dt.bfloat16
AF = mybir.ActivationFunctionType
ALU = mybir.AluOpType

import os
_CUT = float(os.environ.get('KCUT', '99'))
_CUT3 = float(os.environ.get('KCUT3', '99'))
_CUT2 = float(os.environ.get('KCUT2', '99'))
_SMALL = os.environ.get("KSMALL") == "1"
D = 1024
NT = 4 if _SMALL else 64
NOWN = 3 if _SMALL else 33
NOUT = NOWN - 1
SEQ = NT * 128
NSLOT = NT + 2
NKEY = NSLOT * 128
NQ = NOWN * 128
EPS = 1e-6
QSCALE = 96 ** -0.5
NEG = -30000.0
VP = 66


class Trk:
    def __init__(self, nc, es):
        self.nc = nc
        self.es = es
        self.engs = {"pe": nc.tensor, "act": nc.scalar, "dve": nc.vector, "pool": nc.gpsimd, "sp": nc.sync}
        self.esem = {}
        self.cnt = {}
        self.seen = {e: {} for e in self.engs}
        self.last_w = {}
        self.readers = {}
        self.dsem = {}
        self.dcnt = {}
        self.nsem = 0
        self.epoch = 0
        self.rec = None
        self.scratch = set()
        self._bundle = None
        self.excl = set()
        self.new_epoch()

    def _newsem(self, name):
        self.nsem += 1
        return self.es.enter_context(self.nc.semaphore(f"{name}_{self.nsem}"))

    def new_epoch(self):
        self.epoch += 1
        if not self.esem:
            for e in self.engs:
                self.esem[e] = self._newsem(f"e{e}")
                self.cnt[e] = 0
        self.keymap = {}
        self.last_w = {}
        self.readers = {}

    def _wait(self, eng, tok):
        kind, key, val = tok
        if kind == "e":
            if key == eng and eng == "pe":
                return
            sem = self.esem[key]
        else:
            sem = self.dsem[key]
        sk = (kind, key)
        if self.seen[eng].get(sk, 0) >= val:
            return
        self.seen[eng][sk] = val
        self.engs[eng].wait_ge(sem, val)

    def _deps(self, eng, r, w):
        deps = []
        for b in r:
            t = self.last_w.get(b)
            if t:
                deps.append(t)
        for b in w:
            t = self.last_w.get(b)
            if t:
                deps.append(t)
            deps.extend(self.readers.get(b, ()))
        for t in deps:
            self._wait(eng, t)

    def _commit(self, tok, r, w):
        for b in w:
            self.last_w[b] = tok
            self.readers[b] = []
        for b in r:
            if b not in w:
                self.readers.setdefault(b, []).append(tok)

    def _x(self, r, w):
        xr = [b for b in r if b in self.excl and b not in w]
        return (r, list(w) + xr) if xr else (r, w)

    def op(self, eng, fn, r=(), w=()):
        if self.rec is not None:
            self._rec_add(lambda: self.op(eng, fn, r, w), r, w)
            return
        r, w = self._x(r, w)
        self._deps(eng, r, w)
        inst = fn(self.engs[eng])
        self.cnt[eng] += 1
        inst.then_inc(self.esem[eng], 1)
        self._commit(("e", eng, self.cnt[eng]), r, w)

    def grp(self, eng, fns, r=(), w=()):
        if self.rec is not None:
            self._rec_add(lambda: self.grp(eng, fns, r, w), r, w)
            return
        r, w = self._x(r, w)
        self._deps(eng, r, w)
        inst = None
        for fn in fns:
            inst = fn(self.engs[eng])
        self.cnt[eng] += 1
        inst.then_inc(self.esem[eng], 1)
        self._commit(("e", eng, self.cnt[eng]), r, w)

    def _rec_add(self, clo, r, w):
        hit_w = any(b in self.scratch for b in w) and not any(b in self.scratch for b in r)
        hit_r = any(b in self.scratch for b in r)
        if self._bundle is not None:
            self._bundle.append(clo)
            if hit_r:
                inner, self._bundle = self._bundle, None
                self.rec.append(lambda: [g() for g in inner])
            return
        if hit_w:
            self._bundle = [clo]
            return
        self.rec.append(clo)

    def atomic(self, f):
        if self.rec is None:
            f()
            return
        outer, self.rec = self.rec, []
        f()
        inner, self.rec = self.rec, outer
        self.rec.append(lambda: [g() for g in inner])

    def record(self, f, *a):
        assert self.rec is None
        self.rec = []
        f(*a)
        assert self._bundle is None
        ops, self.rec = self.rec, None
        return ops

    @staticmethod
    def interleave(a, b):
        na, nb_ = len(a), len(b)
        ia = ib = 0
        while ia < na or ib < nb_:
            if ib >= nb_ or (ia < na and ia * nb_ <= ib * na):
                a[ia](); ia += 1
            else:
                b[ib](); ib += 1

    def dma(self, q, key, out, in_, r=(), w=(), **kw):
        if self.rec is not None:
            self._rec_add(lambda: self.dma(q, key, out, in_, r, w, **kw), r, w)
            return
        km = self.keymap.setdefault(q, {})
        key = (q, km.setdefault(key, len(km)))
        if key not in self.dsem:
            self.dsem[key] = self._newsem(f"d{key[0]}{key[1]}")
            self.dcnt[key] = 0
        self._deps(q, r, w)
        self.engs[q].dma_start(out=out, in_=in_, **kw).then_inc(self.dsem[key], 16)
        self.dcnt[key] += 16
        self._commit(("d", key, self.dcnt[key]), r, w)

    def barrier(self):
        for e in self.engs:
            for e2 in self.engs:
                if e2 != e and self.cnt[e2] > 0:
                    self._wait(e, ("e", e2, self.cnt[e2]))
            for key, v in self.dcnt.items():
                self._wait(e, ("d", key, v))
        self.new_epoch()


def build(dbg=False, stop=99):
    nc = bass.Bass("TRN2", target_bir_lowering=False)

    def din(name, shape, dt=F32):
        return nc.dram_tensor(name, list(shape), dt, kind="ExternalInput").ap()

    def dscr(name, shape, dt=BF16):
        kind = "ExternalOutput" if (dbg and name in DBG) else "Internal"
        return nc.dram_tensor(name, list(shape), dt, kind=kind).ap()

    DBG = {"QT", "KT", "KPE", "VS", "ZS", "XBC", "MODS", "DTS", "AT2", "YT", "X1"}

    xs = din("xs", [SEQ, D])
    ctx = din("ctx", [256, D])
    cv = din("cv", [128, 16])
    mod_w = None if _SMALL else din("mod_w", [2, D, 3 * D])
    mod_b = din("mod_b", [2, 3 * D])
    norm_w = din("norm_w", [2, D])
    w_in = din("w_in", [D, 2480])
    q_norm = din("q_norm", [1, 384])
    w_uq = din("w_uq", [384, 768])
    kv_norm = din("kv_norm", [1, 256])
    w_ukv = din("w_ukv", [256, 1024])
    conv_w = din("conv_w", [3, 768])
    conv_b = din("conv_b", [1, 768])
    ssdp = din("ssdp", [3, 16])
    ssd_norm = din("ssd_norm", [1, 512])
    w_out_mix = din("w_out_mix", [D, D])
    w_in_pool = din("w_in_pool", [D, 2 * D])
    pool_lin = din("pool_lin", [4, 256, 256])
    pool_scale = din("pool_scale", [1, D])
    w_out_pool = din("w_out_pool", [D, D])
    final_norm = din("final_norm", [1, D])
    ident_d = din("ident", [128, 128], BF16)
    rope_d = din("rope", [NSLOT, 128, 64])
    masks_d = din("masks", [128, 2, 128])
    negm_d = din("negm", [128, 2, 8 * 128], BF16)
    poolm_d = din("poolm", [128, 36, 128], BF16)
    cwt_d = din("cwt", [128, 6, 4])
    out_d = nc.dram_tensor("out", [NOUT * 128, D], F32, kind="ExternalOutput").ap()

    MODS = din("MODS", [3, 3 * D]) if _SMALL else dscr("MODS", [3, 3 * D], F32)
    QT = dscr("QT", [768, NQ])
    KT = dscr("KT", [512, NKEY])
    KPE = dscr("KPE", [32, NKEY])
    GA = dscr("GA", [512, NQ])
    ZS = dscr("ZS", [NOWN, 128, 512])
    XBC = dscr("XBC", [768, NKEY])
    AT = dscr("AT", [8, 64, NQ])
    YT = dscr("YT", [512, NQ])

    with ExitStack() as es:
        tk = Trk(nc, es)

        def sb(st, name, shape, dt=F32):
            return st.enter_context(nc.sbuf_tensor("s_" + name, list(shape), dt))

        def ps(st, name, shape, dt=F32):
            tk.excl.add(name)
            return st.enter_context(nc.psum_tensor("p_" + name, list(shape), dt))

        ident = sb(es, "ident", [128, 128], BF16)
        DT = sb(es, "DT", [128, NSLOT, 16])
        tk.dma("sp", "c0", ident[:], ident_d[:, :], w=["ident"])
        p13 = es.enter_context(ExitStack())
        Vres = sb(p13, "Vres", [128, NSLOT, 8, VP], BF16)
        GAres = sb(p13, "GAres", [128, NOWN, 512], BF16)

        with ExitStack() as p0:
          if not _SMALL:
              cvt = sb(p0, "cvt", [128, 16])
              scb = sb(p0, "scb", [128, 16], BF16)
              mw = sb(p0, "mw", [128, 8, 3 * D], BF16)
              mrow = sb(p0, "mrow", [1, 3 * D])
              mbrow = sb(p0, "mbrow", [1, 3 * D])
              pm = ps(p0, "pm", [128, 512])
              tk.dma("sp", "c1", cvt[:], cv[:, :], w=["cvt"])
              tk.op("act", lambda e: e.activation(out=scb[:], in_=cvt[:], func=AF.Silu), r=["cvt"], w=["scb"])
              vi = 0
              for layer in range(2):
                  tk.dma("pool", "mw", mw[:], mod_w[layer].rearrange("(k p) n -> p k n", p=128), w=["mw"])
                  for var in ([0, 1] if layer == 0 else [0]):
                      tk.dma("sp", "c2", mbrow[:], mod_b[layer:layer + 1, :], w=["mbrow"])
                      for jb in range(6):
                          fns = []
                          for k in range(8):
                              fns.append(lambda e, k=k, jb=jb, var=var: e.matmul(
                                  pm[0:1, :], lhsT=scb[:, var * 8 + k:var * 8 + k + 1],
                                  rhs=mw[:, k, jb * 512:(jb + 1) * 512], start=(k == 0), stop=(k == 7)))
                          tk.grp("pe", fns, r=["scb", "mw"], w=["pm"])
                          tk.op("dve", lambda e, jb=jb: e.tensor_tensor(
                              out=mrow[:, jb * 512:(jb + 1) * 512], in0=pm[0:1, :],
                              in1=mbrow[:, jb * 512:(jb + 1) * 512], op=ALU.add),
                              r=["pm", "mbrow"], w=["mrow"])
                      tk.dma("sp", "c3", MODS[vi:vi + 1, :], mrow[:], r=["mrow"], w=[("MODS", vi)])
                      vi += 1
        tk.barrier()
        if stop < 1:
            return nc

        with ExitStack() as p1:
            win = sb(p1, "win", [128, 8, 2480], BF16)
            wuq = sb(p1, "wuq", [128, 3, 768], BF16)
            wukv = sb(p1, "wukv", [128, 2, 1024], BF16)
            tk.dma("pool", "w1", win[:], w_in.rearrange("(k p) n -> p k n", p=128), w=["win"])
            tk.dma("pool", "w2", wuq[:], w_uq.rearrange("(k p) n -> p k n", p=128), w=["wuq"])
            tk.dma("pool", "w3", wukv[:], w_ukv.rearrange("(k p) n -> p k n", p=128), w=["wukv"])
            tf = sb(p1, "tf", [128, D])
            nwbc = tf
            Abc = [sb(p1, f"Abc{v}", [128, D]) for v in range(2)]
            Bbc = [sb(p1, f"Bbc{v}", [128, D]) for v in range(2)]
            qnbc = sb(p1, "qnbc", [128, 384])
            kvnbc = sb(p1, "kvnbc", [128, 256])
            dtbbc = sb(p1, "dtbbc", [128, 16])
            tk.dma("sp", "c4", nwbc[:], norm_w[0:1, :].partition_broadcast(128), w=["tf"])
            tk.dma("sp", "c5", qnbc[:], q_norm[0:1, :].partition_broadcast(128), w=["qnbc"])
            tk.dma("sp", "c6", kvnbc[:], kv_norm[0:1, :].partition_broadcast(128), w=["kvnbc"])
            tk.dma("sp", "c7", dtbbc[:], ssdp[1:2, :].partition_broadcast(128), w=["dtbbc"])
            for v in range(2):
                tk.dma("sp", f"c8{v}", Bbc[v][:], MODS[v:v + 1, 0:D].partition_broadcast(128),
                       r=[("MODS", v)], w=[f"Bbc{v}"])
                tk.dma("sp", f"c9{v}", Abc[v][:], MODS[v:v + 1, D:2 * D].partition_broadcast(128),
                       r=[("MODS", v)], w=[f"Abc{v}"])
                tk.op("dve", lambda e, v=v: e.scalar_tensor_tensor(
                    out=Abc[v][:], in0=Abc[v][:], scalar=1.0, in1=nwbc[:], op0=ALU.add, op1=ALU.mult),
                    r=[f"Abc{v}", "tf"], w=[f"Abc{v}"])

            NB = 2
            xt = [sb(p1, f"xt{i}", [128, D]) for i in range(NB)]
            ropet = [sb(p1, f"ropet{i}", [128, 64]) for i in range(3)]
            junk = sb(p1, "junk", [128, D], BF16)
            stat = [sb(p1, f"stat{i}", [128, 12]) for i in range(NB)]
            hb = sb(p1, "hb", [128, D], BF16)
            hT = [sb(p1, f"hT{i}", [128, 8, 128], BF16) for i in range(NB)]
            kvn = sb(p1, "kvn", [128, 256], BF16)
            kvnT = sb(p1, "kvnT", [128, 2, 128], BF16)
            knb = sb(p1, "knb", [128, 512], BF16)
            knT_1 = sb(p1, "knT0", [128, 4, 128], BF16)
            knT = [knT_1, knT_1]
            kf = sb(p1, "kf", [128, 32])
            kr = sb(p1, "kr", [128, 64])
            kpb = sb(p1, "kpb", [128, 32], BF16)
            kpT = [sb(p1, f"kpT{i}", [32, 128], BF16) for i in range(NB)]
            dtt = sb(p1, "dtt", [128, 16])
            qan = sb(p1, "qan", [128, 384], BF16)
            qanT = sb(p1, "qanT", [128, 3, 128], BF16)
            qb = sb(p1, "qb", [128, 8, 96], BF16)
            qf = sb(p1, "qf", [128, 8, 32])
            qr = sb(p1, "qr", [128, 8, 64])
            qT_1 = sb(p1, "qT0", [128, 6, 128], BF16)
            qT = [qT_1, qT_1]
            zs_1 = sb(p1, "zs0", [128, 512], BF16)
            zs = [zs_1, zs_1]
            xbT = [sb(p1, f"xbT{i}", [128, 6, 128], BF16) for i in range(NB)]
            TP = ps(p1, "TP", [128, 1024], BF16)
            TPF = TP
            MB = ps(p1, "MB", [128, 512])
            MA = ps(p1, "MA", [128, 512])
            MZ = ps(p1, "MZ", [128, 512])
            W0 = ps(p1, "W0", [128, 1024])
            F0 = ps(p1, "F0", [128, 512])
            F1 = ps(p1, "F1", [128, 512])
            tk.op("pool", lambda e: e.memset(Vres[:].rearrange("p j h c -> p (j h c)"), 1.0), w=["Vres"])

            C_QA, C_KVA, C_KPE, C_GA, C_Z, C_XBC, C_DT = 0, 384, 640, 672, 1184, 1696, 2464

            def rms_stats(src_ap, n, st_ap, rb, wb, name):
                tk.op("act", lambda e: e.activation(out=junk[:, 0:n], in_=src_ap, func=AF.Square,
                                                    accum_out=st_ap[:, 0:1]), r=rb, w=["junk", wb])
                tk.op("act", lambda e: e.activation(out=st_ap[:, 1:2], in_=st_ap[:, 0:1], func=AF.Sqrt,
                                                    scale=1.0 / n, bias=EPS), r=[wb], w=[wb])
                tk.op("dve", lambda e: e.reciprocal(out=st_ap[:, 2:3], in_=st_ap[:, 1:2]), r=[wb], w=[wb])

            def rope_ops(src, dst, tmp, rt, nh, rb, wb, tb):
                c2 = rt[:, 0:32].unsqueeze(1).to_broadcast([128, nh, 32])
                s2a = rt[:, 32:48].unsqueeze(1).to_broadcast([128, nh, 16])
                s2b = rt[:, 48:64].unsqueeze(1).to_broadcast([128, nh, 16])
                tk.op("dve", lambda e: e.tensor_tensor(out=tmp[:, :, 0:32], in0=src, in1=c2, op=ALU.mult),
                      r=rb, w=[tb])
                tk.op("dve", lambda e: e.tensor_tensor(out=tmp[:, :, 32:48], in0=src[:, :, 16:32], in1=s2a,
                                                       op=ALU.mult), r=rb, w=[tb])
                tk.op("dve", lambda e: e.tensor_tensor(out=tmp[:, :, 48:64], in0=src[:, :, 0:16], in1=s2b,
                                                       op=ALU.mult), r=rb, w=[tb])
                tk.op("dve", lambda e: e.tensor_tensor(out=dst, in0=tmp[:, :, 0:32], in1=tmp[:, :, 32:64],
                                                       op=ALU.add), r=[tb], w=wb)

            def kind_of(j):
                return "ctx" if j < 2 else ("own" if j - 2 < NOWN else "oth")

            def front(j):
                s = j % NB
                kind = kind_of(j)
                v = 1 if kind == "ctx" else 0
                src = ctx[j * 128:(j + 1) * 128, :] if kind == "ctx" else xs[(j - 2) * 128:(j - 1) * 128, :]
                tk.dma("sp", f"xt{s}", xt[s][:], src, w=[f"xt{s}"])
                tk.dma("sp", f"rp{j % 3}", ropet[j % 3][:], rope_d[j], w=[f"ropet{j % 3}"])
                st = stat[s]
                rms_stats(xt[s][:], D, st, [f"xt{s}"], f"stat{s}a", "x")
                tk.op("dve", lambda e: e.scalar_tensor_tensor(
                    out=tf[:], in0=xt[s][:], scalar=st[:, 2:3], in1=Abc[v][:], op0=ALU.mult, op1=ALU.mult),
                    r=[f"xt{s}", f"stat{s}a", f"Abc{v}"], w=["tf"])
                tk.op("pool", lambda e: e.tensor_tensor(out=hb[:], in0=tf[:], in1=Bbc[v][:], op=ALU.add),
                      r=["tf", f"Bbc{v}"], w=["hb"])
                def _tr():
                    tk.grp("pe", [lambda e, k=k: e.transpose(TPF[:, k * 128:(k + 1) * 128], hb[:, k * 128:(k + 1) * 128],
                                                              ident[:]) for k in range(8)],
                           r=["hb", "ident"], w=["TP"])
                    tk.op("act", lambda e: e.copy(out=hT[s][:].rearrange("p k t -> p (k t)"), in_=TPF[:, :]),
                          r=["TP"], w=[f"hT{s}"])
                tk.atomic(_tr)

            def mm_all(j, part):
                s = j % NB
                kind = kind_of(j)
                hTs = hT[s]

                def tok_major(pt, c0, width, col0, name):
                    tk.grp("pe", [lambda e, k=k: e.matmul(pt[:, c0:c0 + width], lhsT=hTs[:, k, :],
                                                          rhs=win[:, k, col0:col0 + width],
                                                          start=(k == 0), stop=(k == 7)) for k in range(8)],
                           r=[f"hT{s}", "win"], w=[name])

                def feat_major(pt, c0, col0, name):
                    tk.grp("pe", [lambda e, k=k: e.matmul(pt[:, c0:c0 + 128], lhsT=win[:, k, col0:col0 + 128],
                                                          rhs=hTs[:, k, :],
                                                          start=(k == 0), stop=(k == 7)) for k in range(8)],
                           r=[f"hT{s}", "win"], w=[name])

                if part == 0:
                    tok_major(MB, 0, 288, C_KVA, "MB")
                    tok_major(MB, 288, 16, C_DT, "MB")
                    feat_major(MB, 304, C_XBC + 512, "MB")
                    return
                if part == 1:
                    if kind == "own":
                        tok_major(MA, 0, 384, C_QA, "MA")
                        feat_major(MA, 384, C_XBC + 640, "MA")
                    for c in range(2):
                        feat_major(F1, c * 128, C_XBC + c * 128, "F1")
                    return
                for c in range(2, 4):
                    feat_major(F1, c * 128, C_XBC + c * 128, "F1")
                if kind == "own":
                    tok_major(MZ, 0, 512, C_Z, "MZ")
                    tok_major(F0, 0, 512, C_GA, "F0")

            def chains_a1(j):
                s = j % NB
                st = stat[s]
                tk.op("dve", lambda e: e.tensor_copy(out=xbT[s][:, 4, :], in_=MB[:, 304:432]),
                      r=["MB"], w=[f"xbT{s}"])
                rms_stats(MB[:, 0:256], 256, st[:, 3:6], ["MB"], f"stat{s}b", "kv")
                tk.op("dve", lambda e: e.scalar_tensor_tensor(
                    out=kvn[:], in0=MB[:, 0:256], scalar=st[:, 5:6], in1=kvnbc[:], op0=ALU.mult, op1=ALU.mult),
                    r=["MB", f"stat{s}b", "kvnbc"], w=["kvn"])
                tk.op("act", lambda e: e.copy(out=kf[:], in_=MB[:, 256:288]), r=["MB"], w=["kf"])
                tk.op("dve", lambda e: e.tensor_tensor(out=dtt[:], in0=MB[:, 288:304], in1=dtbbc[:], op=ALU.add),
                      r=["MB", "dtbbc"], w=["dtt"])

            def chains_a2(j):
                s = j % NB
                kind = kind_of(j)
                st = stat[s]
                jq = j - 2
                nxc = 6 if kind == "own" else 5
                tk.op("act", lambda e: e.copy(out=xbT[s][:, 0:4, :].rearrange("p k t -> p (k t)"), in_=F1[:, :]),
                      r=["F1"], w=[f"xbT{s}"])
                if kind == "own":
                    tk.op("dve", lambda e: e.tensor_copy(out=xbT[s][:, 5, :], in_=MA[:, 384:512]),
                          r=["MA"], w=[f"xbT{s}"])
                tk.dma("sp", f"xb{s}", XBC[0:nxc * 128, j * 128:(j + 1) * 128].rearrange("(c p) t -> p c t", p=128),
                       xbT[s][:, 0:nxc, :], r=[f"xbT{s}"], w=[("XBC", j)])
                if kind == "own":
                    rms_stats(MA[:, 0:384], 384, st[:, 6:9], ["MA"], f"stat{s}c", "q")
                    tk.op("dve", lambda e: e.scalar_tensor_tensor(
                        out=qan[:], in0=MA[:, 0:384], scalar=st[:, 8:9], in1=qnbc[:], op0=ALU.mult, op1=ALU.mult),
                        r=["MA", f"stat{s}c", "qnbc"], w=["qan"])
                    tk.op("act", lambda e: e.activation(out=zs[s][:], in_=MZ[:, :], func=AF.Silu), r=["MZ"], w=["zs0"])
                    tk.dma("sp", "zs0", ZS[jq], zs[s][:], r=["zs0"], w=[("ZS", jq)])
                    tk.op("act", lambda e: e.activation(out=GAres[:, jq, :], in_=F0[:, :], func=AF.Silu),
                          r=["F0"], w=[("GAres", jq)])

            def chains_b1(j):
                s = j % NB
                kind = kind_of(j)
                st = stat[s]
                jq = j - 2
                tk.grp("pe", [lambda e, k=k: e.transpose(TP[:, k * 128:(k + 1) * 128], kvn[:, k * 128:(k + 1) * 128],
                                                          ident[:]) for k in range(2)],
                       r=["kvn", "ident"], w=["TP"])
                tk.op("dve", lambda e: e.tensor_copy(out=kvnT[:].rearrange("p k t -> p (k t)"), in_=TP[:, 0:256]),
                      r=["TP"], w=["kvnT"])
                fns = []
                for half in range(2):
                    for k in range(2):
                        fns.append(lambda e, k=k, half=half: e.matmul(
                            W0[:, half * 512:(half + 1) * 512], lhsT=kvnT[:, k, :],
                            rhs=wukv[:, k, half * 512:(half + 1) * 512], start=(k == 0), stop=(k == 1)))
                tk.grp("pe", fns, r=["kvnT", "wukv"], w=["W0"])
                kv3 = W0[:, :].rearrange("p (h c) -> p h c", h=8)
                tk.op("act", lambda e: e.copy(out=Vres[:, j, :, 0:64], in_=kv3[:, :, 64:128]),
                      r=["W0", "Vres"], w=[("V", j)])
                tk.op("dve", lambda e: e.tensor_copy(out=knb[:].rearrange("p (h c) -> p h c", h=8),
                                                     in_=kv3[:, :, 0:64]), r=["W0"], w=["knb"])

            def chains_b2(j):
                s = j % NB
                kind = kind_of(j)
                st = stat[s]
                jq = j - 2
                if kind == "own":
                    tk.grp("pe", [lambda e, k=k: e.transpose(TP[:, k * 128:(k + 1) * 128],
                                                              qan[:, k * 128:(k + 1) * 128], ident[:]) for k in range(3)],
                           r=["qan", "ident"], w=["TP"])
                    tk.op("dve", lambda e: e.tensor_copy(out=qanT[:].rearrange("p k t -> p (k t)"), in_=TP[:, 0:384]),
                          r=["TP"], w=["qanT"])
                    fns = []
                    for (c0, wd) in ((0, 512), (512, 256)):
                        for k in range(3):
                            fns.append(lambda e, k=k, c0=c0, wd=wd: e.matmul(
                                W0[:, c0:c0 + wd], lhsT=qanT[:, k, :], rhs=wuq[:, k, c0:c0 + wd],
                                start=(k == 0), stop=(k == 2)))
                    tk.grp("pe", fns, r=["qanT", "wuq"], w=["W0"])
                tk.grp("pe", [lambda e, k=k: e.transpose(TP[:, k * 128:(k + 1) * 128], knb[:, k * 128:(k + 1) * 128],
                                                          ident[:]) for k in range(4)],
                       r=["knb", "ident"], w=["TP"])
                tk.op("act", lambda e: e.copy(out=knT[s][:].rearrange("p k t -> p (k t)"), in_=TP[:, 0:512]),
                      r=["TP"], w=["knT0"])
                tk.dma("sp", f"kt{s}", KT[:, j * 128:(j + 1) * 128].rearrange("(c p) t -> p c t", p=128),
                       knT[s][:], r=["knT0"], w=[("KT", j)])
                rope_ops(kf[:].unsqueeze(1), kpb[:].unsqueeze(1), kr[:].unsqueeze(1), ropet[j % 3], 1,
                         ["kf", f"ropet{j % 3}"], ["kpb"], "kr")
                tk.op("pe", lambda e: e.transpose(TP[0:32, 0:128], kpb[:, :], ident[:]),
                      r=["kpb", "ident"], w=["TP"])
                tk.op("dve", lambda e: e.tensor_copy(out=kpT[s][:], in_=TP[0:32, 0:128]), r=["TP"], w=[f"kpT{s}"])
                tk.dma("sp", f"kp{s}", KPE[:, j * 128:(j + 1) * 128], kpT[s][:], r=[f"kpT{s}"], w=[("KPE", j)])
                tk.op("act", lambda e: e.activation(out=dtt[:], in_=dtt[:], func=AF.Exp), r=["dtt"], w=["dtt"])
                tk.op("act", lambda e: e.activation(out=DT[:, j, :], in_=dtt[:], func=AF.Ln, bias=1.0),
                      r=["dtt"], w=[("DT", j)])
                if kind != "own":
                    return
                q3 = W0[:, 0:768].rearrange("p (h c) -> p h c", h=8)
                tk.op("act", lambda e: e.activation(out=qb[:, :, 0:64], in_=q3[:, :, 0:64], func=AF.Copy,
                                                    scale=QSCALE), r=["W0"], w=["qb"])
                tk.op("act", lambda e: e.activation(out=qf[:], in_=q3[:, :, 64:96], func=AF.Copy, scale=QSCALE),
                      r=["W0"], w=["qf"])
                rope_ops(qf[:], qb[:, :, 64:96], qr[:], ropet[j % 3], 8, ["qf", f"ropet{j % 3}"], ["qb"], "qr")
                qb2 = qb[:].rearrange("p h c -> p (h c)")
                tk.grp("pe", [lambda e, k=k: e.transpose(TP[:, k * 128:(k + 1) * 128], qb2[:, k * 128:(k + 1) * 128],
                                                          ident[:]) for k in range(6)],
                       r=["qb", "ident"], w=["TP"])
                tk.op("act", lambda e: e.copy(out=qT[s][:].rearrange("p k t -> p (k t)"), in_=TP[:, 0:768]),
                      r=["TP"], w=["qT0"])
                tk.dma("sp", f"qt{s}", QT[:, jq * 128:(jq + 1) * 128].rearrange("(c p) t -> p c t", p=128),
                       qT[s][:], r=["qT0"], w=[("QT", jq)])

            def mm_front(j):
                if j < NSLOT:
                    mm_all(j)
                if j + 1 < NSLOT:
                    front(j + 1)

            def mm_front(j):
                if j < NSLOT:
                    mm_all(j)
                if j + 1 < NSLOT:
                    front(j + 1)

            front(0)
            for j in range(NSLOT):
                mm_all(j, 0)
                chains_a1(j)
                mm_all(j, 1)
                chains_b1(j)
                mm_all(j, 2)
                if j + 1 < NSLOT:
                    front(j + 1)
                chains_a2(j)
                chains_b2(j)
            if dbg:
                VSd = dscr("VS", [128, NSLOT * 8 * VP])
                tk.dma("sp", "vsd", VSd[:, :], Vres[:].rearrange("p j h c -> p (j h c)"),
                       r=[("V", j) for j in range(NSLOT)], w=["VSd"])
                DTS = dscr("DTS", [128, NSLOT * 16], F32)
                tk.dma("sp", "dts", DTS[:, :], DT[:].rearrange("p j c -> p (j c)"),
                       r=[("DT", j) for j in range(NSLOT)], w=["DTS"])
        tk.barrier()
        if stop < 2:
            return nc

        AT2 = dscr("AT2", [NOWN, 128, 512])
        with ExitStack() as p3:
            KH = [sb(p3, f"KH{i}", [96, NKEY], BF16) for i in range(2)]
            QH = [sb(p3, f"QH{i}", [96, NQ], BF16) for i in range(2)]
            ATres = sb(p3, "ATres", [128, NOWN, 512], BF16)
            NPT = 3
            PT = [sb(p3, f"PT{i}", [128, 2, 512], BF16) for i in range(3)]
            zb = sb(p3, "zb", [128, 512], BF16)
            rcp = sb(p3, "rcp", [128, 4])
            t1 = sb(p3, "t1", [128, 4, 64])
            ST = [ps(p3, f"ST{i}", [128, 1024]) for i in range(NPT)]
            OT = [ps(p3, f"OT{i}", [128, 512]) for i in range(2)]
            tk.op("pool", lambda e: e.memset(zb[:], 0.0), w=["zb"])
            for i in range(2):
                tk.dma("sp", f"khr{i}", KH[i][64:96, :], KPE[:, :], w=[f"KHr{i}"])
            blocks = [(b * 4, 4) for b in range(NOWN // 4)]
            if NOWN % 4:
                blocks.append(((NOWN // 4) * 4, NOWN % 4))

            def load_head(h):
                i = h % 2
                tk.dma("sp", f"kh{i}", KH[i][0:64, :], KT[h * 64:(h + 1) * 64, :], w=[f"KH{i}"])
                tk.dma("sp", f"qh{i}", QH[i][:, :], QT[h * 96:(h + 1) * 96, :], w=[f"QH{i}"])

            load_head(0)
            bi = 0
            for h in range(8):
                i = h % 2
                if h + 1 < 8:
                    load_head(h + 1)
                for (t0, nt) in blocks:
                    q0, qw = t0 * 128, nt * 128
                    ob = bi % 2
                    bi += 1
                    tk.op("pe", lambda e: e.matmul(OT[ob][:, :], lhsT=zb[:, 0:128], rhs=zb[:, :], start=True, stop=True),
                          r=["zb"], w=[f"OT{ob}"])
                    NPAIR = NSLOT // 2

                    def S(pk):
                        b_ = pk % NPT
                        tk.grp("pe", [lambda e, u=u: e.matmul(ST[b_][:, u * 512:u * 512 + qw],
                                                              lhsT=KH[i][0:96, (2 * pk + u) * 128:(2 * pk + u + 1) * 128],
                                                              rhs=QH[i][0:96, q0:q0 + qw], start=True, stop=True)
                                      for u in range(2)],
                               r=[f"KH{i}", f"KHr{i}", f"QH{i}"], w=[f"ST{b_}"])

                    S(0)
                    S(1)
                    for pk in range(NPAIR):
                        b_ = pk % NPT
                        pb_ = pk % 3
                        if pk + 2 < NPAIR:
                            S(pk + 2)
                        tk.op("act", lambda e: e.activation(
                            out=PT[pb_][:, :, 0:qw], in_=ST[b_][:, :].rearrange("p (u c) -> p u c", u=2)[:, :, 0:qw],
                            func=AF.Exp), r=[f"ST{b_}"], w=[f"PT{pb_}"])
                        tk.grp("pe", [lambda e, qt=qt, u=u: e.matmul(OT[ob][:, qt * 128:qt * 128 + 65],
                                                                      lhsT=PT[pb_][:, u, qt * 128:(qt + 1) * 128],
                                                                      rhs=Vres[:, 2 * pk + u, h, 0:65], start=False,
                                                                      stop=True, skip_group_check=True)
                                      for u in range(2) for qt in range(nt)], r=[f"PT{pb_}"], w=[f"OT{ob}"])
                    O3 = OT[ob][:, :].rearrange("p (t c) -> p t c", t=4)
                    tk.op("dve", lambda e: e.reciprocal(out=rcp[:, 0:nt], in_=O3[:, 0:nt, 64]), r=[f"OT{ob}"], w=["rcp"])
                    tk.op("dve", lambda e: e.tensor_tensor(out=t1[:, 0:nt, :], in0=O3[:, 0:nt, 0:64],
                                                           in1=rcp[:, 0:nt].unsqueeze(2).to_broadcast([128, nt, 64]),
                                                           op=ALU.mult), r=[f"OT{ob}", "rcp"], w=["t1"])
                    tk.op("pool", lambda e: e.tensor_tensor(out=ATres[:, t0:t0 + nt, h * 64:(h + 1) * 64], in0=t1[:, 0:nt, :],
                                                            in1=GAres[:, t0:t0 + nt, h * 64:(h + 1) * 64], op=ALU.mult),
                          r=["t1"], w=["ATres"])
            tk.dma("sp", "at2", AT2.rearrange("j p c -> p j c"), ATres[:], r=["ATres"], w=["AT2"])
        tk.barrier()
        p13.close()
        if stop < 3:
            return nc

        with ExitStack() as p2:
            U5 = sb(p2, "U5", [128, 5, NKEY], BF16)
            UC = sb(p2, "UC", [128, NQ], BF16)
            cw = sb(p2, "cw", [128, 6, 4])
            masks = sb(p2, "masks", [128, 2, 128])
            negm = sb(p2, "negm", [128, 2, 1024], BF16)
            onesf = sb(p2, "onesf2", [128, 128])
            Abc16 = sb(p2, "Abc16", [128, 16])
            skip16 = sb(p2, "skip16", [128, 16])
            skip8 = sb(p2, "skip8", [128, 8])
            ssdnbc = sb(p2, "ssdnbc", [128, 512])
            Hs = [sb(p2, f"Hs{d}", [128, 256]) for d in range(2)]
            Hb = [sb(p2, f"Hb{d}", [128, 512], BF16) for d in range(2)]
            Cblk = sb(p2, "Cblk", [128, 256], BF16)
            Yb = sb(p2, "Yb", [128, NOWN, 512], BF16)
            a8 = sb(p2, "a8", [128, 8])
            acs16 = sb(p2, "acs16", [128, 16])
            ahl = sb(p2, "ahl", [128, 16], BF16)
            masksb = sb(p2, "masksb", [128, 2, 128], BF16)
            w8 = sb(p2, "w8", [128, 8])
            acs8s = sb(p2, "acs8s", [128, 8])
            diff = sb(p2, "diff", [128, 1024])
            Ee = sb(p2, "Ee", [128, 1024])
            xd = sb(p2, "xd", [128, 512], BF16)
            xdw = sb(p2, "xdw", [128, 512], BF16)
            Bm = sb(p2, "Bm", [128, 128], BF16)
            MT = sb(p2, "MT", [128, 1024], BF16)
            ty = sb(p2, "ty", [128, 512])
            ysum = sb(p2, "ysum", [128, 512])
            zt = [sb(p2, f"zt{i}", [128, 512], BF16) for i in range(2)]
            st2 = sb(p2, "st2", [128, 4])
            junk2 = sb(p2, "junk2", [128, 512], BF16)
            yn = sb(p2, "yn", [128, 512], BF16)
            ynT = [sb(p2, f"ynT{i}", [128, 4, 128], BF16) for i in range(2)]
            TPs = ps(p2, "TPs", [128, 1024], BF16)
            ACS = ps(p2, "ACS", [128, 1024])
            G8 = ps(p2, "G8", [128, 512])
            Yp = ps(p2, "Yp", [128, 512])
            YO = ps(p2, "YO", [128, 512])
            SP = ps(p2, "SP", [128, 512])

            tk.dma("sp", "k0", cw[:], cwt_d[:, :, :], w=["cw"])
            tk.dma("sp", "k1", masks[:], masks_d[:, :, :], w=["masks"])
            tk.dma("sp", "k2", negm[:], negm_d[:, :, :], w=["negm"])
            tk.dma("sp", "k3", Abc16[:], ssdp[0:1, :].partition_broadcast(128), w=["Abc16"])
            tk.dma("sp", "k4", skip16[:], ssdp[2:3, :].partition_broadcast(128), w=["skip16"])
            tk.dma("sp", "k5", ssdnbc[:], ssd_norm[0:1, :].partition_broadcast(128), w=["ssdnbc"])
            tk.op("pool", lambda e: e.memset(onesf[:], 1.0), w=["onesf"])
            tk.op("dve", lambda e: e.tensor_copy(out=masksb[:], in_=masks[:]), r=["masks"], w=["masksb"])
            tk.op("act", lambda e: e.activation(out=Abc16[:], in_=Abc16[:], func=AF.Exp), r=["Abc16"], w=["Abc16"])
            tk.op("dve", lambda e: e.tensor_scalar(out=Abc16[:], in0=Abc16[:], scalar1=-1.0, scalar2=None,
                                                   op0=ALU.mult), r=["Abc16"], w=["Abc16"])
            tk.op("dve", lambda e: e.tensor_tensor(out=skip8[:], in0=skip16[:, 0:8], in1=skip16[:, 8:16], op=ALU.add),
                  r=["skip16"], w=["skip8"])
            for d in range(2):
                tk.op("pool", lambda e, d=d: e.memset(Hs[d][:], 0.0), w=[f"Hs{d}"])
                tk.op("pool", lambda e, d=d: e.memset(Hb[d][:], 0.0), w=[f"Hb{d}"])
            tk.op("pool", lambda e: e.memset(Cblk[:], 0.0), w=["Cblk"])

            pc = p2.enter_context(ExitStack())
            raw = [sb(pc, f"raw{i}", [128, 6, 1026], BF16) for i in range(2)]
            cacc = [sb(pc, f"cacc{i}", [128, 1024]) for i in range(2)]
            spans = [(0, 256, 5, 0, 256)]
            a = 256
            while a < 256 + NQ:
                n = min(1024, 256 + NQ - a)
                spans.append((a, n, 6, 256, NKEY))
                a += n
            while a < NKEY:
                n = min(1024, NKEY - a)
                spans.append((a, n, 5, 256, NKEY))
                a += n
            CEND = 256 + NQ
            if _CUT2 <= 1:
                spans = []
            for si, (a, n, nch, s0, s1) in enumerate(spans):
                rs = si % 2
                rw = raw[rs]
                tk.op("pool", lambda e: e.memset(rw[:, :, 0:1], 0.0), w=[f"raw{rs}"])
                tk.op("pool", lambda e: e.memset(rw[:, :, n + 1:n + 2], 0.0), w=[f"raw{rs}"])
                lo, hi = max(a - 1, s0), min(a + n + 1, s1)
                tk.dma("sp", f"raw{rs}", rw[:, 0:5, lo - (a - 1):hi - (a - 1)],
                       XBC[0:640, lo:hi].rearrange("(c p) t -> p c t", p=128), w=[f"raw{rs}"])
                if nch == 6:
                    hic = min(a + n + 1, CEND)
                    tk.dma("sp", f"rawc{rs}", rw[:, 5, lo - (a - 1):hic - (a - 1)],
                           XBC[640:768, lo:hic], w=[f"raw{rs}"])
                for c in range(nch):
                    ca = cacc[c % 2]
                    cn = f"cacc{c % 2}"
                    tk.op("dve", lambda e: e.tensor_scalar(out=ca[:, 0:n], in0=rw[:, c, 1:n + 1], scalar1=cw[:, c, 1:2],
                                                           scalar2=cw[:, c, 3:4], op0=ALU.mult, op1=ALU.add),
                          r=[f"raw{rs}", "cw"], w=[cn])
                    tk.op("dve", lambda e: e.scalar_tensor_tensor(out=ca[:, 0:n], in0=rw[:, c, 0:n], scalar=cw[:, c, 0:1],
                                                                  in1=ca[:, 0:n], op0=ALU.mult, op1=ALU.add),
                          r=[f"raw{rs}", "cw", cn], w=[cn])
                    tk.op("dve", lambda e: e.scalar_tensor_tensor(out=ca[:, 0:n], in0=rw[:, c, 2:n + 2], scalar=cw[:, c, 2:3],
                                                                  in1=ca[:, 0:n], op0=ALU.mult, op1=ALU.add),
                          r=[f"raw{rs}", "cw", cn], w=[cn])
                    dst = U5[:, c, a:a + n] if c < 5 else UC[:, a - 256:a - 256 + n]
                    tk.op("act", lambda e: e.activation(out=dst, in_=ca[:, 0:n], func=AF.Silu), r=[cn], w=["U"])

            tk.barrier()
            pc.close()
            dec2 = [sb(p2, f"dec{i}", [128, 4]) for i in range(2)]
            ea82 = [sb(p2, f"ea8{i}", [128, 8]) for i in range(2)]
            ydg = [sb(p2, f"ydg{i}", [128, 512]) for i in range(2)]
            sps = [sb(p2, f"sps{i}", [128, 256]) for i in range(2)]
            tsk2 = [sb(p2, f"tsk{i}", [128, 512]) for i in range(2)]

            a_all = sb(p2, "a_all", [128, NSLOT, 16])
            ahi = sb(p2, "ahi", [128, NSLOT, 16], BF16)
            alo = sb(p2, "alo", [128, NSLOT, 16], BF16)
            acs_all = sb(p2, "acs_all", [128, NSLOT, 16])
            ea_all = sb(p2, "ea_all", [128, NSLOT, 16])
            w_all = sb(p2, "w_all", [128, NSLOT, 16])
            dec_all = sb(p2, "dec_all", [128, NSLOT, 2, 4])
            NH = NSLOT // 2
            tk.op("dve", lambda e: e.tensor_tensor(out=a_all[:], in0=DT[:], in1=Abc16[:].unsqueeze(1).to_broadcast([128, NSLOT, 16]),
                                                   op=ALU.mult), r=["Abc16"], w=["a_all"])
            tk.op("act", lambda e: e.copy(out=ahi[:], in_=a_all[:]), r=["a_all"], w=["ahi"])
            tk.op("dve", lambda e: e.tensor_tensor(out=alo[:], in0=a_all[:], in1=ahi[:], op=ALU.subtract),
                  r=["a_all", "ahi"], w=["alo"])
            for d in range(2):
                tot = [Yp, YO]
                totn = ["Yp", "YO"]
                for hf in range(2):
                    sl = slice(hf * NH, (hf + 1) * NH)
                    rhs_ = a_all[:, sl, d * 8:(d + 1) * 8]
                    tk.op("pe", lambda e, hf=hf, rhs_=rhs_: e.matmul(ACS[:, hf * 512:hf * 512 + NH * 8], lhsT=masks[:, d, :],
                                                                     rhs=rhs_, start=True, stop=True),
                          r=["masks", "a_all"], w=["ACS"])
                    tk.op("pe", lambda e, hf=hf, rhs_=rhs_: e.matmul(tot[hf][:, 0:NH * 8], lhsT=onesf[:, :], rhs=rhs_,
                                                                     start=True, stop=True),
                          r=["onesf", "a_all"], w=[totn[hf]])
                for hf in range(2):
                    sl = slice(hf * NH, (hf + 1) * NH)
                    acs_v = ACS[:, hf * 512:hf * 512 + NH * 8].rearrange("p (s h) -> p s h", h=8)
                    tot_v = tot[hf][:, 0:NH * 8].rearrange("p (s h) -> p s h", h=8)
                    tk.op("act", lambda e, sl=sl, acs_v=acs_v: e.copy(out=acs_all[:, sl, d * 8:(d + 1) * 8], in_=acs_v),
                          r=["ACS"], w=["acs_all"])
                    tk.op("dve", lambda e, sl=sl, tot_v=tot_v: e.tensor_tensor(out=w_all[:, sl, d * 8:(d + 1) * 8], in0=tot_v,
                                                                               in1=acs_all[:, sl, d * 8:(d + 1) * 8],
                                                                               op=ALU.subtract),
                          r=[totn[hf], "acs_all"], w=["w_all"])
                    for g in range(2):
                        tk.op("act", lambda e, sl=sl, tot_v=tot_v, g=g: e.activation(
                            out=dec_all[g * 64:(g + 1) * 64, sl, d, :], in_=tot_v[g * 64:(g + 1) * 64, :, g * 4:(g + 1) * 4],
                            func=AF.Exp), r=[totn[hf]], w=["dec_all"])
            tk.op("act", lambda e: e.activation(out=w_all[:], in_=w_all[:], func=AF.Exp), r=["w_all"], w=["w_all"])
            tk.op("act", lambda e: e.activation(out=ea_all[:], in_=acs_all[:], func=AF.Exp), r=["acs_all"], w=["ea_all"])

            def ssd_front(j, d, need_out, final, par):
                cols = slice(j * 128, (j + 1) * 128)
                jq = j - 2
                qcols = slice(jq * 128, (jq + 1) * 128)
                last = 127 if d == 0 else 0
                dt8 = DT[:, j, d * 8:(d + 1) * 8]
                tk.grp("pe", [lambda e, c=c: e.transpose(TPs[:, c * 128:(c + 1) * 128], U5[:, c, cols], ident[:])
                              for c in range(5)], r=["U", "ident"], w=["TPs"])
                acs8v = acs_all[:, j, d * 8:(d + 1) * 8]
                wsrc = w_all[:, j, d * 8:(d + 1) * 8]
                if need_out:
                    fns = []
                    for h_ in range(8):
                        for u, src_ in enumerate((ahi, alo)):
                            fns.append(lambda e, h_=h_, u=u, src_=src_: e.matmul(
                                ACS[:, h_ * 128:(h_ + 1) * 128],
                                lhsT=src_[:, j, d * 8 + h_:d * 8 + h_ + 1].to_broadcast([128, 128]),
                                rhs=masksb[:, d, :], start=(h_ % 4 == 0 and u == 0), stop=False, skip_group_check=True))
                    for hf in range(2):
                        fns.append(lambda e, hf=hf: e.matmul(ACS[:, hf * 512:(hf + 1) * 512], lhsT=ident[:, :],
                                                             rhs=negm[:, d, hf * 512:(hf + 1) * 512], start=False, stop=True,
                                                             skip_group_check=True))
                    tk.grp("pe", fns, r=["ahi", "alo", "masksb", "negm", "ident"], w=["ACS"])
                    ACS3 = ACS[:, :].rearrange("p (h l) -> p h l", h=8)
                    tk.op("dve", lambda e: e.tensor_tensor(out=diff[:].rearrange("p (h l) -> p h l", h=8), in0=ACS3,
                                                           in1=acs8v.unsqueeze(2).to_broadcast([128, 8, 128]),
                                                           op=ALU.subtract), r=["ACS", "acs_all"], w=["diff"])
                    tk.op("act", lambda e: e.activation(out=Ee[:], in_=diff[:], func=AF.Exp), r=["diff"], w=["Ee"])
                xs3 = TPs[:, 0:512].rearrange("p (h c) -> p h c", h=8)
                tk.op("dve", lambda e: e.tensor_tensor(out=xd[:].rearrange("p (h c) -> p h c", h=8), in0=xs3,
                                                       in1=dt8.unsqueeze(2).to_broadcast([128, 8, 64]), op=ALU.mult),
                      r=["TPs"], w=["xd"])
                tk.op("act", lambda e: e.copy(out=Bm[:], in_=TPs[:, 512:640]), r=["TPs"], w=["Bm"])
                if final:
                    tk.op("dve", lambda e: e.tensor_tensor(out=tsk2[par][:].rearrange("p (h c) -> p h c", h=8), in0=xs3,
                                                           in1=skip8[:].unsqueeze(2).to_broadcast([128, 8, 64]),
                                                           op=ALU.mult), r=["TPs", "skip8"], w=[f"tsk{par}"])
                tk.op("pool", lambda e: e.tensor_tensor(out=xdw[:].rearrange("p (h c) -> p h c", h=8),
                                                        in0=xd[:].rearrange("p (h c) -> p h c", h=8),
                                                        in1=wsrc.unsqueeze(2).to_broadcast([128, 8, 64]),
                                                        op=ALU.mult), r=["xd", "w_all"], w=["xdw"])
                tk.op("pe", lambda e: e.matmul(SP[:, :], lhsT=Bm[:, :], rhs=xdw[:, :], start=True, stop=True),
                      r=["Bm", "xdw"], w=["SP"])
                for g in range(2):
                    tk.op("act", lambda e, g=g: e.copy(out=sps[par][g * 64:(g + 1) * 64, :],
                                                       in_=SP[g * 64:(g + 1) * 64, g * 256:(g + 1) * 256]),
                          r=["SP"], w=[f"sps{par}"])
                if need_out:
                    for g in range(2):
                        tk.op("pool", lambda e, g=g: e.tensor_copy(out=Cblk[g * 64:(g + 1) * 64, g * 128:(g + 1) * 128],
                                                                  in_=UC[g * 64:(g + 1) * 64, qcols]),
                              r=["U"], w=["Cblk"])
                    tk.op("pe", lambda e: e.matmul(G8[:, 0:256], lhsT=U5[:, 4, cols], rhs=Cblk[:, :],
                                                   start=True, stop=True), r=["U", "Cblk"], w=["G8"])
                    tk.op("dve", lambda e: e.tensor_tensor(
                        out=MT[:].rearrange("p (g r l) -> p g r l", g=2, r=4),
                        in0=Ee[:].rearrange("p (g r l) -> p g r l", g=2, r=4),
                        in1=G8[:, 0:256].rearrange("p (g l) -> p g l", g=2).unsqueeze(2).to_broadcast([128, 2, 4, 128]),
                        op=ALU.mult), r=["Ee", "G8"], w=["MT"])
                    tk.grp("pe", [lambda e, h=h: e.matmul(Yp[:, h * 64:(h + 1) * 64], lhsT=MT[:, h * 128:(h + 1) * 128],
                                                          rhs=xd[:, h * 64:(h + 1) * 64], start=True, stop=True)
                                  for h in range(8)], r=["MT", "xd"], w=["Yp"])
                    tk.op("act", lambda e: e.copy(out=ydg[par][:], in_=Yp[:, :]), r=["Yp"], w=[f"ydg{par}"])

            def ssd_back(j, d, need_out, final, par):
                jq = j - 2
                qcols = slice(jq * 128, (jq + 1) * 128)
                if need_out:
                    tk.op("pe", lambda e: e.matmul(YO[:, :], lhsT=UC[:, qcols], rhs=Hb[d][:, :], start=True, stop=True),
                          r=["U", f"Hb{d}"], w=["YO"])
                    tk.op("dve", lambda e: e.tensor_tensor(out=ty[:].rearrange("p (h c) -> p h c", h=8),
                                                           in0=YO[:, :].rearrange("p (h c) -> p h c", h=8),
                                                           in1=ea_all[:, j, d * 8:(d + 1) * 8].unsqueeze(2).to_broadcast([128, 8, 64]),
                                                           op=ALU.mult), r=["YO", "ea_all"], w=["ty"])
                H3 = Hs[d][:].rearrange("p (r c) -> p r c", r=4)
                tk.op("dve", lambda e: e.tensor_tensor(out=H3, in0=H3,
                                                       in1=dec_all[:, j, d, :].unsqueeze(2).to_broadcast([128, 4, 64]),
                                                       op=ALU.mult), r=[f"Hs{d}", "dec_all"], w=[f"Hs{d}"])
                tk.op("dve", lambda e: e.tensor_tensor(out=Hs[d][:], in0=Hs[d][:], in1=sps[par][:], op=ALU.add),
                      r=[f"Hs{d}", f"sps{par}"], w=[f"Hs{d}"])
                for g in range(2):
                    tk.op("act", lambda e, g=g: e.copy(out=Hb[d][g * 64:(g + 1) * 64, g * 256:(g + 1) * 256],
                                                       in_=Hs[d][g * 64:(g + 1) * 64, :]), r=[f"Hs{d}"], w=[f"Hb{d}"])
                if not need_out:
                    return
                if not final:
                    tk.op("pool", lambda e: e.tensor_tensor(out=Yb[:, jq, :], in0=ty[:], in1=ydg[par][:], op=ALU.add),
                          r=["ty", f"ydg{par}"], w=[("Yb", jq)])
                    return
                zi = jq % 2
                tk.dma("sp", f"zt{zi}", zt[zi][:], ZS[jq], w=[f"zt{zi}"])
                tk.op("pool", lambda e: e.tensor_tensor(out=ysum[:], in0=ty[:], in1=ydg[par][:], op=ALU.add),
                      r=["ty", f"ydg{par}"], w=["ysum"])
                tk.op("pool", lambda e: e.tensor_tensor(out=ysum[:], in0=ysum[:], in1=Yb[:, jq, :], op=ALU.add),
                      r=["ysum", ("Yb", jq)], w=["ysum"])
                tk.op("pool", lambda e: e.tensor_tensor(out=ysum[:], in0=ysum[:], in1=tsk2[par][:], op=ALU.add),
                      r=["ysum", f"tsk{par}"], w=["ysum"])
                tk.op("pool", lambda e: e.tensor_tensor(out=ysum[:], in0=ysum[:], in1=zt[zi][:], op=ALU.mult),
                      r=["ysum", f"zt{zi}"], w=["ysum"])
                tk.op("act", lambda e: e.activation(out=junk2[:], in_=ysum[:], func=AF.Square,
                                                    accum_out=st2[:, 0:1]), r=["ysum"], w=["junk2", "st2"])
                tk.op("act", lambda e: e.activation(out=st2[:, 1:2], in_=st2[:, 0:1], func=AF.Sqrt,
                                                    scale=1.0 / 512, bias=EPS), r=["st2"], w=["st2"])
                tk.op("dve", lambda e: e.reciprocal(out=st2[:, 2:3], in_=st2[:, 1:2]), r=["st2"], w=["st2"])
                tk.op("dve", lambda e: e.scalar_tensor_tensor(out=yn[:], in0=ysum[:], scalar=st2[:, 2:3],
                                                              in1=ssdnbc[:], op0=ALU.mult, op1=ALU.mult),
                      r=["ysum", "st2", "ssdnbc"], w=["yn"])
                tk.grp("pe", [lambda e, c=c: e.transpose(YO[:, :].bitcast(BF16)[:, c * 128:(c + 1) * 128],
                                                          yn[:, c * 128:(c + 1) * 128], ident[:])
                              for c in range(4)], r=["yn", "ident"], w=["YO"])
                tk.op("act", lambda e: e.copy(out=ynT[zi][:].rearrange("p c t -> p (c t)"),
                                              in_=YO[:, :].bitcast(BF16)[:, 0:512]), r=["YO"], w=[f"ynT{zi}"])
                tk.dma("sp", f"yt{zi}", YT[:, qcols].rearrange("(c p) t -> p c t", p=128), ynT[zi][:],
                       r=[f"ynT{zi}"], w=[("YT", jq)])

            steps = [(1, 1, False, False), (0, 1, False, False)]
            steps += [(j, 1, (j - 2) < NOWN, False) for j in range(NSLOT - 1, 1, -1)]
            steps += [(0, 0, False, False), (1, 0, False, False)]
            steps += [(j, 0, True, True) for j in range(2, 2 + NOWN)]
            ssd_front(*steps[0], 0)
            for k, stp in enumerate(steps):
                fo = tk.record(ssd_front, *steps[k + 1], (k + 1) % 2) if k + 1 < len(steps) else []
                bo = tk.record(ssd_back, *stp, k % 2)
                tk.interleave(fo, bo)
        tk.barrier()
        if stop < 4:
            return nc

        with ExitStack() as p4:
            wo = sb(p4, "wo", [128, 8, D], BF16)
            wip = sb(p4, "wip", [128, 8, 2 * D], BF16)
            lin = sb(p4, "lin", [128, 8, 256], BF16)
            wop = sb(p4, "wop", [128, 8, D], BF16)
            poolm = sb(p4, "poolm", [128, 36, 128], BF16)
            tk.dma("pool", "w1", wo[:], w_out_mix.rearrange("(k p) n -> p k n", p=128), w=["wo"])
            tk.dma("pool", "w2", wip[:], w_in_pool.rearrange("(k p) n -> p k n", p=128), w=["wip"])
            tk.dma("pool", "w3", lin[:], pool_lin.rearrange("g (kk p) n -> p (g kk) n", p=128), w=["lin"])
            tk.dma("pool", "w4", wop[:], w_out_pool.rearrange("(k p) n -> p k n", p=128), w=["wop"])
            tk.dma("sp", "c0", poolm[:], poolm_d[:, :, :], w=["poolm"])
            g0bc = sb(p4, "g0bc", [128, D])
            A1bc = sb(p4, "A1bc", [128, D])
            B1bc = sb(p4, "B1bc", [128, D])
            g1bc = sb(p4, "g1bc", [128, D])
            fnbc = sb(p4, "fnbc", [128, D])
            psbc = sb(p4, "psbc", [128, D])
            nw1 = sb(p4, "nw1", [128, D])
            tk.dma("sp", "c1", g0bc[:], MODS[0:1, 2 * D:3 * D].partition_broadcast(128), w=["g0bc"])
            tk.dma("sp", "c2", B1bc[:], MODS[2:3, 0:D].partition_broadcast(128), w=["B1bc"])
            tk.dma("sp", "c3", A1bc[:], MODS[2:3, D:2 * D].partition_broadcast(128), w=["A1bc"])
            tk.dma("sp", "c4", g1bc[:], MODS[2:3, 2 * D:3 * D].partition_broadcast(128), w=["g1bc"])
            tk.dma("sp", "c5", fnbc[:], final_norm[0:1, :].partition_broadcast(128), w=["fnbc"])
            tk.dma("sp", "c6", psbc[:], pool_scale[0:1, :].partition_broadcast(128), w=["psbc"])
            tk.dma("sp", "c7", nw1[:], norm_w[1:2, :].partition_broadcast(128), w=["nw1"])
            tk.op("dve", lambda e: e.scalar_tensor_tensor(out=A1bc[:], in0=A1bc[:], scalar=1.0, in1=nw1[:],
                                                          op0=ALU.add, op1=ALU.mult), r=["A1bc", "nw1"], w=["A1bc"])
            tk.op("dve", lambda e: e.tensor_tensor(out=wo[:], in0=wo[:], in1=g0bc[:].unsqueeze(1).to_broadcast([128, 8, D]),
                                                   op=ALU.mult), r=["wo", "g0bc"], w=["wo"])
            tk.op("pool", lambda e: e.tensor_tensor(out=wop[:], in0=wop[:], in1=g1bc[:].unsqueeze(1).to_broadcast([128, 8, D]),
                                                    op=ALU.mult), r=["wop", "g1bc"], w=["wop"])
            lin4 = lin[:].rearrange("p (g k) n -> p g k n", g=4)
            ps4 = psbc[:].rearrange("p (g n) -> p g n", g=4).unsqueeze(2).to_broadcast([128, 4, 2, 256])
            tk.op("dve", lambda e: e.tensor_tensor(out=lin4, in0=lin4, in1=ps4, op=ALU.mult), r=["lin", "psbc"], w=["lin"])
            xt4 = [sb(p4, f"xt4{i}", [128, D]) for i in range(2)]
            at4 = [sb(p4, f"at4{i}", [128, 4, 128], BF16) for i in range(2)]
            atk = [sb(p4, f"atk{i}", [128, 512], BF16) for i in range(2)]
            yT4 = [sb(p4, f"yT4{i}", [128, 4, 128], BF16) for i in range(2)]
            tfa = sb(p4, "tfa", [128, D])
            tfb = sb(p4, "tfb", [128, D])
            x1 = [sb(p4, f"x1{i}", [128, D]) for i in range(3)]
            sta = sb(p4, "sta", [128, 4])
            stb = sb(p4, "stb", [128, 4])
            junka = sb(p4, "junka", [128, D], BF16)
            junkb = sb(p4, "junkb", [128, D], BF16)
            hb4 = sb(p4, "hb4", [128, D], BF16)
            hT4 = sb(p4, "hT4", [128, 8, 128], BF16)
            ub = [sb(p4, f"ub{i}", [128, D], BF16) for i in range(4)]
            sg = [sb(p4, f"sg{i}", [128, D], BF16) for i in range(3)]
            mTb = sb(p4, "mTb", [128, 8, 128], BF16)
            pb4 = sb(p4, "pb4", [128, D], BF16)
            pT4 = sb(p4, "pT4", [128, 8, 128], BF16)
            x2 = sb(p4, "x2", [128, D])
            ot = [sb(p4, f"ot{i}", [128, D]) for i in range(2)]
            PA = ps(p4, "PA", [128, 1024])
            PB = ps(p4, "PB", [128, 1024])
            PC = ps(p4, "PC", [128, 1024])
            PD = ps(p4, "PD", [128, 512])
            TP4 = ps(p4, "TP4", [128, 1024], BF16)

            def bundle2(ops):
                a_, b_ = ops[-2], ops[-1]
                ops[-2:] = [lambda: (a_(), b_())]

            def rms4(ops, src, name_r, st_, stn, jk, jkn):
                ops.append(lambda: tk.op("act", lambda e: e.activation(out=jk[:], in_=src, func=AF.Square,
                                                                         accum_out=st_[:, 0:1]), r=[name_r], w=[jkn, stn]))
                ops.append(lambda: tk.op("act", lambda e: e.activation(out=st_[:, 1:2], in_=st_[:, 0:1], func=AF.Sqrt,
                                                                         scale=1.0 / D, bias=EPS), r=[stn], w=[stn]))
                ops.append(lambda: tk.op("dve", lambda e: e.reciprocal(out=st_[:, 2:3], in_=st_[:, 1:2]), r=[stn], w=[stn]))

            def stage_a(j):
                ops = []
                s_ = j % 2
                r_ = j % 3
                u_ = j % 4
                qcols = slice(j * 128, (j + 1) * 128)
                ops.append(lambda: tk.dma("act", f"xt4{s_}", xt4[s_][:], xs[j * 128:(j + 1) * 128, :], w=[f"xt4{s_}"]))
                ops.append(lambda: tk.dma("sp", f"at4{s_}", atk[s_][:], AT2[j], w=[f"atk{s_}"]))
                ops.append(lambda: tk.grp("pe", [lambda e, k=k: e.transpose(TP4[:, k * 128:(k + 1) * 128],
                                                                             atk[s_][:, k * 128:(k + 1) * 128], ident[:])
                                                 for k in range(4)], r=[f"atk{s_}", "ident"], w=["TP4"]))
                ops.append(lambda: tk.op("act", lambda e: e.copy(out=at4[s_][:].rearrange("p k t -> p (k t)"),
                                                                 in_=TP4[:, 0:512]), r=["TP4"], w=[f"at4{s_}"]))
                bundle2(ops)
                ops.append(lambda: tk.dma("sp", f"yT4{s_}", yT4[s_][:], YT[:, qcols].rearrange("(c p) t -> p c t", p=128),
                                          w=[f"yT4{s_}"]))
                fns = []
                for hf in range(2):
                    for k in range(8):
                        lt = at4[s_][:, k, :] if k < 4 else yT4[s_][:, k - 4, :]
                        fns.append(lambda e, k=k, hf=hf, lt=lt: e.matmul(PA[:, hf * 512:(hf + 1) * 512], lhsT=lt,
                                                                        rhs=wo[:, k, hf * 512:(hf + 1) * 512],
                                                                        start=(k == 0), stop=(k == 7)))
                ops.append(lambda: tk.grp("pe", fns, r=[f"at4{s_}", f"yT4{s_}", "wo"], w=["PA"]))
                ops.append(lambda: tk.op("dve", lambda e: e.tensor_tensor(out=x1[r_][:], in0=PA[:, :], in1=xt4[s_][:], op=ALU.add),
                                         r=["PA", f"xt4{s_}"], w=[f"x1{r_}"]))
                rms4(ops, x1[r_][:], f"x1{r_}", sta, "sta", junka, "junka")
                ops.append(lambda: tk.op("dve", lambda e: e.scalar_tensor_tensor(
                    out=tfa[:], in0=x1[r_][:], scalar=sta[:, 2:3], in1=A1bc[:], op0=ALU.mult, op1=ALU.mult),
                    r=[f"x1{r_}", "sta", "A1bc"], w=["tfa"]))
                ops.append(lambda: tk.op("pool", lambda e: e.tensor_tensor(out=hb4[:], in0=tfa[:], in1=B1bc[:], op=ALU.add),
                                         r=["tfa", "B1bc"], w=["hb4"]))
                ops.append(lambda: tk.grp("pe", [lambda e, k=k: e.transpose(TP4[:, k * 128:(k + 1) * 128],
                                                                             hb4[:, k * 128:(k + 1) * 128], ident[:])
                                                 for k in range(8)], r=["hb4", "ident"], w=["TP4"]))
                ops.append(lambda: tk.op("act", lambda e: e.copy(out=hT4[:].rearrange("p k t -> p (k t)"), in_=TP4[:, :]),
                                         r=["TP4"], w=["hT4"]))
                bundle2(ops)
                for (pt, name, c0) in ((PB, "PB", 0), (PC, "PC", D)):
                    fns2 = []
                    for hf in range(2):
                        for k in range(8):
                            fns2.append(lambda e, k=k, hf=hf, pt=pt, c0=c0: e.matmul(
                                pt[:, hf * 512:(hf + 1) * 512], lhsT=hT4[:, k, :],
                                rhs=wip[:, k, c0 + hf * 512:c0 + (hf + 1) * 512], start=(k == 0), stop=(k == 7)))
                    ops.append(lambda fns2=fns2, name=name: tk.grp("pe", fns2, r=["hT4", "wip"], w=[name]))
                ops.append(lambda: tk.op("dve", lambda e: e.tensor_copy(out=ub[u_][:], in_=PB[:, :]), r=["PB"], w=[f"ub{u_}"]))
                ops.append(lambda: tk.op("act", lambda e: e.activation(out=sg[r_][:], in_=PC[:, :], func=AF.Silu),
                                         r=["PC"], w=[f"sg{r_}"]))
                return ops

            def stage_b(i):
                ops = []
                r_ = i % 3
                o_ = i % 2
                var = 0 if i == 0 else 1
                nbs = [1, 2] if i == 0 else [0, 1, 2]
                for hf in range(2):
                    fns = []
                    for c in range(4 * hf, 4 * hf + 4):
                        g = c // 2
                        for n_i, nb in enumerate(nbs):
                            ubn = ub[(i + nb - 1) % 4]
                            last_ = (n_i == len(nbs) - 1)
                            fns.append(lambda e, c=c, g=g, nb=nb, ubn=ubn, n_i=n_i, hf=hf, last_=last_: e.matmul(
                                PD[:, (c - 4 * hf) * 128:(c - 4 * hf + 1) * 128], lhsT=ubn[:, c * 128:(c + 1) * 128],
                                rhs=poolm[:, var * 12 + g * 3 + nb, :], start=(n_i == 0), stop=(last_ and var == 1)))
                            if var == 0:
                                fns.append(lambda e, c=c, g=g, nb=nb, ubn=ubn, hf=hf, last_=last_: e.matmul(
                                    PD[:, (c - 4 * hf) * 128:(c - 4 * hf + 1) * 128], lhsT=ubn[:, c * 128:(c + 1) * 128],
                                    rhs=poolm[:, 24 + g * 3 + nb, :], start=False, stop=last_))
                    ops.append(lambda fns=fns: tk.grp("pe", fns, r=[f"ub{(i + nb - 1) % 4}" for nb in nbs] + ["poolm"], w=["PD"]))
                    ops.append(lambda hf=hf: tk.op("act", lambda e: e.copy(
                        out=mTb[:, 4 * hf:4 * hf + 4, :].rearrange("p k t -> p (k t)"), in_=PD[:, :]), r=["PD"], w=["mTb"]))
                for hf in range(2):
                    fns = []
                    for g in range(2 * hf, 2 * hf + 2):
                        for kk in range(2):
                            fns.append(lambda e, g=g, kk=kk, hf=hf: e.matmul(
                                PD[:, (g - 2 * hf) * 256:(g - 2 * hf + 1) * 256], lhsT=mTb[:, g * 2 + kk, :],
                                rhs=lin[:, g * 2 + kk, :], start=(kk == 0), stop=(kk == 1)))
                    hs = slice(hf * 512, (hf + 1) * 512)
                    ops.append(lambda fns=fns: tk.grp("pe", fns, r=["mTb", "lin"], w=["PD"]))
                    ops.append(lambda hs=hs: tk.op("dve", lambda e: e.tensor_tensor(out=pb4[:, hs], in0=PD[:, :], in1=sg[r_][:, hs],
                                                                                   op=ALU.mult), r=["PD", f"sg{r_}"], w=["pb4"]))
                ops.append(lambda: tk.grp("pe", [lambda e, k=k: e.transpose(TP4[:, k * 128:(k + 1) * 128],
                                                                             pb4[:, k * 128:(k + 1) * 128], ident[:])
                                                 for k in range(8)], r=["pb4", "ident"], w=["TP4"]))
                ops.append(lambda: tk.op("act", lambda e: e.copy(out=pT4[:].rearrange("p k t -> p (k t)"), in_=TP4[:, :]),
                                         r=["TP4"], w=["pT4"]))
                bundle2(ops)
                for hf in range(2):
                    hs = slice(hf * 512, (hf + 1) * 512)
                    fns = [lambda e, k=k, hs=hs: e.matmul(PD[:, :], lhsT=pT4[:, k, :], rhs=wop[:, k, hs],
                                                          start=(k == 0), stop=(k == 7)) for k in range(8)]
                    ops.append(lambda fns=fns: tk.grp("pe", fns, r=["pT4", "wop"], w=["PD"]))
                    ops.append(lambda hs=hs: tk.op("dve", lambda e: e.tensor_tensor(out=x2[:, hs], in0=PD[:, :], in1=x1[r_][:, hs],
                                                                                   op=ALU.add), r=["PD", f"x1{r_}"], w=["x2"]))
                rms4(ops, x2[:], "x2", stb, "stb", junkb, "junkb")
                ops.append(lambda: tk.op("dve", lambda e: e.scalar_tensor_tensor(
                    out=ot[o_][:], in0=x2[:], scalar=stb[:, 2:3], in1=fnbc[:], op0=ALU.mult, op1=ALU.mult),
                    r=["x2", "stb", "fnbc"], w=[f"ot{o_}"]))
                ops.append(lambda: tk.dma("sp", f"ot{o_}", out_d[i * 128:(i + 1) * 128, :], ot[o_][:], r=[f"ot{o_}"],
                                          w=[("out", i)]))
                return ops

            def interleave(a, b):
                na, nb_ = len(a), len(b)
                ia = ib = 0
                while ia < na or ib < nb_:
                    if ib >= nb_ or (ia < na and ia * nb_ <= ib * na):
                        a[ia](); ia += 1
                    else:
                        b[ib](); ib += 1

            for f_ in stage_a(0):
                f_()
            for f_ in stage_a(1):
                f_()
            for j in range(2, NOWN):
                interleave(stage_a(j), stage_b(j - 2))
            for f_ in stage_b(NOWN - 2):
                f_()
        tk.barrier()
    return nc


POOL_WINDOWS = (2, 4, 8, 16)


def _consts(flip):
    bf = ml_dtypes.bfloat16
    ident = np.eye(128, dtype=np.float32).astype(bf)
    rope = np.zeros((NSLOT, 128, 64), np.float32)
    rope[0:2, :, 0:32] = 1.0
    tl = np.arange(SEQ)
    t = (SEQ - 1 - tl) if flip else tl
    row = (t // 64).astype(np.float32)
    col = (t % 64).astype(np.float32)
    inv = (1.0 / (10000.0 ** (np.arange(0, 16, 2, dtype=np.float32) / 16.0))).astype(np.float32)
    ang = np.concatenate([row[:, None] * inv, col[:, None] * inv], axis=-1).astype(np.float32)
    cos, sin = np.cos(ang), np.sin(ang)
    tab = np.concatenate([cos, cos, -sin, sin], axis=-1).astype(np.float32)
    rope[2:] = tab.reshape(NT, 128, 64)
    k = np.arange(128)
    masks = np.zeros((128, 2, 128), np.float32)
    masks[:, 0, :] = (k[:, None] <= k[None, :])
    masks[:, 1, :] = (k[:, None] >= k[None, :])
    negm = np.zeros((128, 2, 8, 128), np.float32)
    negm[:, 0] = np.where(k[:, None] <= k[None, :], 0.0, NEG)[:, None, :]
    negm[:, 1] = np.where(k[:, None] >= k[None, :], 0.0, NEG)[:, None, :]
    negm = negm.reshape(128, 2, 1024).astype(bf)
    poolm = np.zeros((128, 2, 4, 3, 128), np.float32)
    for var in range(2):
        for g, w in enumerate(POOL_WINDOWS):
            if flip:
                lo_off, hi_off = -(w - w // 2 - 1), w // 2
            else:
                lo_off, hi_off = -(w // 2), (w - w // 2 - 1)
            for tt in range(128):
                lo, hi = tt + lo_off, tt + hi_off
                if var == 0:
                    lo = max(lo, 0)
                cnt = hi - lo + 1
                for sidx in range(lo, hi + 1):
                    nb = 1 + (sidx // 128)
                    poolm[sidx % 128, var, g, nb, tt] += 1.0 / cnt
                poolm[tt, var, g, 1, tt] -= 1.0
    pm_hi = poolm.astype(bf)
    pm_lo = (poolm - pm_hi.astype(np.float32)).astype(bf)
    poolm = np.concatenate([pm_hi.reshape(128, 24, 128), pm_lo[:, 0].reshape(128, 12, 128)], axis=1)
    return dict(ident=ident, rope=rope, masks=masks, negm=negm, poolm=poolm)


def _in_maps(inp):
    f = lambda a: np.ascontiguousarray(np.asarray(a, dtype=np.float32))
    x, c, ctx, c_ctx = f(inp["x"]), f(inp["c"]), f(inp["ctx"]), f(inp["c_ctx"])
    consts = [_consts(False), _consts(True)]
    w_in = f(inp["w_in_mix"])[0]
    w_in_f = w_in.copy()
    w_in_f[:, 2464:2472] = w_in[:, 2472:2480]
    w_in_f[:, 2472:2480] = w_in[:, 2464:2472]
    ssdp = np.stack([f(inp["a_log"])[0].reshape(16), f(inp["dt_bias"])[0].reshape(16),
                     f(inp["d_skip"])[0].reshape(16)])
    ssdp_f = np.concatenate([ssdp[:, 8:16], ssdp[:, 0:8]], axis=1)
    conv_w = f(inp["conv_w"])[0]
    shared = dict(
        mod_w=f(inp["mod_w"]), mod_b=f(inp["mod_b"]), norm_w=f(inp["norm_w"]),
        q_norm=f(inp["q_norm"]).reshape(1, 384), w_uq=f(inp["w_uq"])[0],
        kv_norm=f(inp["kv_norm"]).reshape(1, 256), w_ukv=f(inp["w_ukv"])[0],
        conv_b=f(inp["conv_b"]).reshape(1, 768), ssd_norm=f(inp["ssd_norm"]).reshape(1, 512),
        w_out_mix=f(inp["w_out_mix"])[0], w_in_pool=f(inp["w_in_pool"])[0], pool_lin=f(inp["pool_lin"])[0],
        pool_scale=f(inp["pool_scale"]).reshape(1, D), w_out_pool=f(inp["w_out_pool"])[0],
        final_norm=f(inp["final_norm"]).reshape(1, D))
    maps = []
    for core in range(8):
        b, half = core // 2, core % 2
        flip = half == 1
        m = dict(shared)
        if _SMALL:
            m.pop("mod_w")
        xb = x[b][:SEQ]
        m["xs"] = np.ascontiguousarray(xb[::-1]) if flip else np.ascontiguousarray(xb)
        m["ctx"] = np.ascontiguousarray(ctx[b][::-1]) if flip else ctx[b]
        m["cv"] = np.ascontiguousarray(np.concatenate([c[b].reshape(8, 128).T, c_ctx.reshape(8, 128).T], axis=1))
        m["w_in"] = w_in_f if flip else w_in
        cwl = conv_w[::-1] if flip else conv_w
        m["conv_w"] = np.ascontiguousarray(cwl)
        cb = f(inp["conv_b"]).reshape(768)
        m["cwt"] = np.ascontiguousarray(np.concatenate([cwl, cb[None, :]], 0).reshape(4, 6, 128).transpose(2, 1, 0))
        m["ssdp"] = np.ascontiguousarray(ssdp_f if flip else ssdp)
        m.update(consts[half])
        maps.append(m)
    return maps


_NC = {}


def kernel(**inputs):
    if "nc" not in _NC:
        _NC["nc"] = build(stop=float(os.environ.get("KSTOP", "99")))
    maps = _in_maps(inputs)
    res = run_bass_kernel_spmd(_NC["nc"], maps, core_ids=list(range(8)))
    out = np.zeros((4, SEQ, D), np.float32)
    for core in range(8):
        b, half = core // 2, core % 2
        o = np.asarray(res.results[core]["out"], dtype=np.float32)
        if half == 0:
            out[b, 0:NOUT * 128] = o
        else:
            out[b, NOUT * 128:] = o[::-1]
    return out
```
